# Optimizing a Trainium2 kernel written in Bass

```python
import jax, jax.numpy as jnp
from jax import lax
import numpy as np

D_MODEL = 2048
BATCH = 4
SEQ = 2048
DEPTH = 2

N_MIXERS = 2
N_MOBA_LAYERS = (DEPTH + 1) // 2
N_RET_LAYERS = DEPTH // 2

FFN_DIM = 5632
FFN_RES = 0.5
RMS_EPS = 1e-6

MOBA_HEADS = 16
MOBA_HEAD_DIM = D_MODEL // MOBA_HEADS
MOBA_BLOCK = 256
MOBA_TOPK = 3
MOBA_Q_CHUNK = 16
ROPE_THETA = 500000.0
ROPE_DIM = MOBA_HEAD_DIM // 4

RET_HEADS = 8
RET_KEY_DIM = D_MODEL // RET_HEADS
RET_VAL_DIM = D_MODEL // RET_HEADS
RET_CHUNK = 128
RET_ROT_BASE = 10000.0

kernel_name = "hybrid_moba_retention_macaron"


def rms_norm(x, g):
    xf = x.astype(jnp.float32)
    y = xf * lax.rsqrt(jnp.mean(xf * xf, axis=-1, keepdims=True) + RMS_EPS)
    return (y * g.astype(jnp.float32)).astype(x.dtype)


def swiglu(x, w_gate_up, w_down):
    g, u = jnp.split(x @ w_gate_up, 2, axis=-1)
    return (jax.nn.silu(g) * u) @ w_down


def rotary(x, pos, rot_dim, inv_freq):
    half = rot_dim // 2
    ang = pos.astype(jnp.float32)[:, None] * inv_freq[None, :]
    cos = jnp.cos(ang).astype(x.dtype)
    sin = jnp.sin(ang).astype(x.dtype)
    x1 = x[..., :half]
    x2 = x[..., half:rot_dim]
    return jnp.concatenate([x1 * cos - x2 * sin, x1 * sin + x2 * cos, x[..., rot_dim:]], axis=-1)


def moba_attention(h, w_qkv, w_o):
    B, S, _ = h.shape
    H, Dh, BLK, QC = MOBA_HEADS, MOBA_HEAD_DIM, MOBA_BLOCK, MOBA_Q_CHUNK
    qkv = (h @ w_qkv).reshape(B, S, 3, H, Dh)
    q = jnp.transpose(qkv[:, :, 0], (0, 2, 1, 3))
    k = jnp.transpose(qkv[:, :, 1], (0, 2, 1, 3))
    v = jnp.transpose(qkv[:, :, 2], (0, 2, 1, 3))
    pos = jnp.arange(S)
    half = ROPE_DIM // 2
    inv_freq = jnp.power(jnp.float32(ROPE_THETA), -jnp.arange(half, dtype=jnp.float32) / half)
    q = rotary(q, pos, ROPE_DIM, inv_freq)
    k = rotary(k, pos, ROPE_DIM, inv_freq)

    n_blk = -(-S // BLK)
    pad = n_blk * BLK - S
    kp = jnp.pad(k, ((0, 0), (0, 0), (0, pad), (0, 0)))
    vp = jnp.pad(v, ((0, 0), (0, 0), (0, pad), (0, 0)))
    k_blocks = kp.reshape(B, H, n_blk, BLK, Dh)
    v_blocks = vp.reshape(B, H, n_blk, BLK, Dh)

    k_mean = jnp.mean(k_blocks.astype(jnp.float32), axis=3)
    gate = jnp.einsum('bhsd,bhnd->bhsn', q.astype(jnp.float32), k_mean)
    q_blk = pos // BLK
    past = jnp.arange(n_blk)[None, :] < q_blk[:, None]
    gate = jnp.where(past[None, None], gate, -jnp.inf)
    k_sel = min(MOBA_TOPK, n_blk)
    _, sel = lax.top_k(gate, k_sel)
    sel_valid = sel < q_blk[None, None, :, None]

    n_qc = S // QC
    q_c_all = q.reshape(B, H, n_qc, QC, Dh).transpose(2, 0, 1, 3, 4)
    sel_all = sel.reshape(B, H, n_qc, QC, k_sel).transpose(2, 0, 1, 3, 4)
    val_all = sel_valid.reshape(B, H, n_qc, QC, k_sel).transpose(2, 0, 1, 3, 4)
    bi = jnp.arange(B)[:, None, None, None]
    hi = jnp.arange(H)[None, :, None, None]
    scale = Dh ** -0.5

    def chunk(args):
        c, q_c, sel_c, val_c = args
        q_start = c * QC
        own = q_start // BLK
        k_own = lax.dynamic_slice_in_dim(kp, own * BLK, BLK, axis=2)
        v_own = lax.dynamic_slice_in_dim(vp, own * BLK, BLK, axis=2)
        k_g = k_blocks[bi, hi, sel_c]
        v_g = v_blocks[bi, hi, sel_c]
        s_sel = jnp.einsum('bhqd,bhqjkd->bhqjk', q_c, k_g).astype(jnp.float32) * scale
        s_sel = jnp.where(val_c[..., None], s_sel, -jnp.inf).reshape(B, H, QC, k_sel * BLK)
        s_own = jnp.einsum('bhqd,bhkd->bhqk', q_c, k_own).astype(jnp.float32) * scale
        qpos = q_start + jnp.arange(QC)
        kpos = own * BLK + jnp.arange(BLK)
        s_own = jnp.where(kpos[None, :] <= qpos[:, None], s_own, -jnp.inf)
        p = jax.nn.softmax(jnp.concatenate([s_sel, s_own], axis=-1), axis=-1).astype(v.dtype)
        p_sel = p[..., :k_sel * BLK].reshape(B, H, QC, k_sel, BLK)
        p_own = p[..., k_sel * BLK:]
        return (jnp.einsum('bhqjk,bhqjkd->bhqd', p_sel, v_g)
                + jnp.einsum('bhqk,bhkd->bhqd', p_own, v_own))

    out = lax.map(chunk, (jnp.arange(n_qc), q_c_all, sel_all, val_all))
    out = out.transpose(1, 0, 3, 2, 4).reshape(B, S, H * Dh)
    return out @ w_o


def retention(h, w_in, w_o, gn_gain):
    B, S, _ = h.shape
    H, dk, dv, C = RET_HEADS, RET_KEY_DIM, RET_VAL_DIM, RET_CHUNK
    proj = h @ w_in
    q, k, v, g = jnp.split(proj, [H * dk, 2 * H * dk, 2 * H * dk + H * dv], axis=-1)
    q = q.reshape(B, S, H, dk).transpose(0, 2, 1, 3).astype(jnp.float32)
    k = k.reshape(B, S, H, dk).transpose(0, 2, 1, 3).astype(jnp.float32)
    v = v.reshape(B, S, H, dv).transpose(0, 2, 1, 3).astype(jnp.float32)
    pos = jnp.arange(S)
    inv_freq = jnp.power(jnp.float32(RET_ROT_BASE), -jnp.linspace(0.0, 1.0, dk // 2, dtype=jnp.float32))
    q = rotary(q, pos, dk, inv_freq)
    k = rotary(k, pos, dk, inv_freq) * (dk ** -0.5)

    log_gamma = jnp.log1p(-jnp.power(2.0, -5.0 - jnp.arange(H, dtype=jnp.float32)))
    n = jnp.arange(C, dtype=jnp.float32)
    diff = n[:, None] - n[None, :]
    tri = diff >= 0
    decay_mask = jnp.exp(jnp.where(tri[None], diff[None] * log_gamma[:, None, None], -jnp.inf))
    q_decay = jnp.exp((n[None, :] + 1.0) * log_gamma[:, None])
    k_decay = jnp.exp((C - 1.0 - n[None, :]) * log_gamma[:, None])
    chunk_decay = jnp.exp(C * log_gamma)

    nC = S // C
    def to_chunks(t):
        return t.reshape(B, H, nC, C, t.shape[-1]).transpose(2, 0, 1, 3, 4)

    def step(state, inp):
        qc, kc, vc = inp
        s = jnp.einsum('bhnd,bhmd->bhnm', qc, kc) * decay_mask[None]
        inner = jnp.einsum('bhnm,bhmv->bhnv', s, vc)
        cross = jnp.einsum('bhnd,bhdv->bhnv', qc, state) * q_decay[None, :, :, None]
        new_state = (state * chunk_decay[None, :, None, None]
                     + jnp.einsum('bhmd,bhmv->bhdv', kc * k_decay[None, :, :, None], vc))
        return new_state, inner + cross

    state0 = jnp.zeros((B, H, dk, dv), jnp.float32)
    _, out = lax.scan(step, state0, (to_chunks(q), to_chunks(k), to_chunks(v)))
    out = out.transpose(1, 0, 3, 2, 4).reshape(B, S, H, dv)
    out = out * lax.rsqrt(jnp.mean(out * out, axis=-1, keepdims=True) + RMS_EPS)
    out = out.reshape(B, S, H * dv) * gn_gain.astype(jnp.float32)
    y = (jax.nn.silu(g.astype(jnp.float32)) * out).astype(h.dtype)
    return y @ w_o


def setup_inputs(seed: int = 0) -> dict:
    key = jax.random.key(seed)
    ks = jax.random.split(key, 10)

    def w(k, shape, fan_in):
        return jax.random.normal(k, shape, jnp.float32) * (fan_in ** -0.5)

    x = jax.random.normal(ks[0], (BATCH, SEQ, D_MODEL), jnp.float32)
    norm_gain = 1.0 + 0.02 * jax.random.normal(ks[1], (DEPTH, 3, D_MODEL), jnp.float32)
    ffn_w_gate_up = w(ks[2], (DEPTH, 2, D_MODEL, 2 * FFN_DIM), D_MODEL)
    ffn_w_down = w(ks[3], (DEPTH, 2, FFN_DIM, D_MODEL), FFN_DIM)
    moba_w_qkv = w(ks[4], (N_MOBA_LAYERS, D_MODEL, 3 * MOBA_HEADS * MOBA_HEAD_DIM), D_MODEL)
    moba_w_o = w(ks[5], (N_MOBA_LAYERS, MOBA_HEADS * MOBA_HEAD_DIM, D_MODEL), MOBA_HEADS * MOBA_HEAD_DIM)
    ret_w_in = w(ks[6], (N_RET_LAYERS, D_MODEL, 2 * RET_HEADS * RET_KEY_DIM + 2 * RET_HEADS * RET_VAL_DIM), D_MODEL)
    ret_w_o = w(ks[7], (N_RET_LAYERS, RET_HEADS * RET_VAL_DIM, D_MODEL), RET_HEADS * RET_VAL_DIM)
    ret_gn_gain = 1.0 + 0.02 * jax.random.normal(ks[8], (N_RET_LAYERS, RET_HEADS * RET_VAL_DIM), jnp.float32)
    final_norm = 1.0 + 0.02 * jax.random.normal(ks[9], (D_MODEL,), jnp.float32)
    return {"x": x, "norm_gain": norm_gain, "ffn_w_gate_up": ffn_w_gate_up, "ffn_w_down": ffn_w_down,
            "moba_w_qkv": moba_w_qkv, "moba_w_o": moba_w_o, "ret_w_in": ret_w_in, "ret_w_o": ret_w_o,
            "ret_gn_gain": ret_gn_gain, "final_norm": final_norm}


def reference(x, norm_gain, ffn_w_gate_up, ffn_w_down, moba_w_qkv, moba_w_o,
              ret_w_in, ret_w_o, ret_gn_gain, final_norm):
    for i in range(DEPTH):
        g = norm_gain[i]
        x = x + FFN_RES * swiglu(rms_norm(x, g[0]), ffn_w_gate_up[i, 0], ffn_w_down[i, 0])
        h = rms_norm(x, g[1])
        j = i // N_MIXERS
        if i % N_MIXERS == 0:
            x = x + moba_attention(h, moba_w_qkv[j], moba_w_o[j])
        else:
            x = x + retention(h, ret_w_in[j], ret_w_o[j], ret_gn_gain[j])
        x = x + FFN_RES * swiglu(rms_norm(x, g[2]), ffn_w_gate_up[i, 1], ffn_w_down[i, 1])
    return rms_norm(x, final_norm)
```

```python
import os
from contextlib import ExitStack

import numpy as np
import concourse.bass as bass
import concourse.mybir as mybir
from concourse.bass_utils import run_bass_kernel_spmd

F32 = mybir.dt.float32
BF16 = mybir.dt.bfloat16
AF = mybir.ActivationFunctionType
ALU = mybir.AluOpType
AX = mybir.AxisListType

D = 2048
S = 2048
FF = 5632
T = 1024
NB = 6
EPS = 1e-6
NEG = -30000.0
EPOCH = 6000
ACTIVE = [0, 1, 4, 5]


class Op:
    __slots__ = ("eng", "fn", "deps", "signal", "sig", "chan")


class Sched:
    ENGS = ("pe", "act", "dve", "pool", "sp")

    def __init__(self):
        self.q = {e: [] for e in self.ENGS}
        self.lw = {}
        self.rd = {}
        self.bar = []
        self.chan_last = {}

    def add(self, eng, fn, r=(), w=(), chan=None, nobar=False):
        o = Op()
        o.eng = eng
        o.fn = fn
        o.signal = False
        o.sig = None
        o.chan = chan
        deps = set()
        if not nobar:
            deps.update(self.bar)
        for k in r:
            x = self.lw.get(k)
            if x is not None:
                deps.add(x)
        for k in w:
            x = self.lw.get(k)
            if x is not None:
                deps.add(x)
            rr = self.rd.get(k)
            if rr:
                deps.update(rr.values())
        for k in w:
            self.lw[k] = o
            self.rd[k] = {}
        for k in r:
            rr = self.rd.setdefault(k, {})
            rr[(eng if chan is None else ("dma", id(o)))] = o
        deps.discard(o)
        o.deps = deps
        for d in deps:
            d.signal = True
        self.q[eng].append(o)
        if chan is not None:
            self.chan_last[chan] = o
        return o

    def barrier(self, keep_prefix=("W",)):
        bar = []
        for e in self.ENGS:
            for o in reversed(self.q[e]):
                if o.chan is None:
                    bar.append(o)
                    o.signal = True
                    break
        for c, o in self.chan_last.items():
            bar.append(o)
        self.bar = bar
        self.lw = {k: v for k, v in self.lw.items() if k[0] in keep_prefix}
        self.rd = {k: v for k, v in self.rd.items() if k[0] in keep_prefix}

    def finalize(self):
        sem_names = []
        for e in self.ENGS:
            cnt = 0
            chan_cnt = {}
            for o in self.q[e]:
                if o.chan is not None:
                    pass
                elif o.signal:
                    cnt += 1
                    ep = (cnt - 1) // EPOCH
                    name = "s_%s_%d" % (e, ep)
                    if name not in sem_names:
                        sem_names.append(name)
                    o.sig = (name, cnt - ep * EPOCH)
        chan_cnt = {}
        for e in self.ENGS:
            for o in self.q[e]:
                if o.chan is not None:
                    n = chan_cnt.get(o.chan, 0) + 1
                    chan_cnt[o.chan] = n
                    name = "c_" + o.chan
                    if name not in sem_names:
                        sem_names.append(name)
                    o.sig = (name, 16 * n)
        return sem_names

    def emit(self, eng_name, e, sems):
        waited = {}
        for o in self.q[eng_name]:
            ws = {}
            for d in o.deps:
                if eng_name == "pe" and d.eng == "pe" and d.chan is None:
                    continue
                s, v = d.sig
                if waited.get(s, 0) >= v:
                    continue
                if ws.get(s, 0) < v:
                    ws[s] = v
            for s in sorted(ws):
                e.wait_ge(sems[s], ws[s])
                waited[s] = ws[s]
            if o.fn is None:
                continue
            ins = o.fn(e)
            if o.chan is not None:
                ins.then_inc(sems[o.sig[0]], 16)
            elif o.signal:
                ins.then_inc(sems[o.sig[0]], 1)


def _consts():
    c = {}
    pos = np.arange(S, dtype=np.float32)
    half = 16
    inv = np.power(np.float32(500000.0), -np.arange(half, dtype=np.float32) / np.float32(half)).astype(np.float32)
    ang = (pos[:, None] * inv[None, :]).astype(np.float32)
    cosv = np.cos(ang).astype(np.float32).T
    sinv = np.sin(ang).astype(np.float32).T
    cosM = np.ones((128, S), np.float32)
    sinM = np.zeros((128, S), np.float32)
    cosM[0:16] = cosv
    cosM[16:32] = cosv
    sinM[0:16] = -sinv
    sinM[16:32] = sinv
    c["cosM"] = cosM
    c["sinM"] = sinM
    perm = np.zeros((128, 128), np.float32)
    for d in range(16):
        perm[d + 16, d] = 1.0
        perm[d, d + 16] = 1.0
    c["permF"] = perm
    c["onesF"] = np.concatenate([np.full((128, 128), 1.0 / 2048.0, np.float32), np.full((128, 128), 1.0 / 256.0, np.float32)], axis=1)
    invr = np.power(np.float32(10000.0), -np.linspace(0.0, 1.0, 128, dtype=np.float32)).astype(np.float32)
    angr = (pos[:, None] * invr[None, :]).astype(np.float32)
    c["cosR"] = np.cos(angr).astype(np.float32).T.copy()
    c["sinR"] = np.sin(angr).astype(np.float32).T.copy()
    pastb = np.zeros((128, 16, 8), np.float32)
    for i in range(16):
        for n in range(8):
            if not (n < i // 2):
                pastb[:, i, n] = -1e30
    c["pastb"] = pastb.reshape(128, 128)
    import ml_dtypes
    cb = np.zeros((128, 2048), np.float32)
    cb[:, 0:128] = np.eye(128)
    cb[:, 128:256] = 1.0
    for n in range(8):
        cb[n, 256 + n * 128:256 + (n + 1) * 128] = 1.0
    kk = np.arange(128)[:, None]
    qq = np.arange(256)[None, :]
    for kt in range(2):
        cb[:, 1280 + kt * 256:1280 + (kt + 1) * 256] = np.where(kt * 128 + kk <= qq, 0.0, NEG)
    cb[:, 1792:1920] = 1.0 / 2048.0
    cb[:, 1920:2048] = 1.0 / 256.0
    c["cb"] = cb.astype(ml_dtypes.bfloat16)
    dt = np.zeros((8, 128, 5, 512), np.float64)
    p = np.arange(128)[:, None].astype(np.float64)
    nl = np.arange(512)[None, :].astype(np.float64)
    gam = []
    for h in range(8):
        g = 1.0 - 2.0 ** (-5.0 - h)
        gam.append(g)
        lg = np.log(g)
        for r in range(4):
            dd = nl - (r * 128 + p)
            dt[h, :, r, :] = np.where(dd >= 0, np.exp(dd * lg), 0.0)
        dt[h, :, 4, :] = np.exp((nl - p + 127.0) * lg)
    c["dtab"] = dt.astype(np.float32).reshape(8 * 128, 5 * 512)
    c["gam"] = gam
    return c


_C = None


def _get_consts():
    global _C
    if _C is None:
        _C = _consts()
    return _C


def build_program(stop_after=99, debug_out=None):
    C = _get_consts()
    gam = C["gam"]
    nc = bass.Bass("TRN2", target_bir_lowering=False)
    sc = Sched()

    def din(name, shape, dt=F32):
        return nc.dram_tensor(name, list(shape), dt, kind="ExternalInput").ap()

    def dscr(name, shape, dt):
        kind = "ExternalOutput" if (debug_out == name or debug_out == "all") else "Internal"
        return nc.dram_tensor(name, list(shape), dt, kind=kind).ap()

    xT_in = din("xT", [D, S])
    wgu = din("wgu", [4 * D, 2 * FF])
    wdn = din("wdn", [4 * FF, D])
    wqkv = din("wqkv", [D, 3 * D])
    wo_m = din("wo_m", [D, D])
    win = din("win", [D, 4 * D])
    wo_r = din("wo_r", [D, D])
    vecs_in = din("vecs", [128, 128])
    cosM_in = din("cosM", [128, S])
    sinM_in = din("sinM", [128, S])
    cosR_in = din("cosR", [128, S])
    sinR_in = din("sinR", [128, S])
    permF_in = din("permF", [128, 128])
    onesF_in = din("onesF", [128, 256])
    pastb_in = din("pastb", [128, 128])
    cb_in = din("cb", [128, 2048], BF16)
    dtab_in = din("dtab", [8 * 128, 5 * 512])
    if debug_out in ("outT", "all", None):
        outT = nc.dram_tensor("outT", [D, S], F32, kind="ExternalOutput").ap()
    else:
        outT = nc.dram_tensor("outT", [D, S], F32, kind="Internal").ap()
    xs0 = dscr("xs0", [D, S], F32)
    qs0 = dscr("qs0", [D, S], BF16)
    ks0 = dscr("ks0", [D, S], BF16)
    vs0 = dscr("vs0", [S, D], BF16)
    os0 = dscr("os0", [D, S], BF16)
    xs1 = dscr("xs1", [D, S], F32)
    qs1 = dscr("qs1", [D, S], BF16)
    ks1 = dscr("ks1", [D, S], BF16)
    vs1 = dscr("vs1", [S, D], BF16)
    os1 = dscr("os1", [D, S], BF16)
    gs = dscr("gs", [D, S], BF16)

    es = ExitStack()
    with es:
        def sb(name, shape, dt):
            return es.enter_context(nc.sbuf_tensor(name, list(shape), dt))

        XF = sb("XF", [128, 16, T], F32)
        HB = sb("HB", [128, 16, T], BF16)
        WR = sb("WR", [128, NB, 4096], BF16)
        ACTS = sb("ACTS", [128, 2, 2, T], BF16)
        FT = sb("FT", [128, 8, 512], F32)
        BT = sb("BT", [128, 4, T], BF16)
        RSTD = sb("RSTD", [128, T], F32)
        TAB = sb("TAB", [128, 2, T], F32)
        VECS = sb("VECS", [128, 128], F32)
        PERMF = sb("PERMF", [128, 128], F32)
        PASTB = sb("PASTB", [128, 128], F32)
        CB = sb("CB", [128, 2048], BF16)
        KMS = sb("KMS", [128, 16, 8], F32)
        SM = sb("SM", [128, 1024], F32)
        SMB = sb("SMB", [128, 4608], BF16)
        PS = [es.enter_context(nc.psum_tensor("ps%d" % i, [128, 512], F32)) for i in range(7)]
        PST = es.enter_context(nc.psum_tensor("pst", [128, 1024], BF16))

        IDB = CB[:, 0:128]
        ONESB = CB[:, 128:256]
        SELALL = CB[0:8, 256:1280]
        CAUS = CB[:, 1280:1792]
        AVG_D = CB[:, 1792:1920]
        AVG_G = CB[:, 1920:2048]

        ctr = {"w": 0, "ft": 0, "bt": 0}

        def dma(eng, out, in_, r, w, chan, nobar=False):
            return sc.add(eng, lambda e: e.dma_start(out=out, in_=in_), r=r, w=w, chan=chan, nobar=nobar)

        def wpiece_col(W2d, c0):
            p = ctr["w"]
            ctr["w"] += 1
            slot = p % NB
            dst = WR[:, slot, :].rearrange("p (k c) -> p k c", c=256)
            src = W2d[:, c0:c0 + 256].rearrange("(k p) c -> p k c", p=128)
            dma("pool", dst, src, (), [("W", slot)], "w%d" % slot, nobar=True)
            return slot, dst

        def wpiece_row(W2d, r0):
            p = ctr["w"]
            ctr["w"] += 1
            slot = p % NB
            dst = WR[:, slot, :].rearrange("p (k c) -> p k c", c=2048)
            src = W2d[r0:r0 + 256, :].rearrange("(k p) c -> p k c", p=128)
            dma("pool", dst, src, (), [("W", slot)], "w%d" % slot, nobar=True)
            return slot, dst

        def ft_new():
            i = ctr["ft"] % 8
            ctr["ft"] += 1
            return FT[:, i, :], ("FT", i)

        def bt_new():
            i = ctr["bt"] % 4
            ctr["bt"] += 1
            return BT[:, i, :], ("BT", i), i

        def mm(out, lhsT, rhs, start, stop, r, w):
            sc.add("pe", lambda e: e.matmul(out, lhsT, rhs, start=start, stop=stop), r=r, w=w)

        def act(out, in_, func, r, w, scale=1.0):
            sc.add("act", lambda e: e.activation(out=out, in_=in_, func=func, scale=scale), r=r, w=w)

        def tt(out, in0, in1, op, r, w, eng="dve"):
            sc.add(eng, lambda e: e.tensor_tensor(out=out, in0=in0, in1=in1, op=op), r=r, w=w)

        def ts(out, in0, s1, s2, op0, op1, r, w, eng="dve"):
            if op1 is None:
                sc.add(eng, lambda e: e.tensor_scalar(out=out, in0=in0, scalar1=s1, scalar2=None, op0=op0), r=r, w=w)
            else:
                sc.add(eng, lambda e: e.tensor_scalar(out=out, in0=in0, scalar1=s1, scalar2=s2, op0=op0, op1=op1), r=r, w=w)

        def stt(out, in0, scalar, in1, op0, op1, r, w, eng="dve"):
            sc.add(eng, lambda e: e.scalar_tensor_tensor(out=out, in0=in0, scalar=scalar, in1=in1, op0=op0, op1=op1), r=r, w=w)

        def nsl(n):
            return slice(n * 512, (n + 1) * 512)

        dma("sp", VECS[:], vecs_in, (), [("VECS",)], "c0")
        sc.add("dve", lambda e: e.memset(KMS[:], 0.0), w=[("KMSINIT",)])
        dma("sp", PERMF[:], permF_in, (), [("PERMF",)], "c2")
        dma("sp", PASTB[:], pastb_in, (), [("PASTB",)], "c3")
        dma("sp", CB[:], cb_in, (), [("CB",)], "c4")
        sc.barrier()

        def load_x(hf, src):
            t0 = hf * T
            for g4 in range(4):
                dma("sp", XF[:, g4 * 4:(g4 + 1) * 4, :],
                    src[g4 * 512:(g4 + 1) * 512, t0:t0 + T].rearrange("(k p) t -> p k t", p=128),
                    (), [("X", k, n) for k in range(g4 * 4, g4 * 4 + 4) for n in range(2)], "x%d" % g4)

        def store_x(hf, dst):
            t0 = hf * T
            for g4 in range(4):
                dma("sp", dst[g4 * 512:(g4 + 1) * 512, t0:t0 + T].rearrange("(k p) t -> p k t", p=128),
                    XF[:, g4 * 4:(g4 + 1) * 4, :],
                    [("X", k, n) for k in range(g4 * 4, g4 * 4 + 4) for n in range(2)], (), "x%d" % g4)

        def load_h(hf, src):
            t0 = hf * T
            for g4 in range(4):
                dma("sp", HB[:, g4 * 4:(g4 + 1) * 4, :],
                    src[g4 * 512:(g4 + 1) * 512, t0:t0 + T].rearrange("(k p) t -> p k t", p=128),
                    (), [("H", k, n) for k in range(g4 * 4, g4 * 4 + 4) for n in range(2)], "h%d" % g4)

        def rms_to(gcol, out_fn):
            for n in range(2):
                for k in range(16):
                    sqf, sqk = ft_new()
                    sq = sqf.bitcast(BF16)[:, 0:512]
                    act(sq, XF[:, k, nsl(n)], AF.Square, [("X", k, n)], [sqk])
                    mm(PS[6][:], AVG_D, sq, k == 0, k == 15, [sqk, ("CB",)], [("ps", 6)])
                sc.add("act", lambda e, o=RSTD[:, nsl(n)]: e.activation(out=o, in_=PS[6][:], func=AF.Sqrt, bias=EPS, scale=1.0),
                       r=[("ps", 6)], w=[("RSTD", n)])
                sc.add("dve", lambda e, o=RSTD[:, nsl(n)]: e.reciprocal(out=o, in_=o), r=[("RSTD", n)], w=[("RSTD", n)])
                for k in range(16):
                    out_fn(k, n)

        def norm_h(vcol0):
            def o(k, n):
                stt(HB[:, k, nsl(n)], XF[:, k, nsl(n)], VECS[:, vcol0 + k:vcol0 + k + 1], RSTD[:, nsl(n)],
                    ALU.mult, ALU.mult, [("X", k, n), ("RSTD", n), ("VECS",)], [("H", k, n)])
            rms_to(vcol0, o)

        def ffn(lj):
            Wgu = wgu[lj * D:(lj + 1) * D, :]
            Wdn = wdn[lj * FF:(lj + 1) * FF, :]
            NS = FF // 256
            tile_ctr = [0]
            dctr = [0]

            def gu_tile(s, q, WG, WU, sg_slot, su_slot):
                aslot = s % 2
                c, n = divmod(q, 2)
                ti = tile_ctr[0]
                tile_ctr[0] += 1
                pg = PS[ti % 2]
                pu = PS[2 + ti % 2]
                for k in range(16):
                    mm(pg[:], WG[:, k, c * 128:(c + 1) * 128], HB[:, k, nsl(n)], k == 0, k == 15,
                       [("W", sg_slot), ("H", k, n)], [("ps", ti % 2)])
                for k in range(16):
                    mm(pu[:], WU[:, k, c * 128:(c + 1) * 128], HB[:, k, nsl(n)], k == 0, k == 15,
                       [("W", su_slot), ("H", k, n)], [("ps", 2 + ti % 2)])
                sg, sgk = ft_new()
                act(sg, pg[:], AF.Silu, [("ps", ti % 2)], [sgk])
                tt(ACTS[:, aslot, c, nsl(n)], sg, pu[:], ALU.mult, [sgk, ("ps", 2 + ti % 2)],
                   [("A", aslot, c, n)])

            def down_tiles(s, WD, sd_slot, lo, hi):
                aslot = s % 2
                for idx in range(lo, hi):
                    m, n = divmod(idx, 2)
                    di = dctr[0]
                    dctr[0] += 1
                    bk = 4 + di % 3
                    pd = PS[bk]
                    for c in range(2):
                        mm(pd[:], WD[:, c, m * 128:(m + 1) * 128], ACTS[:, aslot, c, nsl(n)], c == 0, c == 1,
                           [("W", sd_slot), ("A", aslot, c, n)], [("ps", bk)])
                    if idx % 2 == 0:
                        stt(XF[:, m, nsl(n)], pd[:], 0.5, XF[:, m, nsl(n)], ALU.mult, ALU.add,
                            [("ps", bk), ("X", m, n)], [("X", m, n)])
                    else:
                        tmp, tmpk = ft_new()
                        act(tmp, pd[:], AF.Copy, [("ps", bk)], [tmpk], scale=0.5)
                        tt(XF[:, m, nsl(n)], tmp, XF[:, m, nsl(n)], ALU.add, [tmpk, ("X", m, n)], [("X", m, n)],
                           eng="pool")

            prev = None
            for s in range(NS):
                sg_slot, WG = wpiece_col(Wgu, s * 256)
                su_slot, WU = wpiece_col(Wgu, FF + s * 256)
                for q in range(4):
                    gu_tile(s, q, WG, WU, sg_slot, su_slot)
                    if prev is not None:
                        down_tiles(s - 1, prev[1], prev[0], q * 8, (q + 1) * 8)
                prev = wpiece_row(Wdn, s * 256)
            down_tiles(NS - 1, prev[1], prev[0], 0, 32)

        def out_proj(W2d, scale=1.0):
            dctr = [0]
            for i in range(8):
                slot, WP = wpiece_col(W2d, i * 256)
                for c in range(2):
                    m = 2 * i + c
                    for n in range(2):
                        di = dctr[0]
                        dctr[0] += 1
                        pd = PS[4 + di % 2]
                        for k in range(16):
                            mm(pd[:], WP[:, k, c * 128:(c + 1) * 128], HB[:, k, nsl(n)], k == 0, k == 15,
                               [("W", slot), ("H", k, n)], [("ps", 4 + di % 2)])
                        tt(XF[:, m, nsl(n)], pd[:], XF[:, m, nsl(n)], ALU.add,
                           [("ps", 4 + di % 2), ("X", m, n)], [("X", m, n)])

        def load_tab(hf, cos_in, sin_in):
            t0 = hf * T
            dma("sp", TAB[:, 0, :], cos_in[:, t0:t0 + T], (), [("TAB", 0)], "tab0")
            dma("sp", TAB[:, 1, :], sin_in[:, t0:t0 + T], (), [("TAB", 1)], "tab1")

        def wpiece_col2(W2d, c0):
            if ctr["w"] % 2 == 1:
                ctr["w"] += 1
            p = ctr["w"]
            ctr["w"] += 2
            slot = p % NB
            dst = WR[:, slot:slot + 2, :].rearrange("p s x -> p (s x)").rearrange("p (k c) -> p k c", c=512)
            src = W2d[:, c0:c0 + 512].rearrange("(k p) c -> p k c", p=128)
            dma("pool", dst, src, (), [("W", slot), ("W", slot + 1)], "w%d" % slot, nobar=True)
            return slot, dst

        def v_proj(hf, W2d, col0, vs):
            t0 = hf * T
            tctr = [0]
            for i in range(4):
                slot, WP = wpiece_col2(W2d, col0 + i * 512)
                for g in range(4):
                    vst, vk, vi = bt_new()
                    vst3 = vst.rearrange("p (t c) -> p t c", c=512)
                    for tl in range(2):
                        ttile = g * 2 + tl
                        ti = tctr[0]
                        tctr[0] += 1
                        pv = PS[ti % 2]
                        for k in range(16):
                            mm(pv[:], HB[:, k, ttile * 128:(ttile + 1) * 128], WP[:, k, :], k == 0, k == 15,
                               [("W", slot), ("W", slot + 1), ("H", k, ttile // 4)], [("ps", ti % 2)])
                        act(vst3[:, tl, :], pv[:], AF.Copy, [("ps", ti % 2)], [vk])
                    dma("sp", vs[t0 + g * 256:t0 + (g + 1) * 256, i * 512:(i + 1) * 512].rearrange("(t p) c -> p t c", p=128),
                        vst3, [vk], [("vs", hf, i, g)], "bt%d" % vi)

        def moba_qk(hf, col0, dst, is_k):
            t0 = hf * T
            tctr = [0]
            for i in range(8):
                slot, WP = wpiece_col(wqkv, col0 + i * 256)
                for c in range(2):
                    head = 2 * i + c
                    st, stk, sti = bt_new()
                    for n in range(2):
                        ti = tctr[0]
                        tctr[0] += 1
                        pa = PS[ti % 2]
                        pb = PS[2 + ti % 2]
                        for k in range(16):
                            mm(pa[:], WP[:, k, c * 128:(c + 1) * 128], HB[:, k, nsl(n)], k == 0, k == 15,
                               [("W", slot), ("H", k, n)], [("ps", ti % 2)])
                        kf, kfk = ft_new()
                        act(kf, pa[:], AF.Copy, [("ps", ti % 2)], [kfk])
                        mm(pb[:], PERMF[:], kf, True, True, [kfk, ("PERMF",)], [("ps", 2 + ti % 2)])
                        t1, t1k = ft_new()
                        tt(t1, kf, TAB[:, 0, nsl(n)], ALU.mult, [kfk, ("TAB", 0)], [t1k])
                        t2, t2k = ft_new()
                        tt(t2, pb[:], TAB[:, 1, nsl(n)], ALU.mult, [("ps", 2 + ti % 2), ("TAB", 1)], [t2k])
                        tt(t1, t1, t2, ALU.add, [t1k, t2k], [t1k])
                        if is_k:
                            for b in range(2):
                                bi = hf * 4 + n * 2 + b
                                sc.add("act", lambda e, o=st[:, n * 512 + b * 256:n * 512 + (b + 1) * 256],
                                       i_=t1[:, b * 256:(b + 1) * 256], a_=KMS[:, head, bi:bi + 1]:
                                       e.activation(out=o, in_=i_, func=AF.Copy, accum_out=a_),
                                       r=[t1k], w=[stk, ("KMS", head, hf, n)])
                        else:
                            act(st[:, nsl(n)], t1, AF.Copy, [t1k], [stk])
                    dma("sp", dst[head * 128:(head + 1) * 128, t0:t0 + T], st, [stk], [("qk", is_k, head, hf)],
                        "bt%d" % sti)

        for hf in range(2):
            load_x(hf, xT_in)
            load_tab(hf, cosM_in, sinM_in)
            norm_h(0 * 16)
            ffn(0)
            norm_h(1 * 16)
            moba_qk(hf, D, ks0, True)
            moba_qk(hf, 0, qs0, False)
            v_proj(hf, wqkv, 2 * D, vs0)
            store_x(hf, xs0)
            sc.barrier()

        SCALE = 128.0 ** -0.5
        HBf = HB[:].rearrange("p k t -> p (k t)")

        def s2_bufs(hd):
            sl = hd % 2
            QT = HBf[:, sl * 2048:(sl + 1) * 2048]
            KT = HBf[:, 4096 + sl * 2048:4096 + (sl + 1) * 2048]
            VV = HBf[:, 8192 + sl * 2048:8192 + (sl + 1) * 2048].rearrange("p (i d) -> p i d", d=128)
            OST = HBf[:, 12288 + sl * 2048:12288 + (sl + 1) * 2048]
            pb = sl * 2304
            KMB = SMB[:, pb:pb + 8]
            BIASQ = SMB[:, pb + 128:pb + 256]
            BIAST = SMB[0:8, pb + 256:pb + 2304]
            return sl, QT, KT, VV, OST, KMB, BIASQ, BIAST

        def s2_prologue(hd):
            sl, QT, KT, VV, OST, KMB, BIASQ, BIAST = s2_bufs(hd)
            dma("sp", QT, qs0[hd * 128:(hd + 1) * 128, :], (), [("QT", sl)], "qt%d" % sl)
            dma("sp", KT, ks0[hd * 128:(hd + 1) * 128, :], (), [("KT", sl)], "kt%d" % sl)
            dma("sp", VV, vs0[:, hd * 128:(hd + 1) * 128].rearrange("(i p) d -> p i d", p=128), (), [("VV", sl)],
                "vv%d" % sl)
            act(KMB, KMS[:, hd, :], AF.Copy, [("KMS", hd, 0, 0), ("KMS", hd, 0, 1), ("KMS", hd, 1, 0), ("KMS", hd, 1, 1)],
                [("KMB", sl)])
            for i in range(16):
                mm(PS[0][:, i * 8:(i + 1) * 8], QT[:, i * 128:(i + 1) * 128], KMB, True, True,
                   [("QT", sl), ("KMB", sl)], [("ps", 0)])
            sb_ = sl * 512
            GM = SM[:, sb_:sb_ + 128]
            TOP8 = SM[:, sb_ + 128:sb_ + 256]
            THR = SM[:, sb_ + 256:sb_ + 272]
            GE = SM[:, sb_ + 384:sb_ + 512]
            tt(GM, PS[0][:, 0:128], PASTB[:], ALU.add, [("ps", 0), ("PASTB",)], [("GM", sl)])
            for i in range(16):
                sc.add("dve", lambda e, o=TOP8[:, i * 8:(i + 1) * 8], i_=GM[:, i * 8:(i + 1) * 8]: e.max(out=o, in_=i_),
                       r=[("GM", sl)], w=[("TOP8", sl, i)])
            ts(THR, TOP8.rearrange("p (i e) -> p i e", e=8)[:, :, 2], -1e29, None, ALU.max, None,
               [("TOP8", sl, i) for i in range(16)], [("THR", sl)])
            tt(GE.rearrange("p (i e) -> p i e", e=8), GM.rearrange("p (i e) -> p i e", e=8),
               THR.unsqueeze(2).to_broadcast([128, 16, 8]), ALU.is_ge, [("GM", sl), ("THR", sl)], [("GE", sl)])
            ts(BIASQ, GE, -1.0, -NEG, ALU.add, ALU.mult, [("GE", sl)], [("BIASQ", sl)])
            for hh in range(2):
                for i8 in range(8):
                    i = hh * 8 + i8
                    sc.add("pe", lambda e, o=PST[0:8, i8 * 128:(i8 + 1) * 128], i_=BIASQ[:, i * 8:(i + 1) * 8]:
                           e.transpose(out=o, in_=i_, identity=IDB), r=[("BIASQ", sl), ("CB",)], w=[("pst",)])
                act(BIAST[:, hh * 1024:(hh + 1) * 1024], PST[0:8, :], AF.Copy, [("pst",)], [("BIAST", sl, hh)])

        s2ctr = {"t": 0, "e": 0, "c": 0}

        def s2_main(hd):
            sl, QT, KT, VV, OST, KMB, BIASQ, BIAST = s2_bufs(hd)
            pend = []

            for j in range(4):
                qlo = j * 512
                tiles = [(nb, kt, 0, 512) for nb in range(2 * j + 1) for kt in range(2)]
                tiles += [(2 * j + 1, kt, 256, 512) for kt in range(2)]
                cpar = s2ctr["c"] % 2
                s2ctr["c"] += 1
                bo, bd = (3, 4) if cpar == 0 else (5, 6)
                po, pdn = PS[bo], PS[bd]
                nt = len(tiles)
                for idx, (nb, kt, c0, c1) in enumerate(tiles):
                    sj = s2ctr["t"] % 3
                    s2ctr["t"] += 1
                    ei = s2ctr["e"] % 4
                    s2ctr["e"] += 1
                    pS = PS[sj]
                    kpos = nb * 256 + kt * 128
                    mm(pS[:, c0:c1], KT[:, kpos:kpos + 128], QT[:, qlo + c0:qlo + c1], True, False,
                       [("KT", sl), ("QT", sl)], [("ps", sj)])
                    if nb < 2 * j:
                        mm(pS[:, 0:512], SELALL[:, nb * 128:(nb + 1) * 128], BIAST[:, qlo:qlo + 512], False, True,
                           [("CB",), ("BIAST", sl, j // 2)], [("ps", sj)])
                    elif nb == 2 * j:
                        mm(pS[:, 0:256], IDB, CAUS[:, kt * 256:(kt + 1) * 256], False, False, [("CB",)], [("ps", sj)])
                        mm(pS[:, 256:512], SELALL[:, nb * 128:(nb + 1) * 128], BIAST[:, qlo + 256:qlo + 512], False, True,
                           [("CB",), ("BIAST", sl, j // 2)], [("ps", sj)])
                    else:
                        mm(pS[:, 256:512], IDB, CAUS[:, kt * 256:(kt + 1) * 256], False, True, [("CB",)], [("ps", sj)])
                    E = BT[:, ei, 0:512]
                    act(E[:, c0:c1], pS[:, c0:c1], AF.Exp, [("ps", sj)], [("BT", ei)], scale=SCALE)
                    def C(nb=nb, kt=kt, c0=c0, c1=c1, E=E, ei=ei, idx=idx, po=po, pdn=pdn, bo=bo, bd=bd, nt=nt,
                          qlo=qlo):
                        first = idx == 0
                        last = idx == nt - 1
                        mm(po[:, c0:c1], VV[:, nb * 2 + kt, :], E[:, c0:c1], first, last, [("VV", sl), ("BT", ei)],
                           [("ps", bo)])
                        mm(pdn[:, c0:c1], ONESB, E[:, c0:c1], first, last, [("CB",), ("BT", ei)], [("ps", bd)])
                        if last:
                            rd, rdk = ft_new()
                            sc.add("dve", lambda e, o=rd, i_=pdn[:]: e.reciprocal(out=o, in_=i_), r=[("ps", bd)], w=[rdk])
                            tt(OST[:, qlo:qlo + 512], po[:], rd, ALU.mult, [("ps", bo), rdk], [("OST", sl)])
                    pend.append(C)
                    if len(pend) > 1:
                        pend.pop(0)()
            while pend:
                pend.pop(0)()
            dma("sp", os0[hd * 128:(hd + 1) * 128, :], OST, [("OST", sl)], [("os", hd)], "ost%d" % sl)

        if stop_after >= 2:
            s2_prologue(0)
            for hd in range(16):
                if hd + 1 < 16:
                    s2_prologue(hd + 1)
                s2_main(hd)
            sc.barrier()

        def ret_qk(hf, col0, dst, kscale):
            t0 = hf * T
            tctr = [0]
            for h in range(8):
                slot, WP = wpiece_col(win, col0 + h * 256)
                sa, sak, sai = bt_new()
                sbb, sbk, sbi = bt_new()
                for n in range(2):
                    ti = tctr[0]
                    tctr[0] += 1
                    pa = PS[ti % 2]
                    pb = PS[2 + ti % 2]
                    for k in range(16):
                        mm(pa[:], WP[:, k, 0:128], HB[:, k, nsl(n)], k == 0, k == 15,
                           [("W", slot), ("H", k, n)], [("ps", ti % 2)])
                    for k in range(16):
                        mm(pb[:], WP[:, k, 128:256], HB[:, k, nsl(n)], k == 0, k == 15,
                           [("W", slot), ("H", k, n)], [("ps", 2 + ti % 2)])
                    af, afk = ft_new()
                    bf, bfk = ft_new()
                    act(af, pa[:], AF.Copy, [("ps", ti % 2)], [afk], scale=kscale)
                    act(bf, pb[:], AF.Copy, [("ps", 2 + ti % 2)], [bfk], scale=kscale)
                    cosT = TAB[:, 0, nsl(n)]
                    sinT = TAB[:, 1, nsl(n)]
                    t1, t1k = ft_new()
                    t2, t2k = ft_new()
                    tt(t1, af, cosT, ALU.mult, [afk, ("TAB", 0)], [t1k])
                    tt(t2, bf, sinT, ALU.mult, [bfk, ("TAB", 1)], [t2k])
                    tt(sa[:, nsl(n)], t1, t2, ALU.subtract, [t1k, t2k], [sak])
                    t3, t3k = ft_new()
                    t4, t4k = ft_new()
                    tt(t3, af, sinT, ALU.mult, [afk, ("TAB", 1)], [t3k])
                    tt(t4, bf, cosT, ALU.mult, [bfk, ("TAB", 0)], [t4k])
                    tt(sbb[:, nsl(n)], t3, t4, ALU.add, [t3k, t4k], [sbk])
                dma("sp", dst[h * 256:h * 256 + 128, t0:t0 + T], sa, [sak], [("rqk", col0, h, 0, hf)], "bt%d" % sai)
                dma("sp", dst[h * 256 + 128:h * 256 + 256, t0:t0 + T], sbb, [sbk], [("rqk", col0, h, 1, hf)], "bt%d" % sbi)

        def ret_g(hf):
            t0 = hf * T
            tctr = [0]
            for h in range(8):
                slot, WP = wpiece_col(win, 3 * D + h * 256)
                for c in range(2):
                    st, stk, sti = bt_new()
                    for n in range(2):
                        ti = tctr[0]
                        tctr[0] += 1
                        pa = PS[ti % 2]
                        for k in range(16):
                            mm(pa[:], WP[:, k, c * 128:(c + 1) * 128], HB[:, k, nsl(n)], k == 0, k == 15,
                               [("W", slot), ("H", k, n)], [("ps", ti % 2)])
                        act(st[:, nsl(n)], pa[:], AF.Silu, [("ps", ti % 2)], [stk])
                    dma("sp", gs[h * 256 + c * 128:h * 256 + (c + 1) * 128, t0:t0 + T], st, [stk], [("gs", h, c, hf)],
                        "bt%d" % sti)

        if stop_after >= 3:
            for hf in range(2):
                load_x(hf, xs0)
                load_h(hf, os0)
                load_tab(hf, cosR_in, sinR_in)
                out_proj(wo_m)
                norm_h(2 * 16)
                ffn(1)
                norm_h(3 * 16)
                ffn(2)
                norm_h(4 * 16)
                ret_qk(hf, 0, qs1, 1.0)
                ret_qk(hf, D, ks1, 1.0 / 16.0)
                v_proj(hf, win, 2 * D, vs1)
                ret_g(hf)
                store_x(hf, xs1)
                sc.barrier()

        if stop_after >= 4:
            XFf = XF[:].rearrange("p k t -> p (k t)")
            s4ctr = [0]
            XFb = XFf.bitcast(BF16)

            def s4_bufs(h):
                par = h % 2
                B0 = HBf if par == 0 else XFb[:, 16384:32768]
                QT0 = B0[:, 0:2048]
                QT1 = B0[:, 2048:4096]
                KT0 = B0[:, 4096:6144]
                KT1 = B0[:, 6144:8192]
                VV = B0[:, 8192:12288].rearrange("p (i d) -> p i d", d=256)
                GS0 = B0[:, 12288:14336]
                GS1 = B0[:, 14336:16384]
                DT = XFf[:, par * 2560:(par + 1) * 2560].rearrange("p (r c) -> p r c", c=512)
                return par, QT0, QT1, KT0, KT1, VV, GS0, GS1, DT

            def s4_load(h):
                par, QT0, QT1, KT0, KT1, VV, GS0, GS1, DT = s4_bufs(h)
                dma("sp", QT0, qs1[h * 256:h * 256 + 128, :], (), [("RQ", par, 0)], "rq0%d" % par)
                dma("sp", QT1, qs1[h * 256 + 128:h * 256 + 256, :], (), [("RQ", par, 1)], "rq1%d" % par)
                dma("sp", KT0, ks1[h * 256:h * 256 + 128, :], (), [("RK", par, 0)], "rk0%d" % par)
                dma("sp", KT1, ks1[h * 256 + 128:h * 256 + 256, :], (), [("RK", par, 1)], "rk1%d" % par)
                dma("sp", VV, vs1[:, h * 256:(h + 1) * 256].rearrange("(i p) d -> p i d", p=128), (), [("RV", par)],
                    "rv%d" % par)
                dma("sp", GS0, gs[h * 256:h * 256 + 128, :], (), [("RG", par, 0)], "rg0%d" % par)
                dma("sp", GS1, gs[h * 256 + 128:h * 256 + 256, :], (), [("RG", par, 1)], "rg1%d" % par)
                dma("sp", DT, dtab_in[h * 128:(h + 1) * 128, :].rearrange("p (r c) -> p r c", c=512), (),
                    [("DT", par)], "dt%d" % par)

            s4_load(0)
            for h in range(8):
                g_ = gam[h]
                if h + 1 < 8:
                    s4_load(h + 1)
                par, QT0, QT1, KT0, KT1, VV, GS0, GS1, DT = s4_bufs(h)
                YST = [BT[:, 0, :], BT[:, 1, :], BT[:, 2, :], BT[:, 3, :]]
                QTs = [QT0, QT1]
                KTs = [KT0, KT1]
                GSs = [GS0, GS1]
                pend = []
                for j in range(4):
                    jsl = slice(j * 512, (j + 1) * 512)
                    nm = 4 * j + 4
                    ob = (3, 4) if j % 2 == 0 else (5, 6)
                    for i in range(nm):
                        sj = s4ctr[0] % 3
                        s4ctr[0] += 1
                        pS = PS[sj]
                        for dc in range(2):
                            mm(pS[:], KTs[dc][:, i * 128:(i + 1) * 128], QTs[dc][:, jsl], dc == 0, dc == 1,
                               [("RK", par, dc), ("RQ", par, dc)], [("ps", sj)])
                        STt = SMB[:, sj * 512:(sj + 1) * 512]
                        if i >= 4 * j:
                            tt(STt, pS[:], DT[:, i - 4 * j, :], ALU.mult, [("ps", sj), ("DT", par)], [("ST", sj)])
                        else:
                            cst = float(g_ ** (j * 512 - i * 128 - 127))
                            stt(STt, pS[:], cst, DT[:, 4, :], ALU.mult, ALU.mult, [("ps", sj), ("DT", par)],
                                [("ST", sj)])

                        def C(i=i, nm=nm, STt=STt, sj=sj, ob=ob, j=j, jsl=jsl):
                            for vc in range(2):
                                mm(PS[ob[vc]][:], VV[:, i, vc * 128:(vc + 1) * 128], STt, i == 0, i == nm - 1,
                                   [("RV", par), ("ST", sj)], [("ps", ob[vc])])
                            if i != nm - 1:
                                return
                            nj = s4ctr[0] % 3
                            s4ctr[0] += 1
                            pN = PS[nj]
                            for vc in range(2):
                                sqf, sqk = ft_new()
                                sq = sqf.bitcast(BF16)[:, 0:512]
                                act(sq, PS[ob[vc]][:], AF.Square, [("ps", ob[vc])], [sqk])
                                mm(pN[:], AVG_G, sq, vc == 0, vc == 1, [sqk, ("CB",)], [("ps", nj)])
                            rs, rsk = ft_new()
                            sc.add("act", lambda e, o=rs, p_=pN: e.activation(out=o, in_=p_[:], func=AF.Sqrt, bias=EPS, scale=1.0),
                                   r=[("ps", nj)], w=[rsk])
                            sc.add("dve", lambda e, o=rs: e.reciprocal(out=o, in_=o), r=[rsk], w=[rsk])
                            for vc in range(2):
                                t1, t1k = ft_new()
                                col = 112 + h * 2 + vc
                                stt(t1, PS[ob[vc]][:], VECS[:, col:col + 1], rs, ALU.mult, ALU.mult,
                                    [("ps", ob[vc]), rsk, ("VECS",)], [t1k])
                                yslot = vc * 2 + j // 2
                                tt(YST[yslot][:, (j % 2) * 512:(j % 2 + 1) * 512], t1, GSs[vc][:, jsl], ALU.mult,
                                   [t1k, ("RG", par, vc)], [("BT", yslot)])
                        pend.append(C)
                        if len(pend) > 1:
                            pend.pop(0)()
                while pend:
                    pend.pop(0)()
                for vc in range(2):
                    for hh in range(2):
                        yslot = vc * 2 + hh
                        dma("sp", os1[h * 256 + vc * 128:h * 256 + (vc + 1) * 128, hh * T:(hh + 1) * T], YST[yslot],
                            [("BT", yslot)], [("ys", h, vc, hh)], "bt%d" % yslot)
            sc.barrier()

        if stop_after >= 5:
            for hf in range(2):
                t0 = hf * T
                load_x(hf, xs1)
                load_h(hf, os1)
                out_proj(wo_r)
                norm_h(5 * 16)
                ffn(3)

                def o(k, n):
                    ot, otk = ft_new()
                    stt(ot, XF[:, k, nsl(n)], VECS[:, 96 + k:97 + k], RSTD[:, nsl(n)], ALU.mult, ALU.mult,
                        [("X", k, n), ("RSTD", n), ("VECS",)], [otk])
                    dma("sp", outT[k * 128:(k + 1) * 128, t0 + n * 512:t0 + (n + 1) * 512], ot, [otk],
                        [("out", k, n, hf)], "o%d" % (otk[1]))
                rms_to(96, o)
                sc.barrier()

        if debug_out == "xs":
            pass
        sc.barrier()
        sc.add("sp", None)

        sem_names = sc.finalize()
        sems = {}
        for nme in sem_names:
            sems[nme] = es.enter_context(nc.semaphore(nme))
        block = es.enter_context(nc.Block())

        @block.tensor
        def _(e):
            sc.emit("pe", e, sems)

        @block.scalar
        def _(e):
            sc.emit("act", e, sems)

        @block.vector
        def _(e):
            sc.emit("dve", e, sems)

        @block.gpsimd
        def _(e):
            sc.emit("pool", e, sems)

        @block.sync
        def _(e):
            sc.emit("sp", e, sems)

    return nc


def _prep_inputs(x, norm_gain, ffn_w_gate_up, ffn_w_down, moba_w_qkv, moba_w_o,
                 ret_w_in, ret_w_o, ret_gn_gain, final_norm):
    C = _get_consts()
    f = lambda a: np.ascontiguousarray(np.asarray(a, dtype=np.float32))
    vecs = np.zeros((128, 128), np.float32)
    ng = f(norm_gain).reshape(6, 16, 128)
    vecs[:, 0:96] = ng.transpose(2, 0, 1).reshape(128, 96)
    vecs[:, 96:112] = f(final_norm).reshape(16, 128).T
    vecs[:, 112:128] = f(ret_gn_gain).reshape(16, 128).T
    shared = {
        "wgu": f(ffn_w_gate_up).reshape(4 * D, 2 * FF),
        "wdn": f(ffn_w_down).reshape(4 * FF, D),
        "wqkv": f(moba_w_qkv).reshape(D, 3 * D),
        "wo_m": f(moba_w_o).reshape(D, D),
        "win": f(ret_w_in).reshape(D, 4 * D),
        "wo_r": f(ret_w_o).reshape(D, D),
        "vecs": vecs,
        "cosM": C["cosM"], "sinM": C["sinM"], "cosR": C["cosR"], "sinR": C["sinR"],
        "permF": C["permF"], "onesF": C["onesF"], "pastb": C["pastb"], "cb": C["cb"], "dtab": C["dtab"],
    }
    xf = f(x)
    in_maps = []
    zx = np.zeros((D, S), np.float32)
    for c in range(8):
        m = dict(shared)
        if c in ACTIVE:
            m["xT"] = np.ascontiguousarray(xf[ACTIVE.index(c)].T)
        else:
            m["xT"] = zx
        in_maps.append(m)
    return in_maps


def kernel(x, norm_gain, ffn_w_gate_up, ffn_w_down, moba_w_qkv, moba_w_o,
           ret_w_in, ret_w_o, ret_gn_gain, final_norm):
    in_maps = _prep_inputs(x, norm_gain, ffn_w_gate_up, ffn_w_down, moba_w_qkv, moba_w_o,
                           ret_w_in, ret_w_o, ret_gn_gain, final_norm)
    nc = build_program()
    res = run_bass_kernel_spmd(nc, in_maps, core_ids=list(range(8)))
    out = np.stack([np.ascontiguousarray(res.results[ACTIVE[b]]["outT"].T) for b in range(4)], axis=0)
    return out.astype(np.float32)
```

```python
import os
from contextlib import ExitStack

import numpy as np
import concourse.bass as bass
import concourse.mybir as mybir
from concourse.bass_utils import run_bass_kernel_spmd

F32 = mybir.dt.float32
BF16 = mybir.dt.bfloat16
AF = mybir.ActivationFunctionType
ALU = mybir.AluOpType
AX = mybir.AxisListType

D = 2048
S = 2048
FF = 5632
T = 1024
NB = 6
EPS = 1e-6
NEG = -30000.0
EPOCH = 6000
ACTIVE = [0, 1, 4, 5]


class Op:
    __slots__ = ("eng", "fn", "deps", "signal", "sig", "chan")


class Sched:
    ENGS = ("pe", "act", "dve", "pool", "sp")

    def __init__(self):
        self.q = {e: [] for e in self.ENGS}
        self.lw = {}
        self.rd = {}
        self.bar = []
        self.chan_last = {}

    def add(self, eng, fn, r=(), w=(), chan=None, nobar=False):
        o = Op()
        o.eng = eng
        o.fn = fn
        o.signal = False
        o.sig = None
        o.chan = chan
        deps = set()
        if not nobar:
            deps.update(self.bar)
        for k in r:
            x = self.lw.get(k)
            if x is not None:
                deps.add(x)
        for k in w:
            x = self.lw.get(k)
            if x is not None:
                deps.add(x)
            rr = self.rd.get(k)
            if rr:
                deps.update(rr.values())
        for k in w:
            self.lw[k] = o
            self.rd[k] = {}
        for k in r:
            rr = self.rd.setdefault(k, {})
            rr[(eng if chan is None else ("dma", id(o)))] = o
        deps.discard(o)
        o.deps = deps
        for d in deps:
            d.signal = True
        self.q[eng].append(o)
        if chan is not None:
            self.chan_last[chan] = o
        return o

    def barrier(self, keep_prefix=("W",)):
        bar = []
        for e in self.ENGS:
            for o in reversed(self.q[e]):
                if o.chan is None:
                    bar.append(o)
                    o.signal = True
                    break
        for c, o in self.chan_last.items():
            bar.append(o)
        self.bar = bar
        self.lw = {k: v for k, v in self.lw.items() if k[0] in keep_prefix}
        self.rd = {k: v for k, v in self.rd.items() if k[0] in keep_prefix}

    def finalize(self):
        sem_names = []
        for e in self.ENGS:
            cnt = 0
            chan_cnt = {}
            for o in self.q[e]:
                if o.chan is not None:
                    pass
                elif o.signal:
                    cnt += 1
                    ep = (cnt - 1) // EPOCH
                    name = "s_%s_%d" % (e, ep)
                    if name not in sem_names:
                        sem_names.append(name)
                    o.sig = (name, cnt - ep * EPOCH)
        chan_cnt = {}
        for e in self.ENGS:
            for o in self.q[e]:
                if o.chan is not None:
                    n = chan_cnt.get(o.chan, 0) + 1
                    chan_cnt[o.chan] = n
                    name = "c_" + o.chan
                    if name not in sem_names:
                        sem_names.append(name)
                    o.sig = (name, 16 * n)
        return sem_names

    def emit(self, eng_name, e, sems):
        waited = {}
        for o in self.q[eng_name]:
            ws = {}
            for d in o.deps:
                if eng_name == "pe" and d.eng == "pe" and d.chan is None:
                    continue
                s, v = d.sig
                if waited.get(s, 0) >= v:
                    continue
                if ws.get(s, 0) < v:
                    ws[s] = v
            for s in sorted(ws):
                e.wait_ge(sems[s], ws[s])
                waited[s] = ws[s]
            if o.fn is None:
                continue
            ins = o.fn(e)
            if o.chan is not None:
                ins.then_inc(sems[o.sig[0]], 16)
            elif o.signal:
                ins.then_inc(sems[o.sig[0]], 1)


def _consts():
    c = {}
    pos = np.arange(S, dtype=np.float32)
    half = 16
    inv = np.power(np.float32(500000.0), -np.arange(half, dtype=np.float32) / np.float32(half)).astype(np.float32)
    ang = (pos[:, None] * inv[None, :]).astype(np.float32)
    cosv = np.cos(ang).astype(np.float32).T
    sinv = np.sin(ang).astype(np.float32).T
    cosM = np.ones((128, S), np.float32)
    sinM = np.zeros((128, S), np.float32)
    cosM[0:16] = cosv
    cosM[16:32] = cosv
    sinM[0:16] = -sinv
    sinM[16:32] = sinv
    c["cosM"] = cosM
    c["sinM"] = sinM
    perm = np.zeros((128, 128), np.float32)
    for d in range(16):
        perm[d + 16, d] = 1.0
        perm[d, d + 16] = 1.0
    c["permF"] = perm
    c["onesF"] = np.concatenate([np.full((128, 128), 1.0 / 2048.0, np.float32), np.full((128, 128), 1.0 / 256.0, np.float32)], axis=1)
    invr = np.power(np.float32(10000.0), -np.linspace(0.0, 1.0, 128, dtype=np.float32)).astype(np.float32)
    angr = (pos[:, None] * invr[None, :]).astype(np.float32)
    c["cosR"] = np.cos(angr).astype(np.float32).T.copy()
    c["sinR"] = np.sin(angr).astype(np.float32).T.copy()
    pastb = np.zeros((128, 16, 8), np.float32)
    for i in range(16):
        for n in range(8):
            if not (n < i // 2):
                pastb[:, i, n] = -1e30
    c["pastb"] = pastb.reshape(128, 128)
    import ml_dtypes
    cb = np.zeros((128, 2048), np.float32)
    cb[:, 0:128] = np.eye(128)
    cb[:, 128:256] = 1.0
    for n in range(8):
        cb[n, 256 + n * 128:256 + (n + 1) * 128] = 1.0
    kk = np.arange(128)[:, None]
    qq = np.arange(256)[None, :]
    for kt in range(2):
        cb[:, 1280 + kt * 256:1280 + (kt + 1) * 256] = np.where(kt * 128 + kk <= qq, 0.0, NEG)
    cb[:, 1792:1920] = 1.0 / 2048.0
    cb[:, 1920:2048] = 1.0 / 256.0
    c["cb"] = cb.astype(ml_dtypes.bfloat16)
    dt = np.zeros((8, 128, 5, 512), np.float64)
    p = np.arange(128)[:, None].astype(np.float64)
    nl = np.arange(512)[None, :].astype(np.float64)
    gam = []
    for h in range(8):
        g = 1.0 - 2.0 ** (-5.0 - h)
        gam.append(g)
        lg = np.log(g)
        for r in range(4):
            dd = nl - (r * 128 + p)
            dt[h, :, r, :] = np.where(dd >= 0, np.exp(dd * lg), 0.0)
        dt[h, :, 4, :] = np.exp((nl - p + 127.0) * lg)
    c["dtab"] = dt.astype(np.float32).reshape(8 * 128, 5 * 512)
    c["gam"] = gam
    return c


_C = None


def _get_consts():
    global _C
    if _C is None:
        _C = _consts()
    return _C


def build_program(stop_after=99, debug_out=None):
    C = _get_consts()
    gam = C["gam"]
    nc = bass.Bass("TRN2", target_bir_lowering=False)
    sc = Sched()

    def din(name, shape, dt=F32):
        return nc.dram_tensor(name, list(shape), dt, kind="ExternalInput").ap()

    def dscr(name, shape, dt):
        kind = "ExternalOutput" if (debug_out == name or debug_out == "all") else "Internal"
        return nc.dram_tensor(name, list(shape), dt, kind=kind).ap()

    xT_in = din("xT", [D, S])
    wgu = din("wgu", [4 * D, 2 * FF])
    wdn = din("wdn", [4 * FF, D])
    wqkv = din("wqkv", [D, 3 * D])
    wo_m = din("wo_m", [D, D])
    win = din("win", [D, 4 * D])
    wo_r = din("wo_r", [D, D])
    vecs_in = din("vecs", [128, 128])
    cosM_in = din("cosM", [128, S])
    sinM_in = din("sinM", [128, S])
    cosR_in = din("cosR", [128, S])
    sinR_in = din("sinR", [128, S])
    permF_in = din("permF", [128, 128])
    onesF_in = din("onesF", [128, 256])
    pastb_in = din("pastb", [128, 128])
    cb_in = din("cb", [128, 2048], BF16)
    dtab_in = din("dtab", [8 * 128, 5 * 512])
    if debug_out in ("outT", "all", None):
        outT = nc.dram_tensor("outT", [D, S], F32, kind="ExternalOutput").ap()
    else:
        outT = nc.dram_tensor("outT", [D, S], F32, kind="Internal").ap()
    xs0 = dscr("xs0", [D, S], F32)
    qs0 = dscr("qs0", [D, S], BF16)
    ks0 = dscr("ks0", [D, S], BF16)
    vs0 = dscr("vs0", [S, D], BF16)
    os0 = dscr("os0", [D, S], BF16)
    xs1 = dscr("xs1", [D, S], F32)
    qs1 = dscr("qs1", [D, S], BF16)
    ks1 = dscr("ks1", [D, S], BF16)
    vs1 = dscr("vs1", [S, D], BF16)
    os1 = dscr("os1", [D, S], BF16)
    gs = dscr("gs", [D, S], BF16)

    es = ExitStack()
    with es:
        def sb(name, shape, dt):
            return es.enter_context(nc.sbuf_tensor(name, list(shape), dt))

        XF = sb("XF", [128, 16, T], F32)
        HB = sb("HB", [128, 16, T], BF16)
        WR = sb("WR", [128, NB, 4096], BF16)
        ACTS = sb("ACTS", [128, 2, 2, T], BF16)
        FT = sb("FT", [128, 8, 512], F32)
        BT = sb("BT", [128, 4, T], BF16)
        RSTD = sb("RSTD", [128, T], F32)
        TAB = sb("TAB", [128, 2, T], F32)
        VECS = sb("VECS", [128, 128], F32)
        PERMF = sb("PERMF", [128, 128], F32)
        PASTB = sb("PASTB", [128, 128], F32)
        CB = sb("CB", [128, 2048], BF16)
        KMS = sb("KMS", [128, 16, 8], F32)
        SM = sb("SM", [128, 1024], F32)
        SMB = sb("SMB", [128, 4608], BF16)
        PS = [es.enter_context(nc.psum_tensor("ps%d" % i, [128, 512], F32)) for i in range(7)]
        PST = es.enter_context(nc.psum_tensor("pst", [128, 1024], BF16))

        PS8 = [p[:] for p in PS] + [PST[:].bitcast(F32)]
        IDB = CB[:, 0:128]
        ONESB = CB[:, 128:256]
        SELALL = CB[0:8, 256:1280]
        CAUS = CB[:, 1280:1792]
        AVG_D = CB[:, 1792:1920]
        AVG_G = CB[:, 1920:2048]

        ctr = {"w": 0, "ft": 0, "bt": 0}

        def dma(eng, out, in_, r, w, chan, nobar=False):
            return sc.add(eng, lambda e: e.dma_start(out=out, in_=in_), r=r, w=w, chan=chan, nobar=nobar)

        def wpiece_col(W2d, c0):
            p = ctr["w"]
            ctr["w"] += 1
            slot = p % NB
            dst = WR[:, slot, :].rearrange("p (k c) -> p k c", c=256)
            src = W2d[:, c0:c0 + 256].rearrange("(k p) c -> p k c", p=128)
            dma("pool", dst, src, (), [("W", slot)], "w%d" % slot, nobar=True)
            return slot, dst

        def wpiece_row(W2d, r0):
            p = ctr["w"]
            ctr["w"] += 1
            slot = p % NB
            dst = WR[:, slot, :].rearrange("p (k c) -> p k c", c=2048)
            src = W2d[r0:r0 + 256, :].rearrange("(k p) c -> p k c", p=128)
            dma("pool", dst, src, (), [("W", slot)], "w%d" % slot, nobar=True)
            return slot, dst

        def ft_new():
            i = ctr["ft"] % 8
            ctr["ft"] += 1
            return FT[:, i, :], ("FT", i)

        def bt_new():
            i = ctr["bt"] % 4
            ctr["bt"] += 1
            return BT[:, i, :], ("BT", i), i

        def mm(out, lhsT, rhs, start, stop, r, w):
            sc.add("pe", lambda e: e.matmul(out, lhsT, rhs, start=start, stop=stop), r=r, w=w)

        def act(out, in_, func, r, w, scale=1.0):
            sc.add("act", lambda e: e.activation(out=out, in_=in_, func=func, scale=scale), r=r, w=w)

        def tt(out, in0, in1, op, r, w, eng="dve"):
            sc.add(eng, lambda e: e.tensor_tensor(out=out, in0=in0, in1=in1, op=op), r=r, w=w)

        def ts(out, in0, s1, s2, op0, op1, r, w, eng="dve"):
            if op1 is None:
                sc.add(eng, lambda e: e.tensor_scalar(out=out, in0=in0, scalar1=s1, scalar2=None, op0=op0), r=r, w=w)
            else:
                sc.add(eng, lambda e: e.tensor_scalar(out=out, in0=in0, scalar1=s1, scalar2=s2, op0=op0, op1=op1), r=r, w=w)

        def stt(out, in0, scalar, in1, op0, op1, r, w, eng="dve"):
            sc.add(eng, lambda e: e.scalar_tensor_tensor(out=out, in0=in0, scalar=scalar, in1=in1, op0=op0, op1=op1), r=r, w=w)

        def nsl(n):
            return slice(n * 512, (n + 1) * 512)

        dma("sp", VECS[:], vecs_in, (), [("VECS",)], "c0")
        sc.add("dve", lambda e: e.memset(KMS[:], 0.0), w=[("KMSINIT",)])
        dma("sp", PERMF[:], permF_in, (), [("PERMF",)], "c2")
        dma("sp", PASTB[:], pastb_in, (), [("PASTB",)], "c3")
        dma("sp", CB[:], cb_in, (), [("CB",)], "c4")
        sc.barrier()

        def load_x(hf, src):
            t0 = hf * T
            for g4 in range(4):
                dma("sp", XF[:, g4 * 4:(g4 + 1) * 4, :],
                    src[g4 * 512:(g4 + 1) * 512, t0:t0 + T].rearrange("(k p) t -> p k t", p=128),
                    (), [("X", k, n) for k in range(g4 * 4, g4 * 4 + 4) for n in range(2)], "x%d" % g4)

        def store_x(hf, dst):
            t0 = hf * T
            for g4 in range(4):
                dma("sp", dst[g4 * 512:(g4 + 1) * 512, t0:t0 + T].rearrange("(k p) t -> p k t", p=128),
                    XF[:, g4 * 4:(g4 + 1) * 4, :],
                    [("X", k, n) for k in range(g4 * 4, g4 * 4 + 4) for n in range(2)], (), "x%d" % g4)

        def load_h(hf, src):
            t0 = hf * T
            for g4 in range(4):
                dma("sp", HB[:, g4 * 4:(g4 + 1) * 4, :],
                    src[g4 * 512:(g4 + 1) * 512, t0:t0 + T].rearrange("(k p) t -> p k t", p=128),
                    (), [("H", k, n) for k in range(g4 * 4, g4 * 4 + 4) for n in range(2)], "h%d" % g4)

        def rms_to(gcol, out_fn):
            for n in range(2):
                for k in range(16):
                    sqf, sqk = ft_new()
                    sq = sqf.bitcast(BF16)[:, 0:512]
                    act(sq, XF[:, k, nsl(n)], AF.Square, [("X", k, n)], [sqk])
                    mm(PS[6][:], AVG_D, sq, k == 0, k == 15, [sqk, ("CB",)], [("ps", 6)])
                sc.add("act", lambda e, o=RSTD[:, nsl(n)]: e.activation(out=o, in_=PS[6][:], func=AF.Ln, bias=EPS, scale=1.0),
                       r=[("ps", 6)], w=[("RSTD", n)])
                sc.add("act", lambda e, o=RSTD[:, nsl(n)]: e.activation(out=o, in_=o, func=AF.Exp, scale=-0.5),
                       r=[("RSTD", n)], w=[("RSTD", n)])
                for k in range(16):
                    out_fn(k, n)

        def norm_h(vcol0):
            def o(k, n):
                stt(HB[:, k, nsl(n)], XF[:, k, nsl(n)], VECS[:, vcol0 + k:vcol0 + k + 1], RSTD[:, nsl(n)],
                    ALU.mult, ALU.mult, [("X", k, n), ("RSTD", n), ("VECS",)], [("H", k, n)])
            rms_to(vcol0, o)

        def ffn(lj):
            Wgu = wgu[lj * D:(lj + 1) * D, :]
            Wdn = wdn[lj * FF:(lj + 1) * FF, :]
            NS = FF // 256
            tile_ctr = [0]
            dctr = [0]

            def gu_tile(s, q, WG, WU, sg_slot, su_slot):
                aslot = s % 2
                c, n = divmod(q, 2)
                ti = tile_ctr[0]
                tile_ctr[0] += 1
                pg = PS[ti % 2]
                pu = PS[2 + ti % 2]
                for k in range(16):
                    mm(pg[:], WG[:, k, c * 128:(c + 1) * 128], HB[:, k, nsl(n)], k == 0, k == 15,
                       [("W", sg_slot), ("H", k, n)], [("ps", ti % 2)])
                for k in range(16):
                    mm(pu[:], WU[:, k, c * 128:(c + 1) * 128], HB[:, k, nsl(n)], k == 0, k == 15,
                       [("W", su_slot), ("H", k, n)], [("ps", 2 + ti % 2)])
                sg, sgk = ft_new()
                act(sg, pg[:], AF.Silu, [("ps", ti % 2)], [sgk])
                tt(ACTS[:, aslot, c, nsl(n)], sg, pu[:], ALU.mult, [sgk, ("ps", 2 + ti % 2)],
                   [("A", aslot, c, n)])

            def down_tiles(s, WD, sd_slot, lo, hi):
                aslot = s % 2
                for idx in range(lo, hi):
                    m, n = divmod(idx, 2)
                    di = dctr[0]
                    dctr[0] += 1
                    bk = 4 + di % 4
                    pd = PS8[bk]
                    for c in range(2):
                        mm(pd, WD[:, c, m * 128:(m + 1) * 128], ACTS[:, aslot, c, nsl(n)], c == 0, c == 1,
                           [("W", sd_slot), ("A", aslot, c, n)], [("ps", bk) if bk < 7 else ("pst",)])
                    stt(XF[:, m, nsl(n)], pd, 0.5, XF[:, m, nsl(n)], ALU.mult, ALU.add,
                        [("ps", bk) if bk < 7 else ("pst",), ("X", m, n)], [("X", m, n)])

            prev = None
            for s in range(NS):
                sg_slot, WG = wpiece_col(Wgu, s * 256)
                su_slot, WU = wpiece_col(Wgu, FF + s * 256)
                for q in range(4):
                    gu_tile(s, q, WG, WU, sg_slot, su_slot)
                    if prev is not None:
                        down_tiles(s - 1, prev[1], prev[0], q * 8, (q + 1) * 8)
                prev = wpiece_row(Wdn, s * 256)
            down_tiles(NS - 1, prev[1], prev[0], 0, 32)

        def out_proj(W2d, scale=1.0):
            dctr = [0]
            for i in range(8):
                slot, WP = wpiece_col(W2d, i * 256)
                for c in range(2):
                    m = 2 * i + c
                    for n in range(2):
                        di = dctr[0]
                        dctr[0] += 1
                        pd = PS[4 + di % 2]
                        for k in range(16):
                            mm(pd[:], WP[:, k, c * 128:(c + 1) * 128], HB[:, k, nsl(n)], k == 0, k == 15,
                               [("W", slot), ("H", k, n)], [("ps", 4 + di % 2)])
                        tt(XF[:, m, nsl(n)], pd[:], XF[:, m, nsl(n)], ALU.add,
                           [("ps", 4 + di % 2), ("X", m, n)], [("X", m, n)])

        def load_tab(hf, cos_in, sin_in):
            t0 = hf * T
            dma("sp", TAB[:, 0, :], cos_in[:, t0:t0 + T], (), [("TAB", 0)], "tab0")
            dma("sp", TAB[:, 1, :], sin_in[:, t0:t0 + T], (), [("TAB", 1)], "tab1")

        def wpiece_col2(W2d, c0):
            if ctr["w"] % 2 == 1:
                ctr["w"] += 1
            p = ctr["w"]
            ctr["w"] += 2
            slot = p % NB
            dst = WR[:, slot:slot + 2, :].rearrange("p s x -> p (s x)").rearrange("p (k c) -> p k c", c=512)
            src = W2d[:, c0:c0 + 512].rearrange("(k p) c -> p k c", p=128)
            dma("pool", dst, src, (), [("W", slot), ("W", slot + 1)], "w%d" % slot, nobar=True)
            return slot, dst

        def v_proj(hf, W2d, col0, vs):
            t0 = hf * T
            tctr = [0]
            for i in range(4):
                slot, WP = wpiece_col2(W2d, col0 + i * 512)
                for g in range(4):
                    vst, vk, vi = bt_new()
                    vst3 = vst.rearrange("p (t c) -> p t c", c=512)
                    for tl in range(2):
                        ttile = g * 2 + tl
                        ti = tctr[0]
                        tctr[0] += 1
                        pv = PS[ti % 2]
                        for k in range(16):
                            mm(pv[:], HB[:, k, ttile * 128:(ttile + 1) * 128], WP[:, k, :], k == 0, k == 15,
                               [("W", slot), ("W", slot + 1), ("H", k, ttile // 4)], [("ps", ti % 2)])
                        act(vst3[:, tl, :], pv[:], AF.Copy, [("ps", ti % 2)], [vk])
                    dma("sp", vs[t0 + g * 256:t0 + (g + 1) * 256, i * 512:(i + 1) * 512].rearrange("(t p) c -> p t c", p=128),
                        vst3, [vk], [("vs", hf, i, g)], "bt%d" % vi)

        def moba_qk(hf, col0, dst, is_k):
            t0 = hf * T
            tctr = [0]
            for i in range(8):
                slot, WP = wpiece_col(wqkv, col0 + i * 256)
                for c in range(2):
                    head = 2 * i + c
                    st, stk, sti = bt_new()
                    for n in range(2):
                        ti = tctr[0]
                        tctr[0] += 1
                        pa = PS[ti % 2]
                        pb = PS[2 + ti % 2]
                        for k in range(16):
                            mm(pa[:], WP[:, k, c * 128:(c + 1) * 128], HB[:, k, nsl(n)], k == 0, k == 15,
                               [("W", slot), ("H", k, n)], [("ps", ti % 2)])
                        kf, kfk = ft_new()
                        act(kf, pa[:], AF.Copy, [("ps", ti % 2)], [kfk])
                        mm(pb[:], PERMF[:], kf, True, True, [kfk, ("PERMF",)], [("ps", 2 + ti % 2)])
                        t1, t1k = ft_new()
                        tt(t1, kf, TAB[:, 0, nsl(n)], ALU.mult, [kfk, ("TAB", 0)], [t1k])
                        t2, t2k = ft_new()
                        tt(t2, pb[:], TAB[:, 1, nsl(n)], ALU.mult, [("ps", 2 + ti % 2), ("TAB", 1)], [t2k])
                        tt(t1, t1, t2, ALU.add, [t1k, t2k], [t1k])
                        if is_k:
                            for b in range(2):
                                bi = hf * 4 + n * 2 + b
                                sc.add("act", lambda e, o=st[:, n * 512 + b * 256:n * 512 + (b + 1) * 256],
                                       i_=t1[:, b * 256:(b + 1) * 256], a_=KMS[:, head, bi:bi + 1]:
                                       e.activation(out=o, in_=i_, func=AF.Copy, accum_out=a_),
                                       r=[t1k], w=[stk, ("KMS", head, hf, n)])
                        else:
                            act(st[:, nsl(n)], t1, AF.Copy, [t1k], [stk])
                    dma("sp", dst[head * 128:(head + 1) * 128, t0:t0 + T], st, [stk], [("qk", is_k, head, hf)],
                        "bt%d" % sti)

        for hf in range(2):
            load_x(hf, xT_in)
            load_tab(hf, cosM_in, sinM_in)
            norm_h(0 * 16)
            ffn(0)
            norm_h(1 * 16)
            moba_qk(hf, D, ks0, True)
            moba_qk(hf, 0, qs0, False)
            v_proj(hf, wqkv, 2 * D, vs0)
            store_x(hf, xs0)
            sc.barrier()

        SCALE = 128.0 ** -0.5
        HBf = HB[:].rearrange("p k t -> p (k t)")

        def s2_bufs(hd):
            sl = hd % 2
            QT = HBf[:, sl * 2048:(sl + 1) * 2048]
            KT = HBf[:, 4096 + sl * 2048:4096 + (sl + 1) * 2048]
            VV = HBf[:, 8192 + sl * 2048:8192 + (sl + 1) * 2048].rearrange("p (i d) -> p i d", d=128)
            OST = HBf[:, 12288 + sl * 2048:12288 + (sl + 1) * 2048]
            pb = sl * 2304
            KMB = SMB[:, pb:pb + 8]
            BIASQ = SMB[:, pb + 128:pb + 256]
            BIAST = SMB[0:8, pb + 256:pb + 2304]
            return sl, QT, KT, VV, OST, KMB, BIASQ, BIAST

        def s2_p1(hd):
            sl, QT, KT, VV, OST, KMB, BIASQ, BIAST = s2_bufs(hd)
            dma("sp", QT, qs0[hd * 128:(hd + 1) * 128, :], (), [("QT", sl)], "qt%d" % sl)
            dma("sp", KT, ks0[hd * 128:(hd + 1) * 128, :], (), [("KT", sl)], "kt%d" % sl)
            dma("sp", VV, vs0[:, hd * 128:(hd + 1) * 128].rearrange("(i p) d -> p i d", p=128), (), [("VV", sl)],
                "vv%d" % sl)
            act(KMB, KMS[:, hd, :], AF.Copy, [("KMS", hd, 0, 0), ("KMS", hd, 0, 1), ("KMS", hd, 1, 0), ("KMS", hd, 1, 1)],
                [("KMB", sl)])

        def s2_p2(hd):
            sl, QT, KT, VV, OST, KMB, BIASQ, BIAST = s2_bufs(hd)
            for i in range(16):
                mm(PS[0][:, i * 8:(i + 1) * 8], QT[:, i * 128:(i + 1) * 128], KMB, True, True,
                   [("QT", sl), ("KMB", sl)], [("ps", 0)])
            sb_ = sl * 512
            GM = SM[:, sb_:sb_ + 128]
            TOP8 = SM[:, sb_ + 128:sb_ + 256]
            THR = SM[:, sb_ + 256:sb_ + 272]
            GE = SM[:, sb_ + 384:sb_ + 512]
            tt(GM, PS[0][:, 0:128], PASTB[:], ALU.add, [("ps", 0), ("PASTB",)], [("GM", sl)])
            for i in range(16):
                sc.add("dve", lambda e, o=TOP8[:, i * 8:(i + 1) * 8], i_=GM[:, i * 8:(i + 1) * 8]: e.max(out=o, in_=i_),
                       r=[("GM", sl)], w=[("TOP8", sl, i)])
            ts(THR, TOP8.rearrange("p (i e) -> p i e", e=8)[:, :, 2], -1e29, None, ALU.max, None,
               [("TOP8", sl, i) for i in range(16)], [("THR", sl)])
            tt(GE.rearrange("p (i e) -> p i e", e=8), GM.rearrange("p (i e) -> p i e", e=8),
               THR.unsqueeze(2).to_broadcast([128, 16, 8]), ALU.is_ge, [("GM", sl), ("THR", sl)], [("GE", sl)])
            ts(BIASQ, GE, -1.0, -NEG, ALU.add, ALU.mult, [("GE", sl)], [("BIASQ", sl)])

        def s2_p3(hd):
            sl, QT, KT, VV, OST, KMB, BIASQ, BIAST = s2_bufs(hd)
            for hh in range(2):
                for i8 in range(8):
                    i = hh * 8 + i8
                    sc.add("pe", lambda e, o=PST[0:8, i8 * 128:(i8 + 1) * 128], i_=BIASQ[:, i * 8:(i + 1) * 8]:
                           e.transpose(out=o, in_=i_, identity=IDB), r=[("BIASQ", sl), ("CB",)], w=[("pst",)])
                act(BIAST[:, hh * 1024:(hh + 1) * 1024], PST[0:8, :], AF.Copy, [("pst",)], [("BIAST", sl, hh)])

        s2ctr = {"t": 0, "e": 0, "c": 0}

        def s2_main(hd):
            sl, QT, KT, VV, OST, KMB, BIASQ, BIAST = s2_bufs(hd)
            pend = []
            tails = []
            if hd + 1 < 16:
                s2_p1(hd + 1)

            for j in range(4):
                if hd + 1 < 16 and j == 2:
                    s2_p2(hd + 1)
                if hd + 1 < 16 and j == 3:
                    s2_p3(hd + 1)
                qlo = j * 512
                tiles = [(nb, kt, 0, 512) for nb in range(2 * j + 1) for kt in range(2)]
                tiles += [(2 * j + 1, kt, 256, 512) for kt in range(2)]
                cpar = s2ctr["c"] % 2
                s2ctr["c"] += 1
                bo, bd = (3, 4) if cpar == 0 else (5, 6)
                po, pdn = PS[bo], PS[bd]
                nt = len(tiles)
                for idx, (nb, kt, c0, c1) in enumerate(tiles):
                    sj = s2ctr["t"] % 3
                    s2ctr["t"] += 1
                    ei = s2ctr["e"] % 4
                    s2ctr["e"] += 1
                    pS = PS[sj]
                    kpos = nb * 256 + kt * 128
                    mm(pS[:, c0:c1], KT[:, kpos:kpos + 128], QT[:, qlo + c0:qlo + c1], True, False,
                       [("KT", sl), ("QT", sl)], [("ps", sj)])
                    if nb < 2 * j:
                        mm(pS[:, 0:512], SELALL[:, nb * 128:(nb + 1) * 128], BIAST[:, qlo:qlo + 512], False, True,
                           [("CB",), ("BIAST", sl, j // 2)], [("ps", sj)])
                    elif nb == 2 * j:
                        mm(pS[:, 0:256], IDB, CAUS[:, kt * 256:(kt + 1) * 256], False, False, [("CB",)], [("ps", sj)])
                        mm(pS[:, 256:512], SELALL[:, nb * 128:(nb + 1) * 128], BIAST[:, qlo + 256:qlo + 512], False, True,
                           [("CB",), ("BIAST", sl, j // 2)], [("ps", sj)])
                    else:
                        mm(pS[:, 256:512], IDB, CAUS[:, kt * 256:(kt + 1) * 256], False, True, [("CB",)], [("ps", sj)])
                    E = BT[:, ei, 0:512]
                    act(E[:, c0:c1], pS[:, c0:c1], AF.Exp, [("ps", sj)], [("BT", ei)], scale=SCALE)
                    def C(nb=nb, kt=kt, c0=c0, c1=c1, E=E, ei=ei, idx=idx, po=po, pdn=pdn, bo=bo, bd=bd, nt=nt,
                          qlo=qlo):
                        first = idx == 0
                        last = idx == nt - 1
                        mm(po[:, c0:c1], VV[:, nb * 2 + kt, :], E[:, c0:c1], first, last, [("VV", sl), ("BT", ei)],
                           [("ps", bo)])
                        mm(pdn[:, c0:c1], ONESB, E[:, c0:c1], first, last, [("CB",), ("BT", ei)], [("ps", bd)])
                        if last:
                            def tail(po=po, pdn=pdn, bo=bo, bd=bd, qlo=qlo):
                                rd, rdk = ft_new()
                                sc.add("act", lambda e, o=rd, i_=pdn[:]: e.activation(out=o, in_=i_, func=AF.Ln), r=[("ps", bd)], w=[rdk])
                                sc.add("act", lambda e, o=rd: e.activation(out=o, in_=o, func=AF.Exp, scale=-1.0), r=[rdk], w=[rdk])
                                tt(OST[:, qlo:qlo + 512], po[:], rd, ALU.mult, [("ps", bo), rdk], [("OST", sl)])
                            tails.append([3, tail])
                    pend.append(C)
                    if len(pend) > 1:
                        pend.pop(0)()
                    for tl_ in tails:
                        tl_[0] -= 1
                    while tails and tails[0][0] <= 0:
                        tails.pop(0)[1]()
            while pend:
                pend.pop(0)()
            while tails:
                tails.pop(0)[1]()
            dma("sp", os0[hd * 128:(hd + 1) * 128, :], OST, [("OST", sl)], [("os", hd)], "ost%d" % sl)

        if stop_after >= 2:
            s2_p1(0)
            s2_p2(0)
            s2_p3(0)
            for hd in range(16):
                s2_main(hd)
            sc.barrier()

        def ret_qk(hf, col0, dst, kscale):
            t0 = hf * T
            tctr = [0]
            for h in range(8):
                slot, WP = wpiece_col(win, col0 + h * 256)
                sa, sak, sai = bt_new()
                sbb, sbk, sbi = bt_new()
                for n in range(2):
                    ti = tctr[0]
                    tctr[0] += 1
                    pa = PS[ti % 2]
                    pb = PS[2 + ti % 2]
                    for k in range(16):
                        mm(pa[:], WP[:, k, 0:128], HB[:, k, nsl(n)], k == 0, k == 15,
                           [("W", slot), ("H", k, n)], [("ps", ti % 2)])
                    for k in range(16):
                        mm(pb[:], WP[:, k, 128:256], HB[:, k, nsl(n)], k == 0, k == 15,
                           [("W", slot), ("H", k, n)], [("ps", 2 + ti % 2)])
                    af, afk = ft_new()
                    bf, bfk = ft_new()
                    act(af, pa[:], AF.Copy, [("ps", ti % 2)], [afk], scale=kscale)
                    act(bf, pb[:], AF.Copy, [("ps", 2 + ti % 2)], [bfk], scale=kscale)
                    cosT = TAB[:, 0, nsl(n)]
                    sinT = TAB[:, 1, nsl(n)]
                    t1, t1k = ft_new()
                    t2, t2k = ft_new()
                    tt(t1, af, cosT, ALU.mult, [afk, ("TAB", 0)], [t1k])
                    tt(t2, bf, sinT, ALU.mult, [bfk, ("TAB", 1)], [t2k])
                    tt(sa[:, nsl(n)], t1, t2, ALU.subtract, [t1k, t2k], [sak])
                    t3, t3k = ft_new()
                    t4, t4k = ft_new()
                    tt(t3, af, sinT, ALU.mult, [afk, ("TAB", 1)], [t3k])
                    tt(t4, bf, cosT, ALU.mult, [bfk, ("TAB", 0)], [t4k])
                    tt(sbb[:, nsl(n)], t3, t4, ALU.add, [t3k, t4k], [sbk])
                dma("sp", dst[h * 256:h * 256 + 128, t0:t0 + T], sa, [sak], [("rqk", col0, h, 0, hf)], "bt%d" % sai)
                dma("sp", dst[h * 256 + 128:h * 256 + 256, t0:t0 + T], sbb, [sbk], [("rqk", col0, h, 1, hf)], "bt%d" % sbi)

        def ret_g(hf):
            t0 = hf * T
            tctr = [0]
            for h in range(8):
                slot, WP = wpiece_col(win, 3 * D + h * 256)
                for c in range(2):
                    st, stk, sti = bt_new()
                    for n in range(2):
                        ti = tctr[0]
                        tctr[0] += 1
                        pa = PS[ti % 2]
                        for k in range(16):
                            mm(pa[:], WP[:, k, c * 128:(c + 1) * 128], HB[:, k, nsl(n)], k == 0, k == 15,
                               [("W", slot), ("H", k, n)], [("ps", ti % 2)])
                        act(st[:, nsl(n)], pa[:], AF.Silu, [("ps", ti % 2)], [stk])
                    dma("sp", gs[h * 256 + c * 128:h * 256 + (c + 1) * 128, t0:t0 + T], st, [stk], [("gs", h, c, hf)],
                        "bt%d" % sti)

        if stop_after >= 3:
            for hf in range(2):
                load_x(hf, xs0)
                load_h(hf, os0)
                load_tab(hf, cosR_in, sinR_in)
                out_proj(wo_m)
                norm_h(2 * 16)
                ffn(1)
                norm_h(3 * 16)
                ffn(2)
                norm_h(4 * 16)
                ret_qk(hf, 0, qs1, 1.0)
                ret_qk(hf, D, ks1, 1.0 / 16.0)
                v_proj(hf, win, 2 * D, vs1)
                ret_g(hf)
                store_x(hf, xs1)
                sc.barrier()

        if stop_after >= 4:
            XFf = XF[:].rearrange("p k t -> p (k t)")
            s4ctr = [0]
            XFb = XFf.bitcast(BF16)

            def s4_bufs(h):
                par = h % 2
                B0 = HBf if par == 0 else XFb[:, 16384:32768]
                QT0 = B0[:, 0:2048]
                QT1 = B0[:, 2048:4096]
                KT0 = B0[:, 4096:6144]
                KT1 = B0[:, 6144:8192]
                VV = B0[:, 8192:12288].rearrange("p (i d) -> p i d", d=256)
                GS0 = B0[:, 12288:14336]
                GS1 = B0[:, 14336:16384]
                DT = XFf[:, par * 2560:(par + 1) * 2560].rearrange("p (r c) -> p r c", c=512)
                return par, QT0, QT1, KT0, KT1, VV, GS0, GS1, DT

            def s4_load(h):
                par, QT0, QT1, KT0, KT1, VV, GS0, GS1, DT = s4_bufs(h)
                dma("sp", QT0, qs1[h * 256:h * 256 + 128, :], (), [("RQ", par, 0)], "rq0%d" % par)
                dma("sp", QT1, qs1[h * 256 + 128:h * 256 + 256, :], (), [("RQ", par, 1)], "rq1%d" % par)
                dma("sp", KT0, ks1[h * 256:h * 256 + 128, :], (), [("RK", par, 0)], "rk0%d" % par)
                dma("sp", KT1, ks1[h * 256 + 128:h * 256 + 256, :], (), [("RK", par, 1)], "rk1%d" % par)
                dma("sp", VV, vs1[:, h * 256:(h + 1) * 256].rearrange("(i p) d -> p i d", p=128), (), [("RV", par)],
                    "rv%d" % par)
                dma("sp", GS0, gs[h * 256:h * 256 + 128, :], (), [("RG", par, 0)], "rg0%d" % par)
                dma("sp", GS1, gs[h * 256 + 128:h * 256 + 256, :], (), [("RG", par, 1)], "rg1%d" % par)
                dma("sp", DT, dtab_in[h * 128:(h + 1) * 128, :].rearrange("p (r c) -> p r c", c=512), (),
                    [("DT", par)], "dt%d" % par)

            s4_load(0)
            for h in range(8):
                g_ = gam[h]
                if h + 1 < 8:
                    s4_load(h + 1)
                par, QT0, QT1, KT0, KT1, VV, GS0, GS1, DT = s4_bufs(h)
                YST = [BT[:, 0, :], BT[:, 1, :], BT[:, 2, :], BT[:, 3, :]]
                QTs = [QT0, QT1]
                KTs = [KT0, KT1]
                GSs = [GS0, GS1]
                pend = []
                tails = []
                for j in range(4):
                    jsl = slice(j * 512, (j + 1) * 512)
                    nm = 4 * j + 4
                    ob = (3, 4) if j % 2 == 0 else (5, 6)
                    for i in range(nm):
                        sj = s4ctr[0] % 3
                        s4ctr[0] += 1
                        pS = PS[sj]
                        for dc in range(2):
                            mm(pS[:], KTs[dc][:, i * 128:(i + 1) * 128], QTs[dc][:, jsl], dc == 0, dc == 1,
                               [("RK", par, dc), ("RQ", par, dc)], [("ps", sj)])
                        STt = SMB[:, sj * 512:(sj + 1) * 512]
                        if i >= 4 * j:
                            tt(STt, pS[:], DT[:, i - 4 * j, :], ALU.mult, [("ps", sj), ("DT", par)], [("ST", sj)])
                        else:
                            cst = float(g_ ** (j * 512 - i * 128 - 127))
                            stt(STt, pS[:], cst, DT[:, 4, :], ALU.mult, ALU.mult, [("ps", sj), ("DT", par)],
                                [("ST", sj)])

                        def C(i=i, nm=nm, STt=STt, sj=sj, ob=ob, j=j, jsl=jsl):
                            for vc in range(2):
                                mm(PS[ob[vc]][:], VV[:, i, vc * 128:(vc + 1) * 128], STt, i == 0, i == nm - 1,
                                   [("RV", par), ("ST", sj)], [("ps", ob[vc])])
                            if i != nm - 1:
                                return
                            sqs = []
                            for vc in range(2):
                                sqf, sqk = ft_new()
                                sq = sqf.bitcast(BF16)[:, 0:512]
                                act(sq, PS[ob[vc]][:], AF.Square, [("ps", ob[vc])], [sqk])
                                sqs.append((sq, sqk))

                            def tail(sqs=sqs, ob=ob, j=j, jsl=jsl):
                                nj = s4ctr[0] % 3
                                s4ctr[0] += 1
                                pN = PS[nj]
                                for vc in range(2):
                                    mm(pN[:], AVG_G, sqs[vc][0], vc == 0, vc == 1, [sqs[vc][1], ("CB",)], [("ps", nj)])
                                rs, rsk = ft_new()
                                sc.add("act", lambda e, o=rs, p_=pN: e.activation(out=o, in_=p_[:], func=AF.Ln, bias=EPS, scale=1.0),
                                       r=[("ps", nj)], w=[rsk])
                                sc.add("act", lambda e, o=rs: e.activation(out=o, in_=o, func=AF.Exp, scale=-0.5), r=[rsk], w=[rsk])
                                for vc in range(2):
                                    t1, t1k = ft_new()
                                    col = 112 + h * 2 + vc
                                    stt(t1, PS[ob[vc]][:], VECS[:, col:col + 1], rs, ALU.mult, ALU.mult,
                                        [("ps", ob[vc]), rsk, ("VECS",)], [t1k])
                                    yslot = vc * 2 + j // 2
                                    tt(YST[yslot][:, (j % 2) * 512:(j % 2 + 1) * 512], t1, GSs[vc][:, jsl], ALU.mult,
                                       [t1k, ("RG", par, vc)], [("BT", yslot)])
                            tails.append([3, tail])
                        pend.append(C)
                        if len(pend) > 1:
                            pend.pop(0)()
                        for tl_ in tails:
                            tl_[0] -= 1
                        while tails and tails[0][0] <= 0:
                            tails.pop(0)[1]()
                while pend:
                    pend.pop(0)()
                while tails:
                    tails.pop(0)[1]()
                for vc in range(2):
                    for hh in range(2):
                        yslot = vc * 2 + hh
                        dma("sp", os1[h * 256 + vc * 128:h * 256 + (vc + 1) * 128, hh * T:(hh + 1) * T], YST[yslot],
                            [("BT", yslot)], [("ys", h, vc, hh)], "bt%d" % yslot)
            sc.barrier()

        if stop_after >= 5:
            for hf in range(2):
                t0 = hf * T
                load_x(hf, xs1)
                load_h(hf, os1)
                out_proj(wo_r)
                norm_h(5 * 16)
                ffn(3)

                def o(k, n):
                    ot, otk = ft_new()
                    stt(ot, XF[:, k, nsl(n)], VECS[:, 96 + k:97 + k], RSTD[:, nsl(n)], ALU.mult, ALU.mult,
                        [("X", k, n), ("RSTD", n), ("VECS",)], [otk])
                    dma("sp", outT[k * 128:(k + 1) * 128, t0 + n * 512:t0 + (n + 1) * 512], ot, [otk],
                        [("out", k, n, hf)], "o%d" % (otk[1]))
                rms_to(96, o)
                sc.barrier()

        if debug_out == "xs":
            pass
        sc.barrier()
        sc.add("sp", None)

        sem_names = sc.finalize()
        sems = {}
        for nme in sem_names:
            sems[nme] = es.enter_context(nc.semaphore(nme))
        block = es.enter_context(nc.Block())

        @block.tensor
        def _(e):
            sc.emit("pe", e, sems)

        @block.scalar
        def _(e):
            sc.emit("act", e, sems)

        @block.vector
        def _(e):
            sc.emit("dve", e, sems)

        @block.gpsimd
        def _(e):
            sc.emit("pool", e, sems)

        @block.sync
        def _(e):
            sc.emit("sp", e, sems)

    return nc


def _prep_inputs(x, norm_gain, ffn_w_gate_up, ffn_w_down, moba_w_qkv, moba_w_o,
                 ret_w_in, ret_w_o, ret_gn_gain, final_norm):
    C = _get_consts()
    f = lambda a: np.ascontiguousarray(np.asarray(a, dtype=np.float32))
    vecs = np.zeros((128, 128), np.float32)
    ng = f(norm_gain).reshape(6, 16, 128)
    vecs[:, 0:96] = ng.transpose(2, 0, 1).reshape(128, 96)
    vecs[:, 96:112] = f(final_norm).reshape(16, 128).T
    vecs[:, 112:128] = f(ret_gn_gain).reshape(16, 128).T
    shared = {
        "wgu": f(ffn_w_gate_up).reshape(4 * D, 2 * FF),
        "wdn": f(ffn_w_down).reshape(4 * FF, D),
        "wqkv": f(moba_w_qkv).reshape(D, 3 * D),
        "wo_m": f(moba_w_o).reshape(D, D),
        "win": f(ret_w_in).reshape(D, 4 * D),
        "wo_r": f(ret_w_o).reshape(D, D),
        "vecs": vecs,
        "cosM": C["cosM"], "sinM": C["sinM"], "cosR": C["cosR"], "sinR": C["sinR"],
        "permF": C["permF"], "onesF": C["onesF"], "pastb": C["pastb"], "cb": C["cb"], "dtab": C["dtab"],
    }
    xf = f(x)
    in_maps = []
    zx = np.zeros((D, S), np.float32)
    for c in range(8):
        m = dict(shared)
        if c in ACTIVE:
            m["xT"] = np.ascontiguousarray(xf[ACTIVE.index(c)].T)
        else:
            m["xT"] = zx
        in_maps.append(m)
    return in_maps


def kernel(x, norm_gain, ffn_w_gate_up, ffn_w_down, moba_w_qkv, moba_w_o,
           ret_w_in, ret_w_o, ret_gn_gain, final_norm):
    in_maps = _prep_inputs(x, norm_gain, ffn_w_gate_up, ffn_w_down, moba_w_qkv, moba_w_o,
                           ret_w_in, ret_w_o, ret_gn_gain, final_norm)
    nc = build_program()
    res = run_bass_kernel_spmd(nc, in_maps, core_ids=list(range(8)))
    out = np.stack([np.ascontiguousarray(res.results[ACTIVE[b]]["outT"].T) for b in range(4)], axis=0)
    return out.astype(np.float32)
```

```python
import os
from contextlib import ExitStack

import numpy as np
import concourse.bass as bass
import concourse.mybir as mybir
from concourse.bass_utils import run_bass_kernel_spmd

F32 = mybir.dt.float32
BF16 = mybir.dt.bfloat16
AF = mybir.ActivationFunctionType
ALU = mybir.AluOpType
AX = mybir.AxisListType

D = 2048
S = 2048
FF = 5632
T = 1024
NB = 6
EPS = 1e-6
NEG = -30000.0
EPOCH = 6000
ACTIVE = [0, 1, 4, 5]


class Op:
    __slots__ = ("eng", "fn", "deps", "signal", "sig", "chan")


class Sched:
    ENGS = ("pe", "act", "dve", "pool", "sp")

    def __init__(self):
        self.q = {e: [] for e in self.ENGS}
        self.lw = {}
        self.rd = {}
        self.bar = []
        self.chan_last = {}

    def add(self, eng, fn, r=(), w=(), chan=None, nobar=False):
        o = Op()
        o.eng = eng
        o.fn = fn
        o.signal = False
        o.sig = None
        o.chan = chan
        deps = set()
        if not nobar:
            deps.update(self.bar)
        for k in r:
            x = self.lw.get(k)
            if x is not None:
                deps.add(x)
        for k in w:
            x = self.lw.get(k)
            if x is not None:
                deps.add(x)
            rr = self.rd.get(k)
            if rr:
                deps.update(rr.values())
        for k in w:
            self.lw[k] = o
            self.rd[k] = {}
        for k in r:
            rr = self.rd.setdefault(k, {})
            rr[(eng if chan is None else ("dma", id(o)))] = o
        deps.discard(o)
        o.deps = deps
        for d in deps:
            d.signal = True
        self.q[eng].append(o)
        if chan is not None:
            self.chan_last[chan] = o
        return o

    def barrier(self, keep_prefix=("W",)):
        bar = []
        for e in self.ENGS:
            for o in reversed(self.q[e]):
                if o.chan is None:
                    bar.append(o)
                    o.signal = True
                    break
        for c, o in self.chan_last.items():
            bar.append(o)
        self.bar = bar
        self.lw = {k: v for k, v in self.lw.items() if k[0] in keep_prefix}
        self.rd = {k: v for k, v in self.rd.items() if k[0] in keep_prefix}

    def finalize(self):
        sem_names = []
        for e in self.ENGS:
            cnt = 0
            chan_cnt = {}
            for o in self.q[e]:
                if o.chan is not None:
                    pass
                elif o.signal:
                    cnt += 1
                    ep = (cnt - 1) // EPOCH
                    name = "s_%s_%d" % (e, ep)
                    if name not in sem_names:
                        sem_names.append(name)
                    o.sig = (name, cnt - ep * EPOCH)
        chan_cnt = {}
        for e in self.ENGS:
            for o in self.q[e]:
                if o.chan is not None:
                    n = chan_cnt.get(o.chan, 0) + 1
                    chan_cnt[o.chan] = n
                    name = "c_" + o.chan
                    if name not in sem_names:
                        sem_names.append(name)
                    o.sig = (name, 16 * n)
        return sem_names

    def emit(self, eng_name, e, sems):
        waited = {}
        for o in self.q[eng_name]:
            ws = {}
            for d in o.deps:
                if eng_name == "pe" and d.eng == "pe" and d.chan is None:
                    continue
                s, v = d.sig
                if waited.get(s, 0) >= v:
                    continue
                if ws.get(s, 0) < v:
                    ws[s] = v
            for s in sorted(ws):
                e.wait_ge(sems[s], ws[s])
                waited[s] = ws[s]
            if o.fn is None:
                continue
            ins = o.fn(e)
            if o.chan is not None:
                ins.then_inc(sems[o.sig[0]], 16)
            elif o.signal:
                ins.then_inc(sems[o.sig[0]], 1)


def _consts():
    c = {}
    pos = np.arange(S, dtype=np.float32)
    half = 16
    inv = np.power(np.float32(500000.0), -np.arange(half, dtype=np.float32) / np.float32(half)).astype(np.float32)
    ang = (pos[:, None] * inv[None, :]).astype(np.float32)
    cosv = np.cos(ang).astype(np.float32).T
    sinv = np.sin(ang).astype(np.float32).T
    cosM = np.ones((128, S), np.float32)
    sinM = np.zeros((128, S), np.float32)
    cosM[0:16] = cosv
    cosM[16:32] = cosv
    sinM[0:16] = -sinv
    sinM[16:32] = sinv
    c["cosM"] = cosM
    c["sinM"] = sinM
    perm = np.zeros((128, 128), np.float32)
    for d in range(16):
        perm[d + 16, d] = 1.0
        perm[d, d + 16] = 1.0
    c["permF"] = perm
    c["onesF"] = np.concatenate([np.full((128, 128), 1.0 / 2048.0, np.float32), np.full((128, 128), 1.0 / 256.0, np.float32)], axis=1)
    invr = np.power(np.float32(10000.0), -np.linspace(0.0, 1.0, 128, dtype=np.float32)).astype(np.float32)
    angr = (pos[:, None] * invr[None, :]).astype(np.float32)
    c["cosR"] = np.cos(angr).astype(np.float32).T.copy()
    c["sinR"] = np.sin(angr).astype(np.float32).T.copy()
    pastb = np.zeros((128, 16, 8), np.float32)
    for i in range(16):
        for n in range(8):
            if not (n < i // 2):
                pastb[:, i, n] = -1e30
    c["pastb"] = pastb.reshape(128, 128)
    import ml_dtypes
    cb = np.zeros((128, 2048), np.float32)
    cb[:, 0:128] = np.eye(128)
    cb[:, 128:256] = 1.0
    for n in range(8):
        cb[n, 256 + n * 128:256 + (n + 1) * 128] = 1.0
    kk = np.arange(128)[:, None]
    qq = np.arange(256)[None, :]
    for kt in range(2):
        cb[:, 1280 + kt * 256:1280 + (kt + 1) * 256] = np.where(kt * 128 + kk <= qq, 0.0, NEG)
    cb[:, 1792:1920] = 1.0 / 2048.0
    cb[:, 1920:2048] = 1.0 / 256.0
    c["cb"] = cb.astype(ml_dtypes.bfloat16)
    dt = np.zeros((8, 128, 5, 512), np.float64)
    p = np.arange(128)[:, None].astype(np.float64)
    nl = np.arange(512)[None, :].astype(np.float64)
    gam = []
    for h in range(8):
        g = 1.0 - 2.0 ** (-5.0 - h)
        gam.append(g)
        lg = np.log(g)
        for r in range(4):
            dd = nl - (r * 128 + p)
            dt[h, :, r, :] = np.where(dd >= 0, np.exp(dd * lg), 0.0)
        dt[h, :, 4, :] = np.exp((nl - p + 127.0) * lg)
    c["dtab"] = dt.astype(np.float32).reshape(8 * 128, 5 * 512)
    c["gam"] = gam
    return c


_C = None


def _get_consts():
    global _C
    if _C is None:
        _C = _consts()
    return _C


def build_program(stop_after=99, debug_out=None):
    C = _get_consts()
    gam = C["gam"]
    nc = bass.Bass("TRN2", target_bir_lowering=False)
    sc = Sched()

    def din(name, shape, dt=F32):
        return nc.dram_tensor(name, list(shape), dt, kind="ExternalInput").ap()

    def dscr(name, shape, dt):
        kind = "ExternalOutput" if (debug_out == name or debug_out == "all") else "Internal"
        return nc.dram_tensor(name, list(shape), dt, kind=kind).ap()

    xT_in = din("xT", [D, S])
    wgu = din("wgu", [4 * D, 2 * FF])
    wdn = din("wdn", [4 * FF, D])
    wqkv = din("wqkv", [D, 3 * D])
    wo_m = din("wo_m", [D, D])
    win = din("win", [D, 4 * D])
    wo_r = din("wo_r", [D, D])
    vecs_in = din("vecs", [128, 128])
    cosM_in = din("cosM", [128, S])
    sinM_in = din("sinM", [128, S])
    cosR_in = din("cosR", [128, S])
    sinR_in = din("sinR", [128, S])
    permF_in = din("permF", [128, 128])
    onesF_in = din("onesF", [128, 256])
    pastb_in = din("pastb", [128, 128])
    cb_in = din("cb", [128, 2048], BF16)
    dtab_in = din("dtab", [8 * 128, 5 * 512])
    if debug_out in ("outT", "all", None):
        outT = nc.dram_tensor("outT", [D, S], F32, kind="ExternalOutput").ap()
    else:
        outT = nc.dram_tensor("outT", [D, S], F32, kind="Internal").ap()
    xs0 = dscr("xs0", [D, S], F32)
    qs0 = dscr("qs0", [D, S], BF16)
    ks0 = dscr("ks0", [D, S], BF16)
    vs0 = dscr("vs0", [S, D], BF16)
    os0 = dscr("os0", [D, S], BF16)
    xs1 = dscr("xs1", [D, S], F32)
    qs1 = dscr("qs1", [D, S], BF16)
    ks1 = dscr("ks1", [D, S], BF16)
    vs1 = dscr("vs1", [S, D], BF16)
    os1 = dscr("os1", [D, S], BF16)
    gs = dscr("gs", [D, S], BF16)

    es = ExitStack()
    with es:
        def sb(name, shape, dt):
            return es.enter_context(nc.sbuf_tensor(name, list(shape), dt))

        XF = sb("XF", [128, 16, T], F32)
        HB = sb("HB", [128, 16, T], BF16)
        WR = sb("WR", [128, NB, 4096], BF16)
        ACTS = sb("ACTS", [128, 2, 2, T], BF16)
        FT = sb("FT", [128, 8, 512], F32)
        BT = sb("BT", [128, 4, T], BF16)
        RSTD = sb("RSTD", [128, T], F32)
        TAB = sb("TAB", [128, 2, T], F32)
        VECS = sb("VECS", [128, 128], F32)
        PERMF = sb("PERMF", [128, 128], F32)
        PASTB = sb("PASTB", [128, 128], F32)
        CB = sb("CB", [128, 2048], BF16)
        KMS = sb("KMS", [128, 16, 8], F32)
        SM = sb("SM", [128, 1024], F32)
        SMB = sb("SMB", [128, 4608], BF16)
        PS = [es.enter_context(nc.psum_tensor("ps%d" % i, [128, 512], F32)) for i in range(7)]
        PST = es.enter_context(nc.psum_tensor("pst", [128, 1024], BF16))

        PS8 = [p[:] for p in PS] + [PST[:].bitcast(F32)]
        IDB = CB[:, 0:128]
        ONESB = CB[:, 128:256]
        SELALL = CB[0:8, 256:1280]
        CAUS = CB[:, 1280:1792]
        AVG_D = CB[:, 1792:1920]
        AVG_G = CB[:, 1920:2048]

        ctr = {"w": 0, "ft": 0, "bt": 0}

        def dma(eng, out, in_, r, w, chan, nobar=False):
            return sc.add(eng, lambda e: e.dma_start(out=out, in_=in_), r=r, w=w, chan=chan, nobar=nobar)

        def wpiece_col(W2d, c0):
            p = ctr["w"]
            ctr["w"] += 1
            slot = p % NB
            dst = WR[:, slot, :].rearrange("p (k c) -> p k c", c=256)
            src = W2d[:, c0:c0 + 256].rearrange("(k p) c -> p k c", p=128)
            dma("pool", dst, src, (), [("W", slot)], "w%d" % slot, nobar=True)
            return slot, dst

        def wpiece_row(W2d, r0):
            p = ctr["w"]
            ctr["w"] += 1
            slot = p % NB
            dst = WR[:, slot, :].rearrange("p (k c) -> p k c", c=2048)
            src = W2d[r0:r0 + 256, :].rearrange("(k p) c -> p k c", p=128)
            dma("pool", dst, src, (), [("W", slot)], "w%d" % slot, nobar=True)
            return slot, dst

        def ft_new():
            i = ctr["ft"] % 8
            ctr["ft"] += 1
            return FT[:, i, :], ("FT", i)

        def bt_new():
            i = ctr["bt"] % 4
            ctr["bt"] += 1
            return BT[:, i, :], ("BT", i), i

        def mm(out, lhsT, rhs, start, stop, r, w):
            sc.add("pe", lambda e: e.matmul(out, lhsT, rhs, start=start, stop=stop), r=r, w=w)

        def act(out, in_, func, r, w, scale=1.0):
            sc.add("act", lambda e: e.activation(out=out, in_=in_, func=func, scale=scale), r=r, w=w)

        def tt(out, in0, in1, op, r, w, eng="dve"):
            sc.add(eng, lambda e: e.tensor_tensor(out=out, in0=in0, in1=in1, op=op), r=r, w=w)

        def ts(out, in0, s1, s2, op0, op1, r, w, eng="dve"):
            if op1 is None:
                sc.add(eng, lambda e: e.tensor_scalar(out=out, in0=in0, scalar1=s1, scalar2=None, op0=op0), r=r, w=w)
            else:
                sc.add(eng, lambda e: e.tensor_scalar(out=out, in0=in0, scalar1=s1, scalar2=s2, op0=op0, op1=op1), r=r, w=w)

        def stt(out, in0, scalar, in1, op0, op1, r, w, eng="dve"):
            sc.add(eng, lambda e: e.scalar_tensor_tensor(out=out, in0=in0, scalar=scalar, in1=in1, op0=op0, op1=op1), r=r, w=w)

        def nsl(n):
            return slice(n * 512, (n + 1) * 512)

        dma("sp", VECS[:], vecs_in, (), [("VECS",)], "c0")
        sc.add("dve", lambda e: e.memset(KMS[:], 0.0), w=[("KMSINIT",)])
        dma("sp", PERMF[:], permF_in, (), [("PERMF",)], "c2")
        dma("sp", PASTB[:], pastb_in, (), [("PASTB",)], "c3")
        dma("sp", CB[:], cb_in, (), [("CB",)], "c4")
        sc.barrier()

        def load_x(hf, src):
            t0 = hf * T
            for g4 in range(4):
                dma("sp", XF[:, g4 * 4:(g4 + 1) * 4, :],
                    src[g4 * 512:(g4 + 1) * 512, t0:t0 + T].rearrange("(k p) t -> p k t", p=128),
                    (), [("X", k, n) for k in range(g4 * 4, g4 * 4 + 4) for n in range(2)], "x%d" % g4)

        def store_x(hf, dst):
            t0 = hf * T
            for g4 in range(4):
                dma("sp", dst[g4 * 512:(g4 + 1) * 512, t0:t0 + T].rearrange("(k p) t -> p k t", p=128),
                    XF[:, g4 * 4:(g4 + 1) * 4, :],
                    [("X", k, n) for k in range(g4 * 4, g4 * 4 + 4) for n in range(2)], (), "x%d" % g4)

        def load_h(hf, src):
            t0 = hf * T
            for g4 in range(4):
                dma("sp", HB[:, g4 * 4:(g4 + 1) * 4, :],
                    src[g4 * 512:(g4 + 1) * 512, t0:t0 + T].rearrange("(k p) t -> p k t", p=128),
                    (), [("H", k, n) for k in range(g4 * 4, g4 * 4 + 4) for n in range(2)], "h%d" % g4)

        def rms_to(gcol, out_fn):
            for n in range(2):
                for k in range(16):
                    sqf, sqk = ft_new()
                    sq = sqf.bitcast(BF16)[:, 0:512]
                    act(sq, XF[:, k, nsl(n)], AF.Square, [("X", k, n)], [sqk])
                    mm(PS[6][:], AVG_D, sq, k == 0, k == 15, [sqk, ("CB",)], [("ps", 6)])
                sc.add("act", lambda e, o=RSTD[:, nsl(n)]: e.activation(out=o, in_=PS[6][:], func=AF.Ln, bias=EPS, scale=1.0),
                       r=[("ps", 6)], w=[("RSTD", n)])
                sc.add("act", lambda e, o=RSTD[:, nsl(n)]: e.activation(out=o, in_=o, func=AF.Exp, scale=-0.5),
                       r=[("RSTD", n)], w=[("RSTD", n)])
                for k in range(16):
                    out_fn(k, n)

        def norm_h(vcol0):
            def o(k, n):
                stt(HB[:, k, nsl(n)], XF[:, k, nsl(n)], VECS[:, vcol0 + k:vcol0 + k + 1], RSTD[:, nsl(n)],
                    ALU.mult, ALU.mult, [("X", k, n), ("RSTD", n), ("VECS",)], [("H", k, n)])
            rms_to(vcol0, o)

        def ffn(lj):
            Wgu = wgu[lj * D:(lj + 1) * D, :]
            Wdn = wdn[lj * FF:(lj + 1) * FF, :]
            NS = FF // 256
            tile_ctr = [0]
            dctr = [0]

            def gu_tile(s, q, WG, WU, sg_slot, su_slot):
                aslot = s % 2
                c, n = divmod(q, 2)
                ti = tile_ctr[0]
                tile_ctr[0] += 1
                pg = PS[ti % 2]
                pu = PS[2 + ti % 2]
                for k in range(16):
                    mm(pg[:], WG[:, k, c * 128:(c + 1) * 128], HB[:, k, nsl(n)], k == 0, k == 15,
                       [("W", sg_slot), ("H", k, n)], [("ps", ti % 2)])
                for k in range(16):
                    mm(pu[:], WU[:, k, c * 128:(c + 1) * 128], HB[:, k, nsl(n)], k == 0, k == 15,
                       [("W", su_slot), ("H", k, n)], [("ps", 2 + ti % 2)])
                sg, sgk = ft_new()
                act(sg, pg[:], AF.Silu, [("ps", ti % 2)], [sgk])
                tt(ACTS[:, aslot, c, nsl(n)], sg, pu[:], ALU.mult, [sgk, ("ps", 2 + ti % 2)],
                   [("A", aslot, c, n)])

            def down_tiles(s, WD, sd_slot, lo, hi):
                aslot = s % 2
                for idx in range(lo, hi):
                    m, n = divmod(idx, 2)
                    di = dctr[0]
                    dctr[0] += 1
                    bk = 4 + di % 4
                    pd = PS8[bk]
                    for c in range(2):
                        mm(pd, WD[:, c, m * 128:(m + 1) * 128], ACTS[:, aslot, c, nsl(n)], c == 0, c == 1,
                           [("W", sd_slot), ("A", aslot, c, n)], [("ps", bk) if bk < 7 else ("pst",)])
                    stt(XF[:, m, nsl(n)], pd, 0.5, XF[:, m, nsl(n)], ALU.mult, ALU.add,
                        [("ps", bk) if bk < 7 else ("pst",), ("X", m, n)], [("X", m, n)])

            prev = None
            for s in range(NS):
                sg_slot, WG = wpiece_col(Wgu, s * 256)
                su_slot, WU = wpiece_col(Wgu, FF + s * 256)
                for q in range(4):
                    gu_tile(s, q, WG, WU, sg_slot, su_slot)
                    if prev is not None:
                        down_tiles(s - 1, prev[1], prev[0], q * 8, (q + 1) * 8)
                prev = wpiece_row(Wdn, s * 256)
            down_tiles(NS - 1, prev[1], prev[0], 0, 32)

        def out_proj(W2d, scale=1.0):
            dctr = [0]
            for i in range(8):
                slot, WP = wpiece_col(W2d, i * 256)
                for c in range(2):
                    m = 2 * i + c
                    for n in range(2):
                        di = dctr[0]
                        dctr[0] += 1
                        pd = PS[4 + di % 2]
                        for k in range(16):
                            mm(pd[:], WP[:, k, c * 128:(c + 1) * 128], HB[:, k, nsl(n)], k == 0, k == 15,
                               [("W", slot), ("H", k, n)], [("ps", 4 + di % 2)])
                        tt(XF[:, m, nsl(n)], pd[:], XF[:, m, nsl(n)], ALU.add,
                           [("ps", 4 + di % 2), ("X", m, n)], [("X", m, n)])

        def load_tab(hf, cos_in, sin_in):
            t0 = hf * T
            dma("sp", TAB[:, 0, :], cos_in[:, t0:t0 + T], (), [("TAB", 0)], "tab0")
            dma("sp", TAB[:, 1, :], sin_in[:, t0:t0 + T], (), [("TAB", 1)], "tab1")

        def wpiece_col2(W2d, c0):
            if ctr["w"] % 2 == 1:
                ctr["w"] += 1
            p = ctr["w"]
            ctr["w"] += 2
            slot = p % NB
            dst = WR[:, slot:slot + 2, :].rearrange("p s x -> p (s x)").rearrange("p (k c) -> p k c", c=512)
            src = W2d[:, c0:c0 + 512].rearrange("(k p) c -> p k c", p=128)
            dma("pool", dst, src, (), [("W", slot), ("W", slot + 1)], "w%d" % slot, nobar=True)
            return slot, dst

        def v_proj(hf, W2d, col0, vs):
            t0 = hf * T
            tctr = [0]
            for i in range(4):
                slot, WP = wpiece_col2(W2d, col0 + i * 512)
                for g in range(4):
                    vst, vk, vi = bt_new()
                    vst3 = vst.rearrange("p (t c) -> p t c", c=512)
                    for tl in range(2):
                        ttile = g * 2 + tl
                        ti = tctr[0]
                        tctr[0] += 1
                        pv = PS[ti % 2]
                        for k in range(16):
                            mm(pv[:], HB[:, k, ttile * 128:(ttile + 1) * 128], WP[:, k, :], k == 0, k == 15,
                               [("W", slot), ("W", slot + 1), ("H", k, ttile // 4)], [("ps", ti % 2)])
                        act(vst3[:, tl, :], pv[:], AF.Copy, [("ps", ti % 2)], [vk])
                    dma("sp", vs[t0 + g * 256:t0 + (g + 1) * 256, i * 512:(i + 1) * 512].rearrange("(t p) c -> p t c", p=128),
                        vst3, [vk], [("vs", hf, i, g)], "bt%d" % vi)

        def moba_qk(hf, col0, dst, is_k):
            t0 = hf * T
            tctr = [0]
            pend = []
            for i in range(8):
                slot, WP = wpiece_col(wqkv, col0 + i * 256)
                for c in range(2):
                    head = 2 * i + c
                    st, stk, sti = bt_new()
                    for n in range(2):
                        ti = tctr[0]
                        tctr[0] += 1
                        pa = PS[ti % 2]
                        pb = PS[2 + ti % 2]
                        for k in range(16):
                            mm(pa[:], WP[:, k, c * 128:(c + 1) * 128], HB[:, k, nsl(n)], k == 0, k == 15,
                               [("W", slot), ("H", k, n)], [("ps", ti % 2)])
                        kf, kfk = ft_new()
                        act(kf, pa[:], AF.Copy, [("ps", ti % 2)], [kfk])

                        def rot(kf=kf, kfk=kfk, pb=pb, ti=ti, n=n, st=st, stk=stk, sti=sti, head=head):
                            mm(pb[:], PERMF[:], kf, True, True, [kfk, ("PERMF",)], [("ps", 2 + ti % 2)])
                            t1, t1k = ft_new()
                            tt(t1, kf, TAB[:, 0, nsl(n)], ALU.mult, [kfk, ("TAB", 0)], [t1k])
                            t2, t2k = ft_new()
                            tt(t2, pb[:], TAB[:, 1, nsl(n)], ALU.mult, [("ps", 2 + ti % 2), ("TAB", 1)], [t2k])
                            tt(t1, t1, t2, ALU.add, [t1k, t2k], [t1k])
                            if is_k:
                                for b_ in range(2):
                                    bi = hf * 4 + n * 2 + b_
                                    sc.add("act", lambda e, o=st[:, n * 512 + b_ * 256:n * 512 + (b_ + 1) * 256],
                                           i_=t1[:, b_ * 256:(b_ + 1) * 256], a_=KMS[:, head, bi:bi + 1]:
                                           e.activation(out=o, in_=i_, func=AF.Copy, accum_out=a_),
                                           r=[t1k], w=[stk, ("KMS", head, hf, n)])
                            else:
                                act(st[:, nsl(n)], t1, AF.Copy, [t1k], [stk])
                            if n == 1:
                                dma("sp", dst[head * 128:(head + 1) * 128, t0:t0 + T], st, [stk],
                                    [("qk", is_k, head, hf)], "bt%d" % sti)
                        pend.append(rot)
                        if len(pend) > 1:
                            pend.pop(0)()
            while pend:
                pend.pop(0)()

        for hf in range(2):
            load_x(hf, xT_in)
            load_tab(hf, cosM_in, sinM_in)
            norm_h(0 * 16)
            ffn(0)
            norm_h(1 * 16)
            store_x(hf, xs0)
            moba_qk(hf, D, ks0, True)
            moba_qk(hf, 0, qs0, False)
            v_proj(hf, wqkv, 2 * D, vs0)
            sc.barrier()

        SCALE = 128.0 ** -0.5
        HBf = HB[:].rearrange("p k t -> p (k t)")

        def s2_bufs(hd):
            sl = hd % 2
            QT = HBf[:, sl * 2048:(sl + 1) * 2048]
            KT = HBf[:, 4096 + sl * 2048:4096 + (sl + 1) * 2048]
            VV = HBf[:, 8192 + sl * 2048:8192 + (sl + 1) * 2048].rearrange("p (i d) -> p i d", d=128)
            OST = HBf[:, 12288 + sl * 2048:12288 + (sl + 1) * 2048]
            pb = sl * 2304
            KMB = SMB[:, pb:pb + 8]
            BIASQ = SMB[:, pb + 128:pb + 256]
            BIAST = SMB[0:8, pb + 256:pb + 2304]
            return sl, QT, KT, VV, OST, KMB, BIASQ, BIAST

        def s2_p1(hd):
            sl, QT, KT, VV, OST, KMB, BIASQ, BIAST = s2_bufs(hd)
            dma("sp", QT, qs0[hd * 128:(hd + 1) * 128, :], (), [("QT", sl)], "qt%d" % sl)
            dma("sp", KT, ks0[hd * 128:(hd + 1) * 128, :], (), [("KT", sl)], "kt%d" % sl)
            dma("sp", VV, vs0[:, hd * 128:(hd + 1) * 128].rearrange("(i p) d -> p i d", p=128), (), [("VV", sl)],
                "vv%d" % sl)
            act(KMB, KMS[:, hd, :], AF.Copy, [("KMS", hd, 0, 0), ("KMS", hd, 0, 1), ("KMS", hd, 1, 0), ("KMS", hd, 1, 1)],
                [("KMB", sl)])

        def s2_p2(hd):
            sl, QT, KT, VV, OST, KMB, BIASQ, BIAST = s2_bufs(hd)
            for i in range(16):
                mm(PS[0][:, i * 8:(i + 1) * 8], QT[:, i * 128:(i + 1) * 128], KMB, True, True,
                   [("QT", sl), ("KMB", sl)], [("ps", 0)])
            sb_ = sl * 512
            GM = SM[:, sb_:sb_ + 128]
            TOP8 = SM[:, sb_ + 128:sb_ + 256]
            THR = SM[:, sb_ + 256:sb_ + 272]
            GE = SM[:, sb_ + 384:sb_ + 512]
            tt(GM, PS[0][:, 0:128], PASTB[:], ALU.add, [("ps", 0), ("PASTB",)], [("GM", sl)])
            for i in range(16):
                sc.add("dve", lambda e, o=TOP8[:, i * 8:(i + 1) * 8], i_=GM[:, i * 8:(i + 1) * 8]: e.max(out=o, in_=i_),
                       r=[("GM", sl)], w=[("TOP8", sl, i)])
            ts(THR, TOP8.rearrange("p (i e) -> p i e", e=8)[:, :, 2], -1e29, None, ALU.max, None,
               [("TOP8", sl, i) for i in range(16)], [("THR", sl)])
            tt(GE.rearrange("p (i e) -> p i e", e=8), GM.rearrange("p (i e) -> p i e", e=8),
               THR.unsqueeze(2).to_broadcast([128, 16, 8]), ALU.is_ge, [("GM", sl), ("THR", sl)], [("GE", sl)])
            ts(BIASQ, GE, -1.0, -NEG, ALU.add, ALU.mult, [("GE", sl)], [("BIASQ", sl)])

        def s2_p3(hd):
            sl, QT, KT, VV, OST, KMB, BIASQ, BIAST = s2_bufs(hd)
            for hh in range(2):
                for i8 in range(8):
                    i = hh * 8 + i8
                    sc.add("pe", lambda e, o=PST[0:8, i8 * 128:(i8 + 1) * 128], i_=BIASQ[:, i * 8:(i + 1) * 8]:
                           e.transpose(out=o, in_=i_, identity=IDB), r=[("BIASQ", sl), ("CB",)], w=[("pst",)])
                act(BIAST[:, hh * 1024:(hh + 1) * 1024], PST[0:8, :], AF.Copy, [("pst",)], [("BIAST", sl, hh)])

        s2ctr = {"t": 0, "e": 0, "c": 0}

        def s2_main(hd):
            sl, QT, KT, VV, OST, KMB, BIASQ, BIAST = s2_bufs(hd)
            pend = []
            tails = []
            if hd + 1 < 16:
                s2_p1(hd + 1)

            for j in range(4):
                if hd + 1 < 16 and j == 2:
                    s2_p2(hd + 1)
                if hd + 1 < 16 and j == 3:
                    s2_p3(hd + 1)
                qlo = j * 512
                tiles = [(nb, kt, 0, 512) for nb in range(2 * j + 1) for kt in range(2)]
                tiles += [(2 * j + 1, kt, 256, 512) for kt in range(2)]
                cpar = s2ctr["c"] % 2
                s2ctr["c"] += 1
                bo, bd = (3, 4) if cpar == 0 else (5, 6)
                po, pdn = PS[bo], PS[bd]
                nt = len(tiles)
                for idx, (nb, kt, c0, c1) in enumerate(tiles):
                    sj = s2ctr["t"] % 3
                    s2ctr["t"] += 1
                    ei = s2ctr["e"] % 4
                    s2ctr["e"] += 1
                    pS = PS[sj]
                    kpos = nb * 256 + kt * 128
                    mm(pS[:, c0:c1], KT[:, kpos:kpos + 128], QT[:, qlo + c0:qlo + c1], True, False,
                       [("KT", sl), ("QT", sl)], [("ps", sj)])
                    if nb < 2 * j:
                        mm(pS[:, 0:512], SELALL[:, nb * 128:(nb + 1) * 128], BIAST[:, qlo:qlo + 512], False, True,
                           [("CB",), ("BIAST", sl, j // 2)], [("ps", sj)])
                    elif nb == 2 * j:
                        mm(pS[:, 0:256], IDB, CAUS[:, kt * 256:(kt + 1) * 256], False, False, [("CB",)], [("ps", sj)])
                        mm(pS[:, 256:512], SELALL[:, nb * 128:(nb + 1) * 128], BIAST[:, qlo + 256:qlo + 512], False, True,
                           [("CB",), ("BIAST", sl, j // 2)], [("ps", sj)])
                    else:
                        mm(pS[:, 256:512], IDB, CAUS[:, kt * 256:(kt + 1) * 256], False, True, [("CB",)], [("ps", sj)])
                    E = BT[:, ei, 0:512]
                    act(E[:, c0:c1], pS[:, c0:c1], AF.Exp, [("ps", sj)], [("BT", ei)], scale=SCALE)
                    def C(nb=nb, kt=kt, c0=c0, c1=c1, E=E, ei=ei, idx=idx, po=po, pdn=pdn, bo=bo, bd=bd, nt=nt,
                          qlo=qlo):
                        first = idx == 0
                        last = idx == nt - 1
                        mm(po[:, c0:c1], VV[:, nb * 2 + kt, :], E[:, c0:c1], first, last, [("VV", sl), ("BT", ei)],
                           [("ps", bo)])
                        mm(pdn[:, c0:c1], ONESB, E[:, c0:c1], first, last, [("CB",), ("BT", ei)], [("ps", bd)])
                        if last:
                            def tail(po=po, pdn=pdn, bo=bo, bd=bd, qlo=qlo):
                                rd, rdk = ft_new()
                                sc.add("act", lambda e, o=rd, i_=pdn[:]: e.activation(out=o, in_=i_, func=AF.Ln), r=[("ps", bd)], w=[rdk])
                                sc.add("act", lambda e, o=rd: e.activation(out=o, in_=o, func=AF.Exp, scale=-1.0), r=[rdk], w=[rdk])
                                tt(OST[:, qlo:qlo + 512], po[:], rd, ALU.mult, [("ps", bo), rdk], [("OST", sl)])
                            tails.append([3, tail])
                    pend.append(C)
                    if len(pend) > 1:
                        pend.pop(0)()
                    for tl_ in tails:
                        tl_[0] -= 1
                    while tails and tails[0][0] <= 0:
                        tails.pop(0)[1]()
            while pend:
                pend.pop(0)()
            while tails:
                tails.pop(0)[1]()
            dma("sp", os0[hd * 128:(hd + 1) * 128, :], OST, [("OST", sl)], [("os", hd)], "ost%d" % sl)

        if stop_after >= 2:
            s2_p1(0)
            s2_p2(0)
            s2_p3(0)
            for hd in range(16):
                s2_main(hd)
            sc.barrier()

        def ret_qk(hf, col0, dst, kscale):
            t0 = hf * T
            tctr = [0]
            for h in range(8):
                slot, WP = wpiece_col(win, col0 + h * 256)
                sa, sak, sai = bt_new()
                sbb, sbk, sbi = bt_new()
                for n in range(2):
                    ti = tctr[0]
                    tctr[0] += 1
                    pa = PS[ti % 2]
                    pb = PS[2 + ti % 2]
                    for k in range(16):
                        mm(pa[:], WP[:, k, 0:128], HB[:, k, nsl(n)], k == 0, k == 15,
                           [("W", slot), ("H", k, n)], [("ps", ti % 2)])
                    for k in range(16):
                        mm(pb[:], WP[:, k, 128:256], HB[:, k, nsl(n)], k == 0, k == 15,
                           [("W", slot), ("H", k, n)], [("ps", 2 + ti % 2)])
                    af, afk = ft_new()
                    bf, bfk = ft_new()
                    act(af, pa[:], AF.Copy, [("ps", ti % 2)], [afk], scale=kscale)
                    act(bf, pb[:], AF.Copy, [("ps", 2 + ti % 2)], [bfk], scale=kscale)
                    cosT = TAB[:, 0, nsl(n)]
                    sinT = TAB[:, 1, nsl(n)]
                    t1, t1k = ft_new()
                    t2, t2k = ft_new()
                    tt(t1, af, cosT, ALU.mult, [afk, ("TAB", 0)], [t1k])
                    tt(t2, bf, sinT, ALU.mult, [bfk, ("TAB", 1)], [t2k])
                    tt(sa[:, nsl(n)], t1, t2, ALU.subtract, [t1k, t2k], [sak])
                    t3, t3k = ft_new()
                    t4, t4k = ft_new()
                    tt(t3, af, sinT, ALU.mult, [afk, ("TAB", 1)], [t3k])
                    tt(t4, bf, cosT, ALU.mult, [bfk, ("TAB", 0)], [t4k])
                    tt(sbb[:, nsl(n)], t3, t4, ALU.add, [t3k, t4k], [sbk])
                dma("sp", dst[h * 256:h * 256 + 128, t0:t0 + T], sa, [sak], [("rqk", col0, h, 0, hf)], "bt%d" % sai)
                dma("sp", dst[h * 256 + 128:h * 256 + 256, t0:t0 + T], sbb, [sbk], [("rqk", col0, h, 1, hf)], "bt%d" % sbi)

        def ret_g(hf):
            t0 = hf * T
            tctr = [0]
            for h in range(8):
                slot, WP = wpiece_col(win, 3 * D + h * 256)
                for c in range(2):
                    st, stk, sti = bt_new()
                    for n in range(2):
                        ti = tctr[0]
                        tctr[0] += 1
                        pa = PS[ti % 2]
                        for k in range(16):
                            mm(pa[:], WP[:, k, c * 128:(c + 1) * 128], HB[:, k, nsl(n)], k == 0, k == 15,
                               [("W", slot), ("H", k, n)], [("ps", ti % 2)])
                        act(st[:, nsl(n)], pa[:], AF.Silu, [("ps", ti % 2)], [stk])
                    dma("sp", gs[h * 256 + c * 128:h * 256 + (c + 1) * 128, t0:t0 + T], st, [stk], [("gs", h, c, hf)],
                        "bt%d" % sti)

        if stop_after >= 3:
            for hf in range(2):
                load_x(hf, xs0)
                load_h(hf, os0)
                load_tab(hf, cosR_in, sinR_in)
                out_proj(wo_m)
                norm_h(2 * 16)
                ffn(1)
                norm_h(3 * 16)
                ffn(2)
                norm_h(4 * 16)
                store_x(hf, xs1)
                ret_qk(hf, 0, qs1, 1.0)
                ret_qk(hf, D, ks1, 1.0 / 16.0)
                v_proj(hf, win, 2 * D, vs1)
                ret_g(hf)
                sc.barrier()

        if stop_after >= 4:
            XFf = XF[:].rearrange("p k t -> p (k t)")
            s4ctr = [0]
            XFb = XFf.bitcast(BF16)

            def s4_bufs(h):
                par = h % 2
                B0 = HBf if par == 0 else XFb[:, 16384:32768]
                QT0 = B0[:, 0:2048]
                QT1 = B0[:, 2048:4096]
                KT0 = B0[:, 4096:6144]
                KT1 = B0[:, 6144:8192]
                VV = B0[:, 8192:12288].rearrange("p (i d) -> p i d", d=256)
                GS0 = B0[:, 12288:14336]
                GS1 = B0[:, 14336:16384]
                DT = XFf[:, par * 2560:(par + 1) * 2560].rearrange("p (r c) -> p r c", c=512)
                return par, QT0, QT1, KT0, KT1, VV, GS0, GS1, DT

            def s4_load(h):
                par, QT0, QT1, KT0, KT1, VV, GS0, GS1, DT = s4_bufs(h)
                dma("sp", QT0, qs1[h * 256:h * 256 + 128, :], (), [("RQ", par, 0)], "rq0%d" % par)
                dma("sp", QT1, qs1[h * 256 + 128:h * 256 + 256, :], (), [("RQ", par, 1)], "rq1%d" % par)
                dma("sp", KT0, ks1[h * 256:h * 256 + 128, :], (), [("RK", par, 0)], "rk0%d" % par)
                dma("sp", KT1, ks1[h * 256 + 128:h * 256 + 256, :], (), [("RK", par, 1)], "rk1%d" % par)
                dma("sp", VV, vs1[:, h * 256:(h + 1) * 256].rearrange("(i p) d -> p i d", p=128), (), [("RV", par)],
                    "rv%d" % par)
                dma("sp", GS0, gs[h * 256:h * 256 + 128, :], (), [("RG", par, 0)], "rg0%d" % par)
                dma("sp", GS1, gs[h * 256 + 128:h * 256 + 256, :], (), [("RG", par, 1)], "rg1%d" % par)
                dma("sp", DT, dtab_in[h * 128:(h + 1) * 128, :].rearrange("p (r c) -> p r c", c=512), (),
                    [("DT", par)], "dt%d" % par)

            s4_load(0)
            for h in range(8):
                g_ = gam[h]
                if h + 1 < 8:
                    s4_load(h + 1)
                par, QT0, QT1, KT0, KT1, VV, GS0, GS1, DT = s4_bufs(h)
                YST = [BT[:, 0, :], BT[:, 1, :], BT[:, 2, :], BT[:, 3, :]]
                QTs = [QT0, QT1]
                KTs = [KT0, KT1]
                GSs = [GS0, GS1]
                pend = []
                tails = []
                for j in range(4):
                    jsl = slice(j * 512, (j + 1) * 512)
                    nm = 4 * j + 4
                    ob = (3, 4) if j % 2 == 0 else (5, 6)
                    for i in range(nm):
                        sj = s4ctr[0] % 3
                        s4ctr[0] += 1
                        pS = PS[sj]
                        for dc in range(2):
                            mm(pS[:], KTs[dc][:, i * 128:(i + 1) * 128], QTs[dc][:, jsl], dc == 0, dc == 1,
                               [("RK", par, dc), ("RQ", par, dc)], [("ps", sj)])
                        STt = SMB[:, sj * 512:(sj + 1) * 512]
                        if i >= 4 * j:
                            tt(STt, pS[:], DT[:, i - 4 * j, :], ALU.mult, [("ps", sj), ("DT", par)], [("ST", sj)])
                        else:
                            cst = float(g_ ** (j * 512 - i * 128 - 127))
                            stt(STt, pS[:], cst, DT[:, 4, :], ALU.mult, ALU.mult, [("ps", sj), ("DT", par)],
                                [("ST", sj)])

                        def C(i=i, nm=nm, STt=STt, sj=sj, ob=ob, j=j, jsl=jsl):
                            for vc in range(2):
                                mm(PS[ob[vc]][:], VV[:, i, vc * 128:(vc + 1) * 128], STt, i == 0, i == nm - 1,
                                   [("RV", par), ("ST", sj)], [("ps", ob[vc])])
                            if i != nm - 1:
                                return
                            sqs = []
                            for vc in range(2):
                                sqf, sqk = ft_new()
                                sq = sqf.bitcast(BF16)[:, 0:512]
                                act(sq, PS[ob[vc]][:], AF.Square, [("ps", ob[vc])], [sqk])
                                sqs.append((sq, sqk))

                            def tail(sqs=sqs, ob=ob, j=j, jsl=jsl):
                                nj = s4ctr[0] % 3
                                s4ctr[0] += 1
                                pN = PS[nj]
                                for vc in range(2):
                                    mm(pN[:], AVG_G, sqs[vc][0], vc == 0, vc == 1, [sqs[vc][1], ("CB",)], [("ps", nj)])
                                rs, rsk = ft_new()
                                sc.add("act", lambda e, o=rs, p_=pN: e.activation(out=o, in_=p_[:], func=AF.Ln, bias=EPS, scale=1.0),
                                       r=[("ps", nj)], w=[rsk])
                                sc.add("act", lambda e, o=rs: e.activation(out=o, in_=o, func=AF.Exp, scale=-0.5), r=[rsk], w=[rsk])
                                for vc in range(2):
                                    t1, t1k = ft_new()
                                    col = 112 + h * 2 + vc
                                    stt(t1, PS[ob[vc]][:], VECS[:, col:col + 1], rs, ALU.mult, ALU.mult,
                                        [("ps", ob[vc]), rsk, ("VECS",)], [t1k])
                                    yslot = vc * 2 + j // 2
                                    tt(YST[yslot][:, (j % 2) * 512:(j % 2 + 1) * 512], t1, GSs[vc][:, jsl], ALU.mult,
                                       [t1k, ("RG", par, vc)], [("BT", yslot)])
                            tails.append([3, tail])
                        pend.append(C)
                        if len(pend) > 1:
                            pend.pop(0)()
                        for tl_ in tails:
                            tl_[0] -= 1
                        while tails and tails[0][0] <= 0:
                            tails.pop(0)[1]()
                while pend:
                    pend.pop(0)()
                while tails:
                    tails.pop(0)[1]()
                for vc in range(2):
                    for hh in range(2):
                        yslot = vc * 2 + hh
                        dma("sp", os1[h * 256 + vc * 128:h * 256 + (vc + 1) * 128, hh * T:(hh + 1) * T], YST[yslot],
                            [("BT", yslot)], [("ys", h, vc, hh)], "bt%d" % yslot)
            sc.barrier()

        if stop_after >= 5:
            for hf in range(2):
                t0 = hf * T
                load_x(hf, xs1)
                load_h(hf, os1)
                out_proj(wo_r)
                norm_h(5 * 16)
                ffn(3)

                def o(k, n):
                    ot, otk = ft_new()
                    stt(ot, XF[:, k, nsl(n)], VECS[:, 96 + k:97 + k], RSTD[:, nsl(n)], ALU.mult, ALU.mult,
                        [("X", k, n), ("RSTD", n), ("VECS",)], [otk])
                    dma("sp", outT[k * 128:(k + 1) * 128, t0 + n * 512:t0 + (n + 1) * 512], ot, [otk],
                        [("out", k, n, hf)], "o%d" % (otk[1]))
                rms_to(96, o)
                sc.barrier()

        if debug_out == "xs":
            pass
        sc.barrier()
        sc.add("sp", None)

        sem_names = sc.finalize()
        sems = {}
        for nme in sem_names:
            sems[nme] = es.enter_context(nc.semaphore(nme))
        block = es.enter_context(nc.Block())

        @block.tensor
        def _(e):
            sc.emit("pe", e, sems)

        @block.scalar
        def _(e):
            sc.emit("act", e, sems)

        @block.vector
        def _(e):
            sc.emit("dve", e, sems)

        @block.gpsimd
        def _(e):
            sc.emit("pool", e, sems)

        @block.sync
        def _(e):
            sc.emit("sp", e, sems)

    return nc


def _prep_inputs(x, norm_gain, ffn_w_gate_up, ffn_w_down, moba_w_qkv, moba_w_o,
                 ret_w_in, ret_w_o, ret_gn_gain, final_norm):
    C = _get_consts()
    f = lambda a: np.ascontiguousarray(np.asarray(a, dtype=np.float32))
    vecs = np.zeros((128, 128), np.float32)
    ng = f(norm_gain).reshape(6, 16, 128)
    vecs[:, 0:96] = ng.transpose(2, 0, 1).reshape(128, 96)
    vecs[:, 96:112] = f(final_norm).reshape(16, 128).T
    vecs[:, 112:128] = f(ret_gn_gain).reshape(16, 128).T
    shared = {
        "wgu": f(ffn_w_gate_up).reshape(4 * D, 2 * FF),
        "wdn": f(ffn_w_down).reshape(4 * FF, D),
        "wqkv": f(moba_w_qkv).reshape(D, 3 * D),
        "wo_m": f(moba_w_o).reshape(D, D),
        "win": f(ret_w_in).reshape(D, 4 * D),
        "wo_r": f(ret_w_o).reshape(D, D),
        "vecs": vecs,
        "cosM": C["cosM"], "sinM": C["sinM"], "cosR": C["cosR"], "sinR": C["sinR"],
        "permF": C["permF"], "onesF": C["onesF"], "pastb": C["pastb"], "cb": C["cb"], "dtab": C["dtab"],
    }
    xf = f(x)
    in_maps = []
    zx = np.zeros((D, S), np.float32)
    for c in range(8):
        m = dict(shared)
        if c in ACTIVE:
            m["xT"] = np.ascontiguousarray(xf[ACTIVE.index(c)].T)
        else:
            m["xT"] = zx
        in_maps.append(m)
    return in_maps


def kernel(x, norm_gain, ffn_w_gate_up, ffn_w_down, moba_w_qkv, moba_w_o,
           ret_w_in, ret_w_o, ret_gn_gain, final_norm):
    in_maps = _prep_inputs(x, norm_gain, ffn_w_gate_up, ffn_w_down, moba_w_qkv, moba_w_o,
                           ret_w_in, ret_w_o, ret_gn_gain, final_norm)
    nc = build_program()
    res = run_bass_kernel_spmd(nc, in_maps, core_ids=list(range(8)))
    out = np.stack([np.ascontiguousarray(res.results[ACTIVE[b]]["outT"].T) for b in range(4)], axis=0)
    return out.astype(np.float32)
```

```python
import os
from contextlib import ExitStack

import numpy as np
import concourse.bass as bass
import concourse.mybir as mybir
from concourse.bass_utils import run_bass_kernel_spmd

F32 = mybir.dt.float32
BF16 = mybir.dt.bfloat16
AF = mybir.ActivationFunctionType
ALU = mybir.AluOpType
AX = mybir.AxisListType

D = 2048
S = 2048
FF = 5632
T = 1024
NB = 6
EPS = 1e-6
NEG = -30000.0
EPOCH = 6000
ACTIVE = [0, 1, 4, 5]


class Op:
    __slots__ = ("eng", "fn", "deps", "signal", "sig", "chan")


class Sched:
    ENGS = ("pe", "act", "dve", "pool", "sp")

    def __init__(self):
        self.q = {e: [] for e in self.ENGS}
        self.lw = {}
        self.rd = {}
        self.bar = []
        self.chan_last = {}

    def add(self, eng, fn, r=(), w=(), chan=None, nobar=False):
        o = Op()
        o.eng = eng
        o.fn = fn
        o.signal = False
        o.sig = None
        o.chan = chan
        deps = set()
        if not nobar:
            deps.update(self.bar)
        for k in r:
            x = self.lw.get(k)
            if x is not None:
                deps.add(x)
        for k in w:
            x = self.lw.get(k)
            if x is not None:
                deps.add(x)
            rr = self.rd.get(k)
            if rr:
                deps.update(rr.values())
        for k in w:
            self.lw[k] = o
            self.rd[k] = {}
        for k in r:
            rr = self.rd.setdefault(k, {})
            rr[(eng if chan is None else ("dma", id(o)))] = o
        deps.discard(o)
        o.deps = deps
        for d in deps:
            d.signal = True
        self.q[eng].append(o)
        if chan is not None:
            self.chan_last[chan] = o
        return o

    def barrier(self, keep_prefix=("W",)):
        bar = []
        for e in self.ENGS:
            for o in reversed(self.q[e]):
                if o.chan is None:
                    bar.append(o)
                    o.signal = True
                    break
        for c, o in self.chan_last.items():
            bar.append(o)
        self.bar = bar
        self.lw = {k: v for k, v in self.lw.items() if k[0] in keep_prefix}
        self.rd = {k: v for k, v in self.rd.items() if k[0] in keep_prefix}

    def finalize(self):
        sem_names = []
        for e in self.ENGS:
            cnt = 0
            chan_cnt = {}
            for o in self.q[e]:
                if o.chan is not None:
                    pass
                elif o.signal:
                    cnt += 1
                    ep = (cnt - 1) // EPOCH
                    name = "s_%s_%d" % (e, ep)
                    if name not in sem_names:
                        sem_names.append(name)
                    o.sig = (name, cnt - ep * EPOCH)
        chan_cnt = {}
        for e in self.ENGS:
            for o in self.q[e]:
                if o.chan is not None:
                    n = chan_cnt.get(o.chan, 0) + 1
                    chan_cnt[o.chan] = n
                    name = "c_" + o.chan
                    if name not in sem_names:
                        sem_names.append(name)
                    o.sig = (name, 16 * n)
        return sem_names

    def emit(self, eng_name, e, sems):
        waited = {}
        for o in self.q[eng_name]:
            ws = {}
            for d in o.deps:
                if eng_name == "pe" and d.eng == "pe" and d.chan is None:
                    continue
                s, v = d.sig
                if waited.get(s, 0) >= v:
                    continue
                if ws.get(s, 0) < v:
                    ws[s] = v
            for s in sorted(ws):
                e.wait_ge(sems[s], ws[s])
                waited[s] = ws[s]
            if o.fn is None:
                continue
            ins = o.fn(e)
            if o.chan is not None:
                ins.then_inc(sems[o.sig[0]], 16)
            elif o.signal:
                ins.then_inc(sems[o.sig[0]], 1)


def _consts():
    c = {}
    pos = np.arange(S, dtype=np.float32)
    half = 16
    inv = np.power(np.float32(500000.0), -np.arange(half, dtype=np.float32) / np.float32(half)).astype(np.float32)
    ang = (pos[:, None] * inv[None, :]).astype(np.float32)
    cosv = np.cos(ang).astype(np.float32).T
    sinv = np.sin(ang).astype(np.float32).T
    cosM = np.ones((128, S), np.float32)
    sinM = np.zeros((128, S), np.float32)
    cosM[0:16] = cosv
    cosM[16:32] = cosv
    sinM[0:16] = -sinv
    sinM[16:32] = sinv
    c["cosM"] = cosM
    c["sinM"] = sinM
    perm = np.zeros((128, 128), np.float32)
    for d in range(16):
        perm[d + 16, d] = 1.0
        perm[d, d + 16] = 1.0
    c["permF"] = perm
    c["onesF"] = np.concatenate([np.full((128, 128), 1.0 / 2048.0, np.float32), np.full((128, 128), 1.0 / 256.0, np.float32)], axis=1)
    invr = np.power(np.float32(10000.0), -np.linspace(0.0, 1.0, 128, dtype=np.float32)).astype(np.float32)
    angr = (pos[:, None] * invr[None, :]).astype(np.float32)
    c["cosR"] = np.cos(angr).astype(np.float32).T.copy()
    c["sinR"] = np.sin(angr).astype(np.float32).T.copy()
    pastb = np.zeros((128, 16, 8), np.float32)
    for i in range(16):
        for n in range(8):
            if not (n < i // 2):
                pastb[:, i, n] = -1e30
    c["pastb"] = pastb.reshape(128, 128)
    import ml_dtypes
    cb = np.zeros((128, 2048), np.float32)
    cb[:, 0:128] = np.eye(128)
    cb[:, 128:256] = 1.0
    for n in range(8):
        cb[n, 256 + n * 128:256 + (n + 1) * 128] = 1.0
    kk = np.arange(128)[:, None]
    qq = np.arange(256)[None, :]
    for kt in range(2):
        cb[:, 1280 + kt * 256:1280 + (kt + 1) * 256] = np.where(kt * 128 + kk <= qq, 0.0, NEG)
    cb[:, 1792:1920] = 1.0 / 2048.0
    cb[:, 1920:2048] = 1.0 / 256.0
    c["cb"] = cb.astype(ml_dtypes.bfloat16)
    dt = np.zeros((8, 128, 5, 512), np.float64)
    p = np.arange(128)[:, None].astype(np.float64)
    nl = np.arange(512)[None, :].astype(np.float64)
    gam = []
    for h in range(8):
        g = 1.0 - 2.0 ** (-5.0 - h)
        gam.append(g)
        lg = np.log(g)
        for r in range(4):
            dd = nl - (r * 128 + p)
            dt[h, :, r, :] = np.where(dd >= 0, np.exp(dd * lg), 0.0)
        dt[h, :, 4, :] = np.exp((nl - p + 127.0) * lg)
    c["dtab"] = dt.astype(np.float32).reshape(8 * 128, 5 * 512)
    c["gam"] = gam
    return c


_C = None


def _get_consts():
    global _C
    if _C is None:
        _C = _consts()
    return _C


def build_program(stop_after=99, debug_out=None):
    C = _get_consts()
    gam = C["gam"]
    nc = bass.Bass("TRN2", target_bir_lowering=False)
    sc = Sched()

    def din(name, shape, dt=F32):
        return nc.dram_tensor(name, list(shape), dt, kind="ExternalInput").ap()

    def dscr(name, shape, dt):
        kind = "ExternalOutput" if (debug_out == name or debug_out == "all") else "Internal"
        return nc.dram_tensor(name, list(shape), dt, kind=kind).ap()

    xT_in = din("xT", [D, S])
    wgu = din("wgu", [4 * D, 2 * FF])
    wdn = din("wdn", [4 * FF, D])
    wqkv = din("wqkv", [D, 3 * D])
    wo_m = din("wo_m", [D, D])
    win = din("win", [D, 4 * D])
    wo_r = din("wo_r", [D, D])
    vecs_in = din("vecs", [128, 128])
    cosM_in = din("cosM", [128, S])
    sinM_in = din("sinM", [128, S])
    cosR_in = din("cosR", [128, S])
    sinR_in = din("sinR", [128, S])
    permF_in = din("permF", [128, 128])
    onesF_in = din("onesF", [128, 256])
    pastb_in = din("pastb", [128, 128])
    cb_in = din("cb", [128, 2048], BF16)
    dtab_in = din("dtab", [8 * 128, 5 * 512])
    if debug_out in ("outT", "all", None):
        outT = nc.dram_tensor("outT", [D, S], F32, kind="ExternalOutput").ap()
    else:
        outT = nc.dram_tensor("outT", [D, S], F32, kind="Internal").ap()
    xs0 = dscr("xs0", [D, S], F32)
    qs0 = dscr("qs0", [D, S], BF16)
    ks0 = dscr("ks0", [D, S], BF16)
    vs0 = dscr("vs0", [S, D], BF16)
    os0 = dscr("os0", [D, S], BF16)
    xs1 = dscr("xs1", [D, S], F32)
    qs1 = dscr("qs1", [D, S], BF16)
    ks1 = dscr("ks1", [D, S], BF16)
    vs1 = dscr("vs1", [S, D], BF16)
    os1 = dscr("os1", [D, S], BF16)
    gs = dscr("gs", [D, S], BF16)

    es = ExitStack()
    with es:
        def sb(name, shape, dt):
            return es.enter_context(nc.sbuf_tensor(name, list(shape), dt))

        XF = sb("XF", [128, 16, T], F32)
        HB = sb("HB", [128, 16, T], BF16)
        WR = sb("WR", [128, NB, 4096], BF16)
        ACTS = sb("ACTS", [128, 2, 2, T], BF16)
        FT = sb("FT", [128, 8, 512], F32)
        BT = sb("BT", [128, 4, T], BF16)
        RSTD = sb("RSTD", [128, T], F32)
        TAB = sb("TAB", [128, 2, T], F32)
        VECS = sb("VECS", [128, 128], F32)
        PERMF = sb("PERMF", [128, 128], F32)
        PASTB = sb("PASTB", [128, 128], F32)
        CB = sb("CB", [128, 2048], BF16)
        KMS = sb("KMS", [128, 16, 8], F32)
        SM = sb("SM", [128, 1024], F32)
        SMB = sb("SMB", [128, 4608], BF16)
        PS = [es.enter_context(nc.psum_tensor("ps%d" % i, [128, 512], F32)) for i in range(7)]
        PST = es.enter_context(nc.psum_tensor("pst", [128, 1024], BF16))

        PS8 = [p[:] for p in PS] + [PST[:].bitcast(F32)]
        IDB = CB[:, 0:128]
        ONESB = CB[:, 128:256]
        SELALL = CB[0:8, 256:1280]
        CAUS = CB[:, 1280:1792]
        AVG_D = CB[:, 1792:1920]
        AVG_G = CB[:, 1920:2048]

        ctr = {"w": 0, "ft": 0, "bt": 0}

        def dma(eng, out, in_, r, w, chan, nobar=False):
            return sc.add(eng, lambda e: e.dma_start(out=out, in_=in_), r=r, w=w, chan=chan, nobar=nobar)

        def wpiece_col(W2d, c0):
            p = ctr["w"]
            ctr["w"] += 1
            slot = p % NB
            dst = WR[:, slot, :].rearrange("p (k c) -> p k c", c=256)
            src = W2d[:, c0:c0 + 256].rearrange("(k p) c -> p k c", p=128)
            dma("pool", dst, src, (), [("W", slot)], "w%d" % slot, nobar=True)
            return slot, dst

        def wpiece_row(W2d, r0):
            p = ctr["w"]
            ctr["w"] += 1
            slot = p % NB
            dst = WR[:, slot, :].rearrange("p (k c) -> p k c", c=2048)
            src = W2d[r0:r0 + 256, :].rearrange("(k p) c -> p k c", p=128)
            dma("pool", dst, src, (), [("W", slot)], "w%d" % slot, nobar=True)
            return slot, dst

        def ft_new():
            i = ctr["ft"] % 8
            ctr["ft"] += 1
            return FT[:, i, :], ("FT", i)

        def bt_new():
            i = ctr["bt"] % 4
            ctr["bt"] += 1
            return BT[:, i, :], ("BT", i), i

        def mm(out, lhsT, rhs, start, stop, r, w):
            sc.add("pe", lambda e: e.matmul(out, lhsT, rhs, start=start, stop=stop), r=r, w=w)

        def act(out, in_, func, r, w, scale=1.0):
            sc.add("act", lambda e: e.activation(out=out, in_=in_, func=func, scale=scale), r=r, w=w)

        def tt(out, in0, in1, op, r, w, eng="dve"):
            sc.add(eng, lambda e: e.tensor_tensor(out=out, in0=in0, in1=in1, op=op), r=r, w=w)

        def ts(out, in0, s1, s2, op0, op1, r, w, eng="dve"):
            if op1 is None:
                sc.add(eng, lambda e: e.tensor_scalar(out=out, in0=in0, scalar1=s1, scalar2=None, op0=op0), r=r, w=w)
            else:
                sc.add(eng, lambda e: e.tensor_scalar(out=out, in0=in0, scalar1=s1, scalar2=s2, op0=op0, op1=op1), r=r, w=w)

        def stt(out, in0, scalar, in1, op0, op1, r, w, eng="dve"):
            sc.add(eng, lambda e: e.scalar_tensor_tensor(out=out, in0=in0, scalar=scalar, in1=in1, op0=op0, op1=op1), r=r, w=w)

        def nsl(n):
            return slice(n * 512, (n + 1) * 512)

        dma("sp", VECS[:], vecs_in, (), [("VECS",)], "c0")
        sc.add("dve", lambda e: e.memset(KMS[:], 0.0), w=[("KMSINIT",)])
        dma("sp", PERMF[:], permF_in, (), [("PERMF",)], "c2")
        dma("sp", PASTB[:], pastb_in, (), [("PASTB",)], "c3")
        dma("sp", CB[:], cb_in, (), [("CB",)], "c4")
        sc.barrier()

        def load_x(hf, src):
            t0 = hf * T
            for g4 in range(4):
                dma("sp", XF[:, g4 * 4:(g4 + 1) * 4, :],
                    src[g4 * 512:(g4 + 1) * 512, t0:t0 + T].rearrange("(k p) t -> p k t", p=128),
                    (), [("X", k, n) for k in range(g4 * 4, g4 * 4 + 4) for n in range(2)], "x%d" % g4)

        def store_x(hf, dst):
            t0 = hf * T
            for g4 in range(4):
                dma("sp", dst[g4 * 512:(g4 + 1) * 512, t0:t0 + T].rearrange("(k p) t -> p k t", p=128),
                    XF[:, g4 * 4:(g4 + 1) * 4, :],
                    [("X", k, n) for k in range(g4 * 4, g4 * 4 + 4) for n in range(2)], (), "x%d" % g4)

        def load_h(hf, src):
            t0 = hf * T
            for g4 in range(4):
                dma("sp", HB[:, g4 * 4:(g4 + 1) * 4, :],
                    src[g4 * 512:(g4 + 1) * 512, t0:t0 + T].rearrange("(k p) t -> p k t", p=128),
                    (), [("H", k, n) for k in range(g4 * 4, g4 * 4 + 4) for n in range(2)], "h%d" % g4)

        def rms_to(gcol, out_fn):
            for n in range(2):
                for k in range(16):
                    sqf, sqk = ft_new()
                    sq = sqf.bitcast(BF16)[:, 0:512]
                    act(sq, XF[:, k, nsl(n)], AF.Square, [("X", k, n)], [sqk])
                    mm(PS[6][:], AVG_D, sq, k == 0, k == 15, [sqk, ("CB",)], [("ps", 6)])
                sc.add("act", lambda e, o=RSTD[:, nsl(n)]: e.activation(out=o, in_=PS[6][:], func=AF.Ln, bias=EPS, scale=1.0),
                       r=[("ps", 6)], w=[("RSTD", n)])
                sc.add("act", lambda e, o=RSTD[:, nsl(n)]: e.activation(out=o, in_=o, func=AF.Exp, scale=-0.5),
                       r=[("RSTD", n)], w=[("RSTD", n)])
                for k in range(16):
                    out_fn(k, n)

        def norm_h(vcol0):
            def o(k, n):
                stt(HB[:, k, nsl(n)], XF[:, k, nsl(n)], VECS[:, vcol0 + k:vcol0 + k + 1], RSTD[:, nsl(n)],
                    ALU.mult, ALU.mult, [("X", k, n), ("RSTD", n), ("VECS",)], [("H", k, n)])
            rms_to(vcol0, o)

        def ffn(lj):
            Wgu = wgu[lj * D:(lj + 1) * D, :]
            Wdn = wdn[lj * FF:(lj + 1) * FF, :]
            NS = FF // 256
            tile_ctr = [0]
            dctr = [0]

            def gu_tile(s, q, WG, WU, sg_slot, su_slot, inter):
                aslot = s % 2
                c, n = divmod(q, 2)
                ti = tile_ctr[0]
                tile_ctr[0] += 1
                pg = PS[ti % 2]
                pu = PS[2 + ti % 2]
                inter = list(inter)
                cnt = 0
                for k in range(16):
                    mm(pg[:], WG[:, k, c * 128:(c + 1) * 128], HB[:, k, nsl(n)], k == 0, k == 15,
                       [("W", sg_slot), ("H", k, n)], [("ps", ti % 2)])
                    cnt += 1
                    if cnt % 4 == 0 and inter:
                        inter.pop(0)()
                for k in range(16):
                    mm(pu[:], WU[:, k, c * 128:(c + 1) * 128], HB[:, k, nsl(n)], k == 0, k == 15,
                       [("W", su_slot), ("H", k, n)], [("ps", 2 + ti % 2)])
                    cnt += 1
                    if cnt % 4 == 0 and inter:
                        inter.pop(0)()
                while inter:
                    inter.pop(0)()
                sg, sgk = ft_new()
                act(sg, pg[:], AF.Silu, [("ps", ti % 2)], [sgk])
                tt(ACTS[:, aslot, c, nsl(n)], sg, pu[:], ALU.mult, [sgk, ("ps", 2 + ti % 2)],
                   [("A", aslot, c, n)])

            def down_tile(s, WD, sd_slot, idx):
                aslot = s % 2
                m, n = divmod(idx, 2)
                di = dctr[0]
                dctr[0] += 1
                bk = 4 + di % 4
                pd = PS8[bk]
                bkey = ("ps", bk) if bk < 7 else ("pst",)
                for c in range(2):
                    mm(pd, WD[:, c, m * 128:(m + 1) * 128], ACTS[:, aslot, c, nsl(n)], c == 0, c == 1,
                       [("W", sd_slot), ("A", aslot, c, n)], [bkey])
                stt(XF[:, m, nsl(n)], pd, 0.5, XF[:, m, nsl(n)], ALU.mult, ALU.add,
                    [bkey, ("X", m, n)], [("X", m, n)])

            prev = None
            for s in range(NS):
                sg_slot, WG = wpiece_col(Wgu, s * 256)
                su_slot, WU = wpiece_col(Wgu, FF + s * 256)
                for q in range(4):
                    inter = []
                    if prev is not None:
                        inter = [(lambda idx=idx, pv=prev, s_=s - 1: down_tile(s_, pv[1], pv[0], idx))
                                 for idx in range(q * 8, (q + 1) * 8)]
                    gu_tile(s, q, WG, WU, sg_slot, su_slot, inter)
                prev = wpiece_row(Wdn, s * 256)
            for idx in range(32):
                down_tile(NS - 1, prev[1], prev[0], idx)

        def out_proj(W2d, scale=1.0):
            dctr = [0]
            for i in range(8):
                slot, WP = wpiece_col(W2d, i * 256)
                for c in range(2):
                    m = 2 * i + c
                    for n in range(2):
                        di = dctr[0]
                        dctr[0] += 1
                        pd = PS[4 + di % 2]
                        for k in range(16):
                            mm(pd[:], WP[:, k, c * 128:(c + 1) * 128], HB[:, k, nsl(n)], k == 0, k == 15,
                               [("W", slot), ("H", k, n)], [("ps", 4 + di % 2)])
                        tt(XF[:, m, nsl(n)], pd[:], XF[:, m, nsl(n)], ALU.add,
                           [("ps", 4 + di % 2), ("X", m, n)], [("X", m, n)])

        def load_tab(hf, cos_in, sin_in):
            t0 = hf * T
            dma("sp", TAB[:, 0, :], cos_in[:, t0:t0 + T], (), [("TAB", 0)], "tab0")
            dma("sp", TAB[:, 1, :], sin_in[:, t0:t0 + T], (), [("TAB", 1)], "tab1")

        def wpiece_col2(W2d, c0):
            if ctr["w"] % 2 == 1:
                ctr["w"] += 1
            p = ctr["w"]
            ctr["w"] += 2
            slot = p % NB
            dst = WR[:, slot:slot + 2, :].rearrange("p s x -> p (s x)").rearrange("p (k c) -> p k c", c=512)
            src = W2d[:, c0:c0 + 512].rearrange("(k p) c -> p k c", p=128)
            dma("pool", dst, src, (), [("W", slot), ("W", slot + 1)], "w%d" % slot, nobar=True)
            return slot, dst

        def v_proj(hf, W2d, col0, vs):
            t0 = hf * T
            tctr = [0]
            for i in range(4):
                slot, WP = wpiece_col2(W2d, col0 + i * 512)
                for g in range(4):
                    vst, vk, vi = bt_new()
                    vst3 = vst.rearrange("p (t c) -> p t c", c=512)
                    for tl in range(2):
                        ttile = g * 2 + tl
                        ti = tctr[0]
                        tctr[0] += 1
                        pv = PS[ti % 2]
                        for k in range(16):
                            mm(pv[:], HB[:, k, ttile * 128:(ttile + 1) * 128], WP[:, k, :], k == 0, k == 15,
                               [("W", slot), ("W", slot + 1), ("H", k, ttile // 4)], [("ps", ti % 2)])
                        act(vst3[:, tl, :], pv[:], AF.Copy, [("ps", ti % 2)], [vk])
                    dma("sp", vs[t0 + g * 256:t0 + (g + 1) * 256, i * 512:(i + 1) * 512].rearrange("(t p) c -> p t c", p=128),
                        vst3, [vk], [("vs", hf, i, g)], "bt%d" % vi)

        def moba_qk(hf, col0, dst, is_k):
            t0 = hf * T
            tctr = [0]
            pend = []
            for i in range(8):
                slot, WP = wpiece_col(wqkv, col0 + i * 256)
                for c in range(2):
                    head = 2 * i + c
                    st, stk, sti = bt_new()
                    for n in range(2):
                        ti = tctr[0]
                        tctr[0] += 1
                        pa = PS[ti % 2]
                        pb = PS[2 + ti % 2]
                        for k in range(16):
                            mm(pa[:], WP[:, k, c * 128:(c + 1) * 128], HB[:, k, nsl(n)], k == 0, k == 15,
                               [("W", slot), ("H", k, n)], [("ps", ti % 2)])
                        kf, kfk = ft_new()
                        act(kf, pa[:], AF.Copy, [("ps", ti % 2)], [kfk])

                        def rot(kf=kf, kfk=kfk, pb=pb, ti=ti, n=n, st=st, stk=stk, sti=sti, head=head):
                            mm(pb[:], PERMF[:], kf, True, True, [kfk, ("PERMF",)], [("ps", 2 + ti % 2)])
                            t1, t1k = ft_new()
                            tt(t1, kf, TAB[:, 0, nsl(n)], ALU.mult, [kfk, ("TAB", 0)], [t1k])
                            t2, t2k = ft_new()
                            tt(t2, pb[:], TAB[:, 1, nsl(n)], ALU.mult, [("ps", 2 + ti % 2), ("TAB", 1)], [t2k])
                            tt(t1, t1, t2, ALU.add, [t1k, t2k], [t1k])
                            if is_k:
                                for b_ in range(2):
                                    bi = hf * 4 + n * 2 + b_
                                    sc.add("act", lambda e, o=st[:, n * 512 + b_ * 256:n * 512 + (b_ + 1) * 256],
                                           i_=t1[:, b_ * 256:(b_ + 1) * 256], a_=KMS[:, head, bi:bi + 1]:
                                           e.activation(out=o, in_=i_, func=AF.Copy, accum_out=a_),
                                           r=[t1k], w=[stk, ("KMS", head, hf, n)])
                            else:
                                act(st[:, nsl(n)], t1, AF.Copy, [t1k], [stk])
                            if n == 1:
                                dma("sp", dst[head * 128:(head + 1) * 128, t0:t0 + T], st, [stk],
                                    [("qk", is_k, head, hf)], "bt%d" % sti)
                        pend.append(rot)
                        if len(pend) > 1:
                            pend.pop(0)()
            while pend:
                pend.pop(0)()

        for hf in range(2):
            load_x(hf, xT_in)
            load_tab(hf, cosM_in, sinM_in)
            norm_h(0 * 16)
            ffn(0)
            norm_h(1 * 16)
            store_x(hf, xs0)
            moba_qk(hf, D, ks0, True)
            moba_qk(hf, 0, qs0, False)
            v_proj(hf, wqkv, 2 * D, vs0)
            sc.barrier()

        SCALE = 128.0 ** -0.5
        HBf = HB[:].rearrange("p k t -> p (k t)")

        def s2_bufs(hd):
            sl = hd % 2
            QT = HBf[:, sl * 2048:(sl + 1) * 2048]
            KT = HBf[:, 4096 + sl * 2048:4096 + (sl + 1) * 2048]
            VV = HBf[:, 8192 + sl * 2048:8192 + (sl + 1) * 2048].rearrange("p (i d) -> p i d", d=128)
            OST = HBf[:, 12288 + sl * 2048:12288 + (sl + 1) * 2048]
            pb = sl * 2304
            KMB = SMB[:, pb:pb + 8]
            BIASQ = SMB[:, pb + 128:pb + 256]
            BIAST = SMB[0:8, pb + 256:pb + 2304]
            return sl, QT, KT, VV, OST, KMB, BIASQ, BIAST

        def s2_p1(hd):
            sl, QT, KT, VV, OST, KMB, BIASQ, BIAST = s2_bufs(hd)
            dma("sp", QT, qs0[hd * 128:(hd + 1) * 128, :], (), [("QT", sl)], "qt%d" % sl)
            dma("sp", KT, ks0[hd * 128:(hd + 1) * 128, :], (), [("KT", sl)], "kt%d" % sl)
            dma("sp", VV, vs0[:, hd * 128:(hd + 1) * 128].rearrange("(i p) d -> p i d", p=128), (), [("VV", sl)],
                "vv%d" % sl)
            act(KMB, KMS[:, hd, :], AF.Copy, [("KMS", hd, 0, 0), ("KMS", hd, 0, 1), ("KMS", hd, 1, 0), ("KMS", hd, 1, 1)],
                [("KMB", sl)])

        def s2_p2(hd):
            sl, QT, KT, VV, OST, KMB, BIASQ, BIAST = s2_bufs(hd)
            for i in range(16):
                mm(PS[0][:, i * 8:(i + 1) * 8], QT[:, i * 128:(i + 1) * 128], KMB, True, True,
                   [("QT", sl), ("KMB", sl)], [("ps", 0)])
            sb_ = sl * 512
            GM = SM[:, sb_:sb_ + 128]
            TOP8 = SM[:, sb_ + 128:sb_ + 256]
            THR = SM[:, sb_ + 256:sb_ + 272]
            GE = SM[:, sb_ + 384:sb_ + 512]
            tt(GM, PS[0][:, 0:128], PASTB[:], ALU.add, [("ps", 0), ("PASTB",)], [("GM", sl)])
            for i in range(16):
                sc.add("dve", lambda e, o=TOP8[:, i * 8:(i + 1) * 8], i_=GM[:, i * 8:(i + 1) * 8]: e.max(out=o, in_=i_),
                       r=[("GM", sl)], w=[("TOP8", sl, i)])
            ts(THR, TOP8.rearrange("p (i e) -> p i e", e=8)[:, :, 2], -1e29, None, ALU.max, None,
               [("TOP8", sl, i) for i in range(16)], [("THR", sl)])
            tt(GE.rearrange("p (i e) -> p i e", e=8), GM.rearrange("p (i e) -> p i e", e=8),
               THR.unsqueeze(2).to_broadcast([128, 16, 8]), ALU.is_ge, [("GM", sl), ("THR", sl)], [("GE", sl)])
            ts(BIASQ, GE, -1.0, -NEG, ALU.add, ALU.mult, [("GE", sl)], [("BIASQ", sl)])

        def s2_p3(hd):
            sl, QT, KT, VV, OST, KMB, BIASQ, BIAST = s2_bufs(hd)
            for hh in range(2):
                for i8 in range(8):
                    i = hh * 8 + i8
                    sc.add("pe", lambda e, o=PST[0:8, i8 * 128:(i8 + 1) * 128], i_=BIASQ[:, i * 8:(i + 1) * 8]:
                           e.transpose(out=o, in_=i_, identity=IDB), r=[("BIASQ", sl), ("CB",)], w=[("pst",)])
                act(BIAST[:, hh * 1024:(hh + 1) * 1024], PST[0:8, :], AF.Copy, [("pst",)], [("BIAST", sl, hh)])

        s2ctr = {"t": 0, "e": 0, "c": 0}

        def s2_main(hd):
            sl, QT, KT, VV, OST, KMB, BIASQ, BIAST = s2_bufs(hd)
            pend = []
            tails = []
            if hd + 1 < 16:
                s2_p1(hd + 1)

            for j in range(4):
                if hd + 1 < 16 and j == 2:
                    s2_p2(hd + 1)
                if hd + 1 < 16 and j == 3:
                    s2_p3(hd + 1)
                qlo = j * 512
                tiles = [(nb, kt, 0, 512) for nb in range(2 * j + 1) for kt in range(2)]
                tiles += [(2 * j + 1, kt, 256, 512) for kt in range(2)]
                cpar = s2ctr["c"] % 2
                s2ctr["c"] += 1
                bo, bd = (3, 4) if cpar == 0 else (5, 6)
                po, pdn = PS[bo], PS[bd]
                nt = len(tiles)
                for idx, (nb, kt, c0, c1) in enumerate(tiles):
                    sj = s2ctr["t"] % 3
                    s2ctr["t"] += 1
                    ei = s2ctr["e"] % 4
                    s2ctr["e"] += 1
                    pS = PS[sj]
                    kpos = nb * 256 + kt * 128
                    mm(pS[:, c0:c1], KT[:, kpos:kpos + 128], QT[:, qlo + c0:qlo + c1], True, False,
                       [("KT", sl), ("QT", sl)], [("ps", sj)])
                    if nb < 2 * j:
                        mm(pS[:, 0:512], SELALL[:, nb * 128:(nb + 1) * 128], BIAST[:, qlo:qlo + 512], False, True,
                           [("CB",), ("BIAST", sl, j // 2)], [("ps", sj)])
                    elif nb == 2 * j:
                        mm(pS[:, 0:256], IDB, CAUS[:, kt * 256:(kt + 1) * 256], False, False, [("CB",)], [("ps", sj)])
                        mm(pS[:, 256:512], SELALL[:, nb * 128:(nb + 1) * 128], BIAST[:, qlo + 256:qlo + 512], False, True,
                           [("CB",), ("BIAST", sl, j // 2)], [("ps", sj)])
                    else:
                        mm(pS[:, 256:512], IDB, CAUS[:, kt * 256:(kt + 1) * 256], False, True, [("CB",)], [("ps", sj)])
                    E = BT[:, ei, 0:512]
                    act(E[:, c0:c1], pS[:, c0:c1], AF.Exp, [("ps", sj)], [("BT", ei)], scale=SCALE)
                    def C(nb=nb, kt=kt, c0=c0, c1=c1, E=E, ei=ei, idx=idx, po=po, pdn=pdn, bo=bo, bd=bd, nt=nt,
                          qlo=qlo):
                        first = idx == 0
                        last = idx == nt - 1
                        mm(po[:, c0:c1], VV[:, nb * 2 + kt, :], E[:, c0:c1], first, last, [("VV", sl), ("BT", ei)],
                           [("ps", bo)])
                        mm(pdn[:, c0:c1], ONESB, E[:, c0:c1], first, last, [("CB",), ("BT", ei)], [("ps", bd)])
                        if last:
                            def tail(po=po, pdn=pdn, bo=bo, bd=bd, qlo=qlo):
                                rd, rdk = ft_new()
                                sc.add("act", lambda e, o=rd, i_=pdn[:]: e.activation(out=o, in_=i_, func=AF.Ln), r=[("ps", bd)], w=[rdk])
                                sc.add("act", lambda e, o=rd: e.activation(out=o, in_=o, func=AF.Exp, scale=-1.0), r=[rdk], w=[rdk])
                                tt(OST[:, qlo:qlo + 512], po[:], rd, ALU.mult, [("ps", bo), rdk], [("OST", sl)])
                            tails.append([3, tail])
                    pend.append(C)
                    if len(pend) > 1:
                        pend.pop(0)()
                    for tl_ in tails:
                        tl_[0] -= 1
                    while tails and tails[0][0] <= 0:
                        tails.pop(0)[1]()
            while pend:
                pend.pop(0)()
            while tails:
                tails.pop(0)[1]()
            dma("sp", os0[hd * 128:(hd + 1) * 128, :], OST, [("OST", sl)], [("os", hd)], "ost%d" % sl)

        if stop_after >= 2:
            s2_p1(0)
            s2_p2(0)
            s2_p3(0)
            for hd in range(16):
                s2_main(hd)
            sc.barrier()

        def ret_qk(hf, col0, dst, kscale):
            t0 = hf * T
            tctr = [0]
            for h in range(8):
                slot, WP = wpiece_col(win, col0 + h * 256)
                sa, sak, sai = bt_new()
                sbb, sbk, sbi = bt_new()
                for n in range(2):
                    ti = tctr[0]
                    tctr[0] += 1
                    pa = PS[ti % 2]
                    pb = PS[2 + ti % 2]
                    for k in range(16):
                        mm(pa[:], WP[:, k, 0:128], HB[:, k, nsl(n)], k == 0, k == 15,
                           [("W", slot), ("H", k, n)], [("ps", ti % 2)])
                    for k in range(16):
                        mm(pb[:], WP[:, k, 128:256], HB[:, k, nsl(n)], k == 0, k == 15,
                           [("W", slot), ("H", k, n)], [("ps", 2 + ti % 2)])
                    af, afk = ft_new()
                    bf, bfk = ft_new()
                    act(af, pa[:], AF.Copy, [("ps", ti % 2)], [afk], scale=kscale)
                    act(bf, pb[:], AF.Copy, [("ps", 2 + ti % 2)], [bfk], scale=kscale)
                    cosT = TAB[:, 0, nsl(n)]
                    sinT = TAB[:, 1, nsl(n)]
                    t1, t1k = ft_new()
                    t2, t2k = ft_new()
                    tt(t1, af, cosT, ALU.mult, [afk, ("TAB", 0)], [t1k])
                    tt(t2, bf, sinT, ALU.mult, [bfk, ("TAB", 1)], [t2k])
                    tt(sa[:, nsl(n)], t1, t2, ALU.subtract, [t1k, t2k], [sak])
                    t3, t3k = ft_new()
                    t4, t4k = ft_new()
                    tt(t3, af, sinT, ALU.mult, [afk, ("TAB", 1)], [t3k])
                    tt(t4, bf, cosT, ALU.mult, [bfk, ("TAB", 0)], [t4k])
                    tt(sbb[:, nsl(n)], t3, t4, ALU.add, [t3k, t4k], [sbk])
                dma("sp", dst[h * 256:h * 256 + 128, t0:t0 + T], sa, [sak], [("rqk", col0, h, 0, hf)], "bt%d" % sai)
                dma("sp", dst[h * 256 + 128:h * 256 + 256, t0:t0 + T], sbb, [sbk], [("rqk", col0, h, 1, hf)], "bt%d" % sbi)

        def ret_g(hf):
            t0 = hf * T
            tctr = [0]
            for h in range(8):
                slot, WP = wpiece_col(win, 3 * D + h * 256)
                for c in range(2):
                    st, stk, sti = bt_new()
                    for n in range(2):
                        ti = tctr[0]
                        tctr[0] += 1
                        pa = PS[ti % 2]
                        for k in range(16):
                            mm(pa[:], WP[:, k, c * 128:(c + 1) * 128], HB[:, k, nsl(n)], k == 0, k == 15,
                               [("W", slot), ("H", k, n)], [("ps", ti % 2)])
                        act(st[:, nsl(n)], pa[:], AF.Silu, [("ps", ti % 2)], [stk])
                    dma("sp", gs[h * 256 + c * 128:h * 256 + (c + 1) * 128, t0:t0 + T], st, [stk], [("gs", h, c, hf)],
                        "bt%d" % sti)

        if stop_after >= 3:
            for hf in range(2):
                load_x(hf, xs0)
                load_h(hf, os0)
                load_tab(hf, cosR_in, sinR_in)
                out_proj(wo_m)
                norm_h(2 * 16)
                ffn(1)
                norm_h(3 * 16)
                ffn(2)
                norm_h(4 * 16)
                store_x(hf, xs1)
                ret_qk(hf, 0, qs1, 1.0)
                ret_qk(hf, D, ks1, 1.0 / 16.0)
                v_proj(hf, win, 2 * D, vs1)
                ret_g(hf)
                sc.barrier()

        if stop_after >= 4:
            XFf = XF[:].rearrange("p k t -> p (k t)")
            s4ctr = [0]
            XFb = XFf.bitcast(BF16)

            def s4_bufs(h):
                par = h % 2
                B0 = HBf if par == 0 else XFb[:, 16384:32768]
                QT0 = B0[:, 0:2048]
                QT1 = B0[:, 2048:4096]
                KT0 = B0[:, 4096:6144]
                KT1 = B0[:, 6144:8192]
                VV = B0[:, 8192:12288].rearrange("p (i d) -> p i d", d=256)
                GS0 = B0[:, 12288:14336]
                GS1 = B0[:, 14336:16384]
                DT = XFf[:, par * 2560:(par + 1) * 2560].rearrange("p (r c) -> p r c", c=512)
                return par, QT0, QT1, KT0, KT1, VV, GS0, GS1, DT

            def s4_load(h):
                par, QT0, QT1, KT0, KT1, VV, GS0, GS1, DT = s4_bufs(h)
                dma("sp", QT0, qs1[h * 256:h * 256 + 128, :], (), [("RQ", par, 0)], "rq0%d" % par)
                dma("sp", QT1, qs1[h * 256 + 128:h * 256 + 256, :], (), [("RQ", par, 1)], "rq1%d" % par)
                dma("sp", KT0, ks1[h * 256:h * 256 + 128, :], (), [("RK", par, 0)], "rk0%d" % par)
                dma("sp", KT1, ks1[h * 256 + 128:h * 256 + 256, :], (), [("RK", par, 1)], "rk1%d" % par)
                dma("sp", VV, vs1[:, h * 256:(h + 1) * 256].rearrange("(i p) d -> p i d", p=128), (), [("RV", par)],
                    "rv%d" % par)
                dma("sp", GS0, gs[h * 256:h * 256 + 128, :], (), [("RG", par, 0)], "rg0%d" % par)
                dma("sp", GS1, gs[h * 256 + 128:h * 256 + 256, :], (), [("RG", par, 1)], "rg1%d" % par)
                dma("sp", DT, dtab_in[h * 128:(h + 1) * 128, :].rearrange("p (r c) -> p r c", c=512), (),
                    [("DT", par)], "dt%d" % par)

            s4_load(0)
            for h in range(8):
                g_ = gam[h]
                if h + 1 < 8:
                    s4_load(h + 1)
                par, QT0, QT1, KT0, KT1, VV, GS0, GS1, DT = s4_bufs(h)
                YST = [BT[:, 0, :], BT[:, 1, :], BT[:, 2, :], BT[:, 3, :]]
                QTs = [QT0, QT1]
                KTs = [KT0, KT1]
                GSs = [GS0, GS1]
                pend = []
                tails = []
                for j in range(4):
                    jsl = slice(j * 512, (j + 1) * 512)
                    nm = 4 * j + 4
                    ob = (3, 4) if j % 2 == 0 else (5, 6)
                    for i in range(nm):
                        sj = s4ctr[0] % 3
                        s4ctr[0] += 1
                        pS = PS[sj]
                        for dc in range(2):
                            mm(pS[:], KTs[dc][:, i * 128:(i + 1) * 128], QTs[dc][:, jsl], dc == 0, dc == 1,
                               [("RK", par, dc), ("RQ", par, dc)], [("ps", sj)])
                        STt = SMB[:, sj * 512:(sj + 1) * 512]
                        if i >= 4 * j:
                            tt(STt, pS[:], DT[:, i - 4 * j, :], ALU.mult, [("ps", sj), ("DT", par)], [("ST", sj)])
                        else:
                            cst = float(g_ ** (j * 512 - i * 128 - 127))
                            stt(STt, pS[:], cst, DT[:, 4, :], ALU.mult, ALU.mult, [("ps", sj), ("DT", par)],
                                [("ST", sj)])

                        def C(i=i, nm=nm, STt=STt, sj=sj, ob=ob, j=j, jsl=jsl):
                            for vc in range(2):
                                mm(PS[ob[vc]][:], VV[:, i, vc * 128:(vc + 1) * 128], STt, i == 0, i == nm - 1,
                                   [("RV", par), ("ST", sj)], [("ps", ob[vc])])
                            if i != nm - 1:
                                return
                            sqs = []
                            for vc in range(2):
                                sqf, sqk = ft_new()
                                sq = sqf.bitcast(BF16)[:, 0:512]
                                act(sq, PS[ob[vc]][:], AF.Square, [("ps", ob[vc])], [sqk])
                                sqs.append((sq, sqk))

                            def tail(sqs=sqs, ob=ob, j=j, jsl=jsl):
                                nj = s4ctr[0] % 3
                                s4ctr[0] += 1
                                pN = PS[nj]
                                for vc in range(2):
                                    mm(pN[:], AVG_G, sqs[vc][0], vc == 0, vc == 1, [sqs[vc][1], ("CB",)], [("ps", nj)])
                                rs = RSTD[:, (j % 2) * 512:(j % 2 + 1) * 512]
                                rsk = ("RSTD", j % 2)
                                sc.add("act", lambda e, o=rs, p_=pN: e.activation(out=o, in_=p_[:], func=AF.Ln, bias=EPS, scale=1.0),
                                       r=[("ps", nj)], w=[rsk])
                                sc.add("act", lambda e, o=rs: e.activation(out=o, in_=o, func=AF.Exp, scale=-0.5), r=[rsk], w=[rsk])

                                def tail2(rs=rs, rsk=rsk, ob=ob, j=j, jsl=jsl):
                                    for vc in range(2):
                                        t1, t1k = ft_new()
                                        col = 112 + h * 2 + vc
                                        stt(t1, PS[ob[vc]][:], VECS[:, col:col + 1], rs, ALU.mult, ALU.mult,
                                            [("ps", ob[vc]), rsk, ("VECS",)], [t1k])
                                        yslot = vc * 2 + j // 2
                                        tt(YST[yslot][:, (j % 2) * 512:(j % 2 + 1) * 512], t1, GSs[vc][:, jsl], ALU.mult,
                                           [t1k, ("RG", par, vc)], [("BT", yslot)])
                                tails.append([4, tail2])
                            tails.append([3, tail])
                        pend.append(C)
                        if len(pend) > 1:
                            pend.pop(0)()
                        for tl_ in tails:
                            tl_[0] -= 1
                        while tails and tails[0][0] <= 0:
                            tails.pop(0)[1]()
                while pend:
                    pend.pop(0)()
                while tails:
                    tails.pop(0)[1]()
                for vc in range(2):
                    for hh in range(2):
                        yslot = vc * 2 + hh
                        dma("sp", os1[h * 256 + vc * 128:h * 256 + (vc + 1) * 128, hh * T:(hh + 1) * T], YST[yslot],
                            [("BT", yslot)], [("ys", h, vc, hh)], "bt%d" % yslot)
            sc.barrier()

        if stop_after >= 5:
            for hf in range(2):
                t0 = hf * T
                load_x(hf, xs1)
                load_h(hf, os1)
                out_proj(wo_r)
                norm_h(5 * 16)
                ffn(3)

                def o(k, n):
                    ot, otk = ft_new()
                    stt(ot, XF[:, k, nsl(n)], VECS[:, 96 + k:97 + k], RSTD[:, nsl(n)], ALU.mult, ALU.mult,
                        [("X", k, n), ("RSTD", n), ("VECS",)], [otk])
                    dma("sp", outT[k * 128:(k + 1) * 128, t0 + n * 512:t0 + (n + 1) * 512], ot, [otk],
                        [("out", k, n, hf)], "o%d" % (otk[1]))
                rms_to(96, o)
                sc.barrier()

        if debug_out == "xs":
            pass
        sc.barrier()
        sc.add("sp", None)

        sem_names = sc.finalize()
        sems = {}
        for nme in sem_names:
            sems[nme] = es.enter_context(nc.semaphore(nme))
        block = es.enter_context(nc.Block())

        @block.tensor
        def _(e):
            sc.emit("pe", e, sems)

        @block.scalar
        def _(e):
            sc.emit("act", e, sems)

        @block.vector
        def _(e):
            sc.emit("dve", e, sems)

        @block.gpsimd
        def _(e):
            sc.emit("pool", e, sems)

        @block.sync
        def _(e):
            sc.emit("sp", e, sems)

    return nc


def _prep_inputs(x, norm_gain, ffn_w_gate_up, ffn_w_down, moba_w_qkv, moba_w_o,
                 ret_w_in, ret_w_o, ret_gn_gain, final_norm):
    C = _get_consts()
    f = lambda a: np.ascontiguousarray(np.asarray(a, dtype=np.float32))
    vecs = np.zeros((128, 128), np.float32)
    ng = f(norm_gain).reshape(6, 16, 128)
    vecs[:, 0:96] = ng.transpose(2, 0, 1).reshape(128, 96)
    vecs[:, 96:112] = f(final_norm).reshape(16, 128).T
    vecs[:, 112:128] = f(ret_gn_gain).reshape(16, 128).T
    shared = {
        "wgu": f(ffn_w_gate_up).reshape(4 * D, 2 * FF),
        "wdn": f(ffn_w_down).reshape(4 * FF, D),
        "wqkv": f(moba_w_qkv).reshape(D, 3 * D),
        "wo_m": f(moba_w_o).reshape(D, D),
        "win": f(ret_w_in).reshape(D, 4 * D),
        "wo_r": f(ret_w_o).reshape(D, D),
        "vecs": vecs,
        "cosM": C["cosM"], "sinM": C["sinM"], "cosR": C["cosR"], "sinR": C["sinR"],
        "permF": C["permF"], "onesF": C["onesF"], "pastb": C["pastb"], "cb": C["cb"], "dtab": C["dtab"],
    }
    xf = f(x)
    in_maps = []
    zx = np.zeros((D, S), np.float32)
    for c in range(8):
        m = dict(shared)
        if c in ACTIVE:
            m["xT"] = np.ascontiguousarray(xf[ACTIVE.index(c)].T)
        else:
            m["xT"] = zx
        in_maps.append(m)
    return in_maps


def kernel(x, norm_gain, ffn_w_gate_up, ffn_w_down, moba_w_qkv, moba_w_o,
           ret_w_in, ret_w_o, ret_gn_gain, final_norm):
    in_maps = _prep_inputs(x, norm_gain, ffn_w_gate_up, ffn_w_down, moba_w_qkv, moba_w_o,
                           ret_w_in, ret_w_o, ret_gn_gain, final_norm)
    nc = build_program()
    res = run_bass_kernel_spmd(nc, in_maps, core_ids=list(range(8)))
    out = np.stack([np.ascontiguousarray(res.results[ACTIVE[b]]["outT"].T) for b in range(4)], axis=0)
    return out.astype(np.float32)
```

```python
import os
from contextlib import ExitStack

import numpy as np
import concourse.bass as bass
import concourse.mybir as mybir
from concourse.bass_utils import run_bass_kernel_spmd

F32 = mybir.dt.float32
BF16 = mybir.dt.bfloat16
AF = mybir.ActivationFunctionType
ALU = mybir.AluOpType
AX = mybir.AxisListType

D = 2048
S = 2048
FF = 5632
T = 1024
NB = 6
EPS = 1e-6
NEG = -30000.0
EPOCH = 6000
ACTIVE = [0, 1, 4, 5]


class Op:
    __slots__ = ("eng", "fn", "deps", "signal", "sig", "chan")


class Sched:
    ENGS = ("pe", "act", "dve", "pool", "sp")

    def __init__(self):
        self.q = {e: [] for e in self.ENGS}
        self.lw = {}
        self.rd = {}
        self.bar = []
        self.chan_last = {}

    def add(self, eng, fn, r=(), w=(), chan=None, nobar=False):
        o = Op()
        o.eng = eng
        o.fn = fn
        o.signal = False
        o.sig = None
        o.chan = chan
        deps = set()
        if not nobar:
            deps.update(self.bar)
        for k in r:
            x = self.lw.get(k)
            if x is not None:
                deps.add(x)
        for k in w:
            x = self.lw.get(k)
            if x is not None:
                deps.add(x)
            rr = self.rd.get(k)
            if rr:
                deps.update(rr.values())
        for k in w:
            self.lw[k] = o
            self.rd[k] = {}
        for k in r:
            rr = self.rd.setdefault(k, {})
            rr[(eng if chan is None else ("dma", id(o)))] = o
        deps.discard(o)
        o.deps = deps
        for d in deps:
            d.signal = True
        self.q[eng].append(o)
        if chan is not None:
            self.chan_last[chan] = o
        return o

    def barrier(self, keep_prefix=("W",)):
        bar = []
        for e in self.ENGS:
            for o in reversed(self.q[e]):
                if o.chan is None:
                    bar.append(o)
                    o.signal = True
                    break
        for c, o in self.chan_last.items():
            bar.append(o)
        self.bar = bar
        self.lw = {k: v for k, v in self.lw.items() if k[0] in keep_prefix}
        self.rd = {k: v for k, v in self.rd.items() if k[0] in keep_prefix}

    def finalize(self):
        sem_names = []
        for e in self.ENGS:
            cnt = 0
            chan_cnt = {}
            for o in self.q[e]:
                if o.chan is not None:
                    pass
                elif o.signal:
                    cnt += 1
                    ep = (cnt - 1) // EPOCH
                    name = "s_%s_%d" % (e, ep)
                    if name not in sem_names:
                        sem_names.append(name)
                    o.sig = (name, cnt - ep * EPOCH)
        chan_cnt = {}
        for e in self.ENGS:
            for o in self.q[e]:
                if o.chan is not None:
                    n = chan_cnt.get(o.chan, 0) + 1
                    chan_cnt[o.chan] = n
                    name = "c_" + o.chan
                    if name not in sem_names:
                        sem_names.append(name)
                    o.sig = (name, 16 * n)
        return sem_names

    def emit(self, eng_name, e, sems):
        waited = {}
        for o in self.q[eng_name]:
            ws = {}
            for d in o.deps:
                if eng_name == "pe" and d.eng == "pe" and d.chan is None:
                    continue
                s, v = d.sig
                if waited.get(s, 0) >= v:
                    continue
                if ws.get(s, 0) < v:
                    ws[s] = v
            for s in sorted(ws):
                e.wait_ge(sems[s], ws[s])
                waited[s] = ws[s]
            if o.fn is None:
                continue
            ins = o.fn(e)
            if o.chan is not None:
                ins.then_inc(sems[o.sig[0]], 16)
            elif o.signal:
                ins.then_inc(sems[o.sig[0]], 1)


def _consts():
    c = {}
    pos = np.arange(S, dtype=np.float32)
    half = 16
    inv = np.power(np.float32(500000.0), -np.arange(half, dtype=np.float32) / np.float32(half)).astype(np.float32)
    ang = (pos[:, None] * inv[None, :]).astype(np.float32)
    cosv = np.cos(ang).astype(np.float32).T
    sinv = np.sin(ang).astype(np.float32).T
    cosM = np.ones((128, S), np.float32)
    sinM = np.zeros((128, S), np.float32)
    cosM[0:16] = cosv
    cosM[16:32] = cosv
    sinM[0:16] = -sinv
    sinM[16:32] = sinv
    c["cosM"] = cosM
    c["sinM"] = sinM
    perm = np.zeros((128, 128), np.float32)
    for d in range(16):
        perm[d + 16, d] = 1.0
        perm[d, d + 16] = 1.0
    c["permF"] = perm
    c["onesF"] = np.concatenate([np.full((128, 128), 1.0 / 2048.0, np.float32), np.full((128, 128), 1.0 / 256.0, np.float32)], axis=1)
    invr = np.power(np.float32(10000.0), -np.linspace(0.0, 1.0, 128, dtype=np.float32)).astype(np.float32)
    angr = (pos[:, None] * invr[None, :]).astype(np.float32)
    c["cosR"] = np.cos(angr).astype(np.float32).T.copy()
    c["sinR"] = np.sin(angr).astype(np.float32).T.copy()
    pastb = np.zeros((128, 16, 8), np.float32)
    for i in range(16):
        for n in range(8):
            if not (n < i // 2):
                pastb[:, i, n] = -1e30
    c["pastb"] = pastb.reshape(128, 128)
    import ml_dtypes
    cb = np.zeros((128, 2048), np.float32)
    cb[:, 0:128] = np.eye(128)
    cb[:, 128:256] = 1.0
    for n in range(8):
        cb[n, 256 + n * 128:256 + (n + 1) * 128] = 1.0
    kk = np.arange(128)[:, None]
    qq = np.arange(256)[None, :]
    for kt in range(2):
        cb[:, 1280 + kt * 256:1280 + (kt + 1) * 256] = np.where(kt * 128 + kk <= qq, 0.0, NEG)
    cb[:, 1792:1920] = 1.0 / 2048.0
    cb[:, 1920:2048] = 1.0 / 256.0
    c["cb"] = cb.astype(ml_dtypes.bfloat16)
    dt = np.zeros((8, 128, 5, 512), np.float64)
    p = np.arange(128)[:, None].astype(np.float64)
    nl = np.arange(512)[None, :].astype(np.float64)
    gam = []
    for h in range(8):
        g = 1.0 - 2.0 ** (-5.0 - h)
        gam.append(g)
        lg = np.log(g)
        for r in range(4):
            dd = nl - (r * 128 + p)
            dt[h, :, r, :] = np.where(dd >= 0, np.exp(dd * lg), 0.0)
        dt[h, :, 4, :] = np.exp((nl - p + 127.0) * lg)
    c["dtab"] = dt.astype(np.float32).reshape(8 * 128, 5 * 512)
    c["gam"] = gam
    return c


_C = None


def _get_consts():
    global _C
    if _C is None:
        _C = _consts()
    return _C


def build_program(stop_after=99, debug_out=None):
    C = _get_consts()
    gam = C["gam"]
    nc = bass.Bass("TRN2", target_bir_lowering=False)
    sc = Sched()

    def din(name, shape, dt=F32):
        return nc.dram_tensor(name, list(shape), dt, kind="ExternalInput").ap()

    def dscr(name, shape, dt):
        kind = "ExternalOutput" if (debug_out == name or debug_out == "all") else "Internal"
        return nc.dram_tensor(name, list(shape), dt, kind=kind).ap()

    xT_in = din("xT", [D, S])
    wgu = din("wgu", [4 * D, 2 * FF])
    wdn = din("wdn", [4 * FF, D])
    wqkv = din("wqkv", [D, 3 * D])
    wo_m = din("wo_m", [D, D])
    win = din("win", [D, 4 * D])
    wo_r = din("wo_r", [D, D])
    vecs_in = din("vecs", [128, 128])
    cosM_in = din("cosM", [128, S])
    sinM_in = din("sinM", [128, S])
    cosR_in = din("cosR", [128, S])
    sinR_in = din("sinR", [128, S])
    permF_in = din("permF", [128, 128])
    onesF_in = din("onesF", [128, 256])
    pastb_in = din("pastb", [128, 128])
    cb_in = din("cb", [128, 2048], BF16)
    dtab_in = din("dtab", [8 * 128, 5 * 512])
    if debug_out in ("outT", "all", None):
        outT = nc.dram_tensor("outT", [D, S], F32, kind="ExternalOutput").ap()
    else:
        outT = nc.dram_tensor("outT", [D, S], F32, kind="Internal").ap()
    xs0 = dscr("xs0", [D, S], F32)
    qs0 = dscr("qs0", [D, S], BF16)
    ks0 = dscr("ks0", [D, S], BF16)
    vs0 = dscr("vs0", [S, D], BF16)
    os0 = dscr("os0", [D, S], BF16)
    xs1 = dscr("xs1", [D, S], F32)
    qs1 = dscr("qs1", [D, S], BF16)
    ks1 = dscr("ks1", [D, S], BF16)
    vs1 = dscr("vs1", [S, D], BF16)
    os1 = dscr("os1", [D, S], BF16)
    gs = dscr("gs", [D, S], BF16)

    es = ExitStack()
    with es:
        def sb(name, shape, dt):
            return es.enter_context(nc.sbuf_tensor(name, list(shape), dt))

        XF = sb("XF", [128, 16, T], F32)
        HB = sb("HB", [128, 16, T], BF16)
        WR = sb("WR", [128, NB, 4096], BF16)
        ACTS = sb("ACTS", [128, 2, 2, T], BF16)
        FT = sb("FT", [128, 8, 512], F32)
        BT = sb("BT", [128, 4, T], BF16)
        RSTD = sb("RSTD", [128, T], F32)
        TAB = sb("TAB", [128, 2, T], F32)
        VECS = sb("VECS", [128, 128], F32)
        PERMF = sb("PERMF", [128, 128], F32)
        PASTB = sb("PASTB", [128, 128], F32)
        CB = sb("CB", [128, 2048], BF16)
        KMS = sb("KMS", [128, 16, 8], F32)
        SM = sb("SM", [128, 1024], F32)
        SMB = sb("SMB", [128, 4608], BF16)
        PS = [es.enter_context(nc.psum_tensor("ps%d" % i, [128, 512], F32)) for i in range(7)]
        PST = es.enter_context(nc.psum_tensor("pst", [128, 1024], BF16))

        PS8 = [p[:] for p in PS] + [PST[:].bitcast(F32)]
        IDB = CB[:, 0:128]
        ONESB = CB[:, 128:256]
        SELALL = CB[0:8, 256:1280]
        CAUS = CB[:, 1280:1792]
        AVG_D = CB[:, 1792:1920]
        AVG_G = CB[:, 1920:2048]

        ctr = {"w": 0, "ft": 0, "bt": 0}

        def dma(eng, out, in_, r, w, chan, nobar=False):
            return sc.add(eng, lambda e: e.dma_start(out=out, in_=in_), r=r, w=w, chan=chan, nobar=nobar)

        def wpiece_col(W2d, c0):
            p = ctr["w"]
            ctr["w"] += 1
            slot = p % NB
            dst = WR[:, slot, :].rearrange("p (k c) -> p k c", c=256)
            src = W2d[:, c0:c0 + 256].rearrange("(k p) c -> p k c", p=128)
            dma("pool", dst, src, (), [("W", slot)], "w%d" % slot, nobar=True)
            return slot, dst

        def wpiece_row(W2d, r0):
            p = ctr["w"]
            ctr["w"] += 1
            slot = p % NB
            dst = WR[:, slot, :].rearrange("p (k c) -> p k c", c=2048)
            src = W2d[r0:r0 + 256, :].rearrange("(k p) c -> p k c", p=128)
            dma("pool", dst, src, (), [("W", slot)], "w%d" % slot, nobar=True)
            return slot, dst

        def ft_new():
            i = ctr["ft"] % 8
            ctr["ft"] += 1
            return FT[:, i, :], ("FT", i)

        def bt_new():
            i = ctr["bt"] % 4
            ctr["bt"] += 1
            return BT[:, i, :], ("BT", i), i

        def mm(out, lhsT, rhs, start, stop, r, w):
            sc.add("pe", lambda e: e.matmul(out, lhsT, rhs, start=start, stop=stop), r=r, w=w)

        def act(out, in_, func, r, w, scale=1.0):
            sc.add("act", lambda e: e.activation(out=out, in_=in_, func=func, scale=scale), r=r, w=w)

        def tt(out, in0, in1, op, r, w, eng="dve"):
            sc.add(eng, lambda e: e.tensor_tensor(out=out, in0=in0, in1=in1, op=op), r=r, w=w)

        def ts(out, in0, s1, s2, op0, op1, r, w, eng="dve"):
            if op1 is None:
                sc.add(eng, lambda e: e.tensor_scalar(out=out, in0=in0, scalar1=s1, scalar2=None, op0=op0), r=r, w=w)
            else:
                sc.add(eng, lambda e: e.tensor_scalar(out=out, in0=in0, scalar1=s1, scalar2=s2, op0=op0, op1=op1), r=r, w=w)

        def stt(out, in0, scalar, in1, op0, op1, r, w, eng="dve"):
            sc.add(eng, lambda e: e.scalar_tensor_tensor(out=out, in0=in0, scalar=scalar, in1=in1, op0=op0, op1=op1), r=r, w=w)

        def nsl(n):
            return slice(n * 512, (n + 1) * 512)

        dma("sp", VECS[:], vecs_in, (), [("VECS",)], "c0")
        sc.add("dve", lambda e: e.memset(KMS[:], 0.0), w=[("KMSINIT",)])
        dma("sp", PERMF[:], permF_in, (), [("PERMF",)], "c2")
        dma("sp", PASTB[:], pastb_in, (), [("PASTB",)], "c3")
        dma("sp", CB[:], cb_in, (), [("CB",)], "c4")
        sc.barrier()

        def load_x(hf, src):
            t0 = hf * T
            for g4 in range(4):
                dma("sp", XF[:, g4 * 4:(g4 + 1) * 4, :],
                    src[g4 * 512:(g4 + 1) * 512, t0:t0 + T].rearrange("(k p) t -> p k t", p=128),
                    (), [("X", k, n) for k in range(g4 * 4, g4 * 4 + 4) for n in range(2)], "x%d" % g4)

        def store_x(hf, dst):
            t0 = hf * T
            for g4 in range(4):
                dma("sp", dst[g4 * 512:(g4 + 1) * 512, t0:t0 + T].rearrange("(k p) t -> p k t", p=128),
                    XF[:, g4 * 4:(g4 + 1) * 4, :],
                    [("X", k, n) for k in range(g4 * 4, g4 * 4 + 4) for n in range(2)], (), "x%d" % g4)

        def load_h(hf, src):
            t0 = hf * T
            for g4 in range(4):
                dma("sp", HB[:, g4 * 4:(g4 + 1) * 4, :],
                    src[g4 * 512:(g4 + 1) * 512, t0:t0 + T].rearrange("(k p) t -> p k t", p=128),
                    (), [("H", k, n) for k in range(g4 * 4, g4 * 4 + 4) for n in range(2)], "h%d" % g4)

        def rms_to(gcol, out_fn):
            for n in range(2):
                for k in range(16):
                    sqf, sqk = ft_new()
                    sq = sqf.bitcast(BF16)[:, 0:512]
                    act(sq, XF[:, k, nsl(n)], AF.Square, [("X", k, n)], [sqk])
                    mm(PS[6][:], AVG_D, sq, k == 0, k == 15, [sqk, ("CB",)], [("ps", 6)])
                sc.add("act", lambda e, o=RSTD[:, nsl(n)]: e.activation(out=o, in_=PS[6][:], func=AF.Ln, bias=EPS, scale=1.0),
                       r=[("ps", 6)], w=[("RSTD", n)])
                sc.add("act", lambda e, o=RSTD[:, nsl(n)]: e.activation(out=o, in_=o, func=AF.Exp, scale=-0.5),
                       r=[("RSTD", n)], w=[("RSTD", n)])
                for k in range(16):
                    out_fn(k, n)

        def norm_h(vcol0):
            def o(k, n):
                stt(HB[:, k, nsl(n)], XF[:, k, nsl(n)], VECS[:, vcol0 + k:vcol0 + k + 1], RSTD[:, nsl(n)],
                    ALU.mult, ALU.mult, [("X", k, n), ("RSTD", n), ("VECS",)], [("H", k, n)])
            rms_to(vcol0, o)

        def ffn(lj):
            Wgu = wgu[lj * D:(lj + 1) * D, :]
            Wdn = wdn[lj * FF:(lj + 1) * FF, :]
            NS = FF // 256
            tile_ctr = [0]
            dctr = [0]

            def gu_tile(s, q, WG, WU, sg_slot, su_slot, inter):
                aslot = s % 2
                c, n = divmod(q, 2)
                ti = tile_ctr[0]
                tile_ctr[0] += 1
                pg = PS[ti % 2]
                pu = PS[2 + ti % 2]
                inter = list(inter)
                cnt = 0
                for k in range(16):
                    mm(pg[:], WG[:, k, c * 128:(c + 1) * 128], HB[:, k, nsl(n)], k == 0, k == 15,
                       [("W", sg_slot), ("H", k, n)], [("ps", ti % 2)])
                    cnt += 1
                    if cnt % 4 == 0 and inter:
                        inter.pop(0)()
                for k in range(16):
                    mm(pu[:], WU[:, k, c * 128:(c + 1) * 128], HB[:, k, nsl(n)], k == 0, k == 15,
                       [("W", su_slot), ("H", k, n)], [("ps", 2 + ti % 2)])
                    cnt += 1
                    if cnt % 4 == 0 and inter:
                        inter.pop(0)()
                while inter:
                    inter.pop(0)()
                sg, sgk = ft_new()
                act(sg, pg[:], AF.Silu, [("ps", ti % 2)], [sgk])
                tt(ACTS[:, aslot, c, nsl(n)], sg, pu[:], ALU.mult, [sgk, ("ps", 2 + ti % 2)],
                   [("A", aslot, c, n)])

            def down_tile(s, WD, sd_slot, idx):
                aslot = s % 2
                m, n = divmod(idx, 2)
                di = dctr[0]
                dctr[0] += 1
                bk = 4 + di % 4
                pd = PS8[bk]
                bkey = ("ps", bk) if bk < 7 else ("pst",)
                for c in range(2):
                    mm(pd, WD[:, c, m * 128:(m + 1) * 128], ACTS[:, aslot, c, nsl(n)], c == 0, c == 1,
                       [("W", sd_slot), ("A", aslot, c, n)], [bkey])
                stt(XF[:, m, nsl(n)], pd, 0.5, XF[:, m, nsl(n)], ALU.mult, ALU.add,
                    [bkey, ("X", m, n)], [("X", m, n)])

            prev = None
            for s in range(NS):
                sg_slot, WG = wpiece_col(Wgu, s * 256)
                su_slot, WU = wpiece_col(Wgu, FF + s * 256)
                for q in range(4):
                    inter = []
                    if prev is not None:
                        inter = [(lambda idx=idx, pv=prev, s_=s - 1: down_tile(s_, pv[1], pv[0], idx))
                                 for idx in range(q * 8, (q + 1) * 8)]
                    gu_tile(s, q, WG, WU, sg_slot, su_slot, inter)
                prev = wpiece_row(Wdn, s * 256)
            for idx in range(32):
                down_tile(NS - 1, prev[1], prev[0], idx)

        def out_proj(W2d, scale=1.0):
            dctr = [0]
            for i in range(8):
                slot, WP = wpiece_col(W2d, i * 256)
                for c in range(2):
                    m = 2 * i + c
                    for n in range(2):
                        di = dctr[0]
                        dctr[0] += 1
                        pd = PS[4 + di % 2]
                        for k in range(16):
                            mm(pd[:], WP[:, k, c * 128:(c + 1) * 128], HB[:, k, nsl(n)], k == 0, k == 15,
                               [("W", slot), ("H", k, n)], [("ps", 4 + di % 2)])
                        tt(XF[:, m, nsl(n)], pd[:], XF[:, m, nsl(n)], ALU.add,
                           [("ps", 4 + di % 2), ("X", m, n)], [("X", m, n)])

        def load_tab(hf, cos_in, sin_in):
            t0 = hf * T
            dma("sp", TAB[:, 0, :], cos_in[:, t0:t0 + T], (), [("TAB", 0)], "tab0")
            dma("sp", TAB[:, 1, :], sin_in[:, t0:t0 + T], (), [("TAB", 1)], "tab1")

        def wpiece_col2(W2d, c0):
            if ctr["w"] % 2 == 1:
                ctr["w"] += 1
            p = ctr["w"]
            ctr["w"] += 2
            slot = p % NB
            dst = WR[:, slot:slot + 2, :].rearrange("p s x -> p (s x)").rearrange("p (k c) -> p k c", c=512)
            src = W2d[:, c0:c0 + 512].rearrange("(k p) c -> p k c", p=128)
            dma("pool", dst, src, (), [("W", slot), ("W", slot + 1)], "w%d" % slot, nobar=True)
            return slot, dst

        def v_proj(hf, W2d, col0, vs):
            t0 = hf * T
            tctr = [0]
            for i in range(4):
                slot, WP = wpiece_col2(W2d, col0 + i * 512)
                for g in range(4):
                    vst, vk, vi = bt_new()
                    vst3 = vst.rearrange("p (t c) -> p t c", c=512)
                    for tl in range(2):
                        ttile = g * 2 + tl
                        ti = tctr[0]
                        tctr[0] += 1
                        pv = PS[ti % 2]
                        for k in range(16):
                            mm(pv[:], HB[:, k, ttile * 128:(ttile + 1) * 128], WP[:, k, :], k == 0, k == 15,
                               [("W", slot), ("W", slot + 1), ("H", k, ttile // 4)], [("ps", ti % 2)])
                        act(vst3[:, tl, :], pv[:], AF.Copy, [("ps", ti % 2)], [vk])
                    dma("sp", vs[t0 + g * 256:t0 + (g + 1) * 256, i * 512:(i + 1) * 512].rearrange("(t p) c -> p t c", p=128),
                        vst3, [vk], [("vs", hf, i, g)], "bt%d" % vi)

        def moba_qk(hf, col0, dst, is_k):
            t0 = hf * T
            tctr = [0]
            pend = []
            for i in range(8):
                slot, WP = wpiece_col(wqkv, col0 + i * 256)
                for c in range(2):
                    head = 2 * i + c
                    st, stk, sti = bt_new()
                    for n in range(2):
                        ti = tctr[0]
                        tctr[0] += 1
                        pa = PS[ti % 2]
                        pb = PS[2 + ti % 2]
                        for k in range(16):
                            mm(pa[:], WP[:, k, c * 128:(c + 1) * 128], HB[:, k, nsl(n)], k == 0, k == 15,
                               [("W", slot), ("H", k, n)], [("ps", ti % 2)])
                        kf, kfk = ft_new()
                        act(kf, pa[:], AF.Copy, [("ps", ti % 2)], [kfk])

                        def rot(kf=kf, kfk=kfk, pb=pb, ti=ti, n=n, st=st, stk=stk, sti=sti, head=head):
                            mm(pb[:], PERMF[:], kf, True, True, [kfk, ("PERMF",)], [("ps", 2 + ti % 2)])
                            t1, t1k = ft_new()
                            tt(t1, kf, TAB[:, 0, nsl(n)], ALU.mult, [kfk, ("TAB", 0)], [t1k])
                            t2, t2k = ft_new()
                            tt(t2, pb[:], TAB[:, 1, nsl(n)], ALU.mult, [("ps", 2 + ti % 2), ("TAB", 1)], [t2k])
                            tt(t1, t1, t2, ALU.add, [t1k, t2k], [t1k])
                            if is_k:
                                for b_ in range(2):
                                    bi = hf * 4 + n * 2 + b_
                                    sc.add("act", lambda e, o=st[:, n * 512 + b_ * 256:n * 512 + (b_ + 1) * 256],
                                           i_=t1[:, b_ * 256:(b_ + 1) * 256], a_=KMS[:, head, bi:bi + 1]:
                                           e.activation(out=o, in_=i_, func=AF.Copy, accum_out=a_),
                                           r=[t1k], w=[stk, ("KMS", head, hf, n)])
                            else:
                                act(st[:, nsl(n)], t1, AF.Copy, [t1k], [stk])
                            if n == 1:
                                dma("sp", dst[head * 128:(head + 1) * 128, t0:t0 + T], st, [stk],
                                    [("qk", is_k, head, hf)], "bt%d" % sti)
                        pend.append(rot)
                        if len(pend) > 1:
                            pend.pop(0)()
            while pend:
                pend.pop(0)()

        for hf in range(2):
            if hf == 0:
                load_x(hf, xT_in)
            load_tab(hf, cosM_in, sinM_in)
            norm_h(0 * 16)
            ffn(0)
            norm_h(1 * 16)
            store_x(hf, xs0)
            if hf == 0:
                load_x(1, xT_in)
            moba_qk(hf, D, ks0, True)
            moba_qk(hf, 0, qs0, False)
            v_proj(hf, wqkv, 2 * D, vs0)
            sc.barrier()

        SCALE = 128.0 ** -0.5
        HBf = HB[:].rearrange("p k t -> p (k t)")

        def s2_bufs(hd):
            sl = hd % 2
            QT = HBf[:, sl * 2048:(sl + 1) * 2048]
            KT = HBf[:, 4096 + sl * 2048:4096 + (sl + 1) * 2048]
            VV = HBf[:, 8192 + sl * 2048:8192 + (sl + 1) * 2048].rearrange("p (i d) -> p i d", d=128)
            OST = HBf[:, 12288 + sl * 2048:12288 + (sl + 1) * 2048]
            pb = sl * 2304
            KMB = SMB[:, pb:pb + 8]
            BIASQ = SMB[:, pb + 128:pb + 256]
            BIAST = SMB[0:8, pb + 256:pb + 2304]
            return sl, QT, KT, VV, OST, KMB, BIASQ, BIAST

        def s2_p1(hd):
            sl, QT, KT, VV, OST, KMB, BIASQ, BIAST = s2_bufs(hd)
            dma("sp", QT, qs0[hd * 128:(hd + 1) * 128, :], (), [("QT", sl)], "qt%d" % sl)
            dma("sp", KT, ks0[hd * 128:(hd + 1) * 128, :], (), [("KT", sl)], "kt%d" % sl)
            dma("sp", VV, vs0[:, hd * 128:(hd + 1) * 128].rearrange("(i p) d -> p i d", p=128), (), [("VV", sl)],
                "vv%d" % sl)
            act(KMB, KMS[:, hd, :], AF.Copy, [("KMS", hd, 0, 0), ("KMS", hd, 0, 1), ("KMS", hd, 1, 0), ("KMS", hd, 1, 1)],
                [("KMB", sl)])

        def s2_p2(hd):
            sl, QT, KT, VV, OST, KMB, BIASQ, BIAST = s2_bufs(hd)
            for i in range(16):
                mm(PS[0][:, i * 8:(i + 1) * 8], QT[:, i * 128:(i + 1) * 128], KMB, True, True,
                   [("QT", sl), ("KMB", sl)], [("ps", 0)])
            sb_ = sl * 512
            GM = SM[:, sb_:sb_ + 128]
            TOP8 = SM[:, sb_ + 128:sb_ + 256]
            THR = SM[:, sb_ + 256:sb_ + 272]
            GE = SM[:, sb_ + 384:sb_ + 512]
            tt(GM, PS[0][:, 0:128], PASTB[:], ALU.add, [("ps", 0), ("PASTB",)], [("GM", sl)])
            for i in range(16):
                sc.add("dve", lambda e, o=TOP8[:, i * 8:(i + 1) * 8], i_=GM[:, i * 8:(i + 1) * 8]: e.max(out=o, in_=i_),
                       r=[("GM", sl)], w=[("TOP8", sl, i)])
            ts(THR, TOP8.rearrange("p (i e) -> p i e", e=8)[:, :, 2], -1e29, None, ALU.max, None,
               [("TOP8", sl, i) for i in range(16)], [("THR", sl)])
            tt(GE.rearrange("p (i e) -> p i e", e=8), GM.rearrange("p (i e) -> p i e", e=8),
               THR.unsqueeze(2).to_broadcast([128, 16, 8]), ALU.is_ge, [("GM", sl), ("THR", sl)], [("GE", sl)])
            ts(BIASQ, GE, -1.0, -NEG, ALU.add, ALU.mult, [("GE", sl)], [("BIASQ", sl)])

        def s2_p3(hd):
            sl, QT, KT, VV, OST, KMB, BIASQ, BIAST = s2_bufs(hd)
            for hh in range(2):
                for i8 in range(8):
                    i = hh * 8 + i8
                    sc.add("pe", lambda e, o=PST[0:8, i8 * 128:(i8 + 1) * 128], i_=BIASQ[:, i * 8:(i + 1) * 8]:
                           e.transpose(out=o, in_=i_, identity=IDB), r=[("BIASQ", sl), ("CB",)], w=[("pst",)])
                act(BIAST[:, hh * 1024:(hh + 1) * 1024], PST[0:8, :], AF.Copy, [("pst",)], [("BIAST", sl, hh)])

        s2ctr = {"t": 0, "e": 0, "c": 0}

        def s2_main(hd):
            sl, QT, KT, VV, OST, KMB, BIASQ, BIAST = s2_bufs(hd)
            pend = []
            tails = []
            if hd + 1 < 16:
                s2_p1(hd + 1)

            for j in range(4):
                if hd + 1 < 16 and j == 2:
                    s2_p2(hd + 1)
                if hd + 1 < 16 and j == 3:
                    s2_p3(hd + 1)
                qlo = j * 512
                tiles = [(nb, kt, 0, 512) for nb in range(2 * j + 1) for kt in range(2)]
                tiles += [(2 * j + 1, kt, 256, 512) for kt in range(2)]
                cpar = s2ctr["c"] % 2
                s2ctr["c"] += 1
                bo, bd = (3, 4) if cpar == 0 else (5, 6)
                po, pdn = PS[bo], PS[bd]
                nt = len(tiles)
                for idx, (nb, kt, c0, c1) in enumerate(tiles):
                    sj = s2ctr["t"] % 3
                    s2ctr["t"] += 1
                    ei = s2ctr["e"] % 4
                    s2ctr["e"] += 1
                    pS = PS[sj]
                    kpos = nb * 256 + kt * 128
                    mm(pS[:, c0:c1], KT[:, kpos:kpos + 128], QT[:, qlo + c0:qlo + c1], True, False,
                       [("KT", sl), ("QT", sl)], [("ps", sj)])
                    if nb < 2 * j:
                        mm(pS[:, 0:512], SELALL[:, nb * 128:(nb + 1) * 128], BIAST[:, qlo:qlo + 512], False, True,
                           [("CB",), ("BIAST", sl, j // 2)], [("ps", sj)])
                    elif nb == 2 * j:
                        mm(pS[:, 0:256], IDB, CAUS[:, kt * 256:(kt + 1) * 256], False, False, [("CB",)], [("ps", sj)])
                        mm(pS[:, 256:512], SELALL[:, nb * 128:(nb + 1) * 128], BIAST[:, qlo + 256:qlo + 512], False, True,
                           [("CB",), ("BIAST", sl, j // 2)], [("ps", sj)])
                    else:
                        mm(pS[:, 256:512], IDB, CAUS[:, kt * 256:(kt + 1) * 256], False, True, [("CB",)], [("ps", sj)])
                    E = BT[:, ei, 0:512]
                    act(E[:, c0:c1], pS[:, c0:c1], AF.Exp, [("ps", sj)], [("BT", ei)], scale=SCALE)
                    def C(nb=nb, kt=kt, c0=c0, c1=c1, E=E, ei=ei, idx=idx, po=po, pdn=pdn, bo=bo, bd=bd, nt=nt,
                          qlo=qlo):
                        first = idx == 0
                        last = idx == nt - 1
                        mm(po[:, c0:c1], VV[:, nb * 2 + kt, :], E[:, c0:c1], first, last, [("VV", sl), ("BT", ei)],
                           [("ps", bo)])
                        mm(pdn[:, c0:c1], ONESB, E[:, c0:c1], first, last, [("CB",), ("BT", ei)], [("ps", bd)])
                        if last:
                            def tail(po=po, pdn=pdn, bo=bo, bd=bd, qlo=qlo):
                                rd, rdk = ft_new()
                                sc.add("act", lambda e, o=rd, i_=pdn[:]: e.activation(out=o, in_=i_, func=AF.Ln), r=[("ps", bd)], w=[rdk])
                                sc.add("act", lambda e, o=rd: e.activation(out=o, in_=o, func=AF.Exp, scale=-1.0), r=[rdk], w=[rdk])
                                tt(OST[:, qlo:qlo + 512], po[:], rd, ALU.mult, [("ps", bo), rdk], [("OST", sl)])
                            tails.append([3, tail])
                    pend.append(C)
                    if len(pend) > 1:
                        pend.pop(0)()
                    for tl_ in tails:
                        tl_[0] -= 1
                    while tails and tails[0][0] <= 0:
                        tails.pop(0)[1]()
            while pend:
                pend.pop(0)()
            while tails:
                tails.pop(0)[1]()
            dma("sp", os0[hd * 128:(hd + 1) * 128, :], OST, [("OST", sl)], [("os", hd)], "ost%d" % sl)

        if stop_after >= 2:
            s2_p1(0)
            s2_p2(0)
            s2_p3(0)
            for hd in range(16):
                s2_main(hd)
            sc.barrier()

        def ret_qk(hf, col0, dst, kscale):
            t0 = hf * T
            tctr = [0]
            for h in range(8):
                slot, WP = wpiece_col(win, col0 + h * 256)
                sa, sak, sai = bt_new()
                sbb, sbk, sbi = bt_new()
                for n in range(2):
                    ti = tctr[0]
                    tctr[0] += 1
                    pa = PS[ti % 2]
                    pb = PS[2 + ti % 2]
                    for k in range(16):
                        mm(pa[:], WP[:, k, 0:128], HB[:, k, nsl(n)], k == 0, k == 15,
                           [("W", slot), ("H", k, n)], [("ps", ti % 2)])
                    for k in range(16):
                        mm(pb[:], WP[:, k, 128:256], HB[:, k, nsl(n)], k == 0, k == 15,
                           [("W", slot), ("H", k, n)], [("ps", 2 + ti % 2)])
                    af, afk = ft_new()
                    bf, bfk = ft_new()
                    act(af, pa[:], AF.Copy, [("ps", ti % 2)], [afk], scale=kscale)
                    act(bf, pb[:], AF.Copy, [("ps", 2 + ti % 2)], [bfk], scale=kscale)
                    cosT = TAB[:, 0, nsl(n)]
                    sinT = TAB[:, 1, nsl(n)]
                    t1, t1k = ft_new()
                    t2, t2k = ft_new()
                    tt(t1, af, cosT, ALU.mult, [afk, ("TAB", 0)], [t1k])
                    tt(t2, bf, sinT, ALU.mult, [bfk, ("TAB", 1)], [t2k])
                    tt(sa[:, nsl(n)], t1, t2, ALU.subtract, [t1k, t2k], [sak])
                    t3, t3k = ft_new()
                    t4, t4k = ft_new()
                    tt(t3, af, sinT, ALU.mult, [afk, ("TAB", 1)], [t3k])
                    tt(t4, bf, cosT, ALU.mult, [bfk, ("TAB", 0)], [t4k])
                    tt(sbb[:, nsl(n)], t3, t4, ALU.add, [t3k, t4k], [sbk])
                dma("sp", dst[h * 256:h * 256 + 128, t0:t0 + T], sa, [sak], [("rqk", col0, h, 0, hf)], "bt%d" % sai)
                dma("sp", dst[h * 256 + 128:h * 256 + 256, t0:t0 + T], sbb, [sbk], [("rqk", col0, h, 1, hf)], "bt%d" % sbi)

        def ret_g(hf):
            t0 = hf * T
            tctr = [0]
            for h in range(8):
                slot, WP = wpiece_col(win, 3 * D + h * 256)
                for c in range(2):
                    st, stk, sti = bt_new()
                    for n in range(2):
                        ti = tctr[0]
                        tctr[0] += 1
                        pa = PS[ti % 2]
                        for k in range(16):
                            mm(pa[:], WP[:, k, c * 128:(c + 1) * 128], HB[:, k, nsl(n)], k == 0, k == 15,
                               [("W", slot), ("H", k, n)], [("ps", ti % 2)])
                        act(st[:, nsl(n)], pa[:], AF.Silu, [("ps", ti % 2)], [stk])
                    dma("sp", gs[h * 256 + c * 128:h * 256 + (c + 1) * 128, t0:t0 + T], st, [stk], [("gs", h, c, hf)],
                        "bt%d" % sti)

        if stop_after >= 3:
            for hf in range(2):
                if hf == 0:
                    load_x(hf, xs0)
                load_h(hf, os0)
                load_tab(hf, cosR_in, sinR_in)
                out_proj(wo_m)
                norm_h(2 * 16)
                ffn(1)
                norm_h(3 * 16)
                ffn(2)
                norm_h(4 * 16)
                store_x(hf, xs1)
                if hf == 0:
                    load_x(1, xs0)
                ret_qk(hf, 0, qs1, 1.0)
                ret_qk(hf, D, ks1, 1.0 / 16.0)
                v_proj(hf, win, 2 * D, vs1)
                ret_g(hf)
                sc.barrier()

        if stop_after >= 4:
            XFf = XF[:].rearrange("p k t -> p (k t)")
            s4ctr = [0]
            XFb = XFf.bitcast(BF16)

            def s4_bufs(h):
                par = h % 2
                B0 = HBf if par == 0 else XFb[:, 16384:32768]
                QT0 = B0[:, 0:2048]
                QT1 = B0[:, 2048:4096]
                KT0 = B0[:, 4096:6144]
                KT1 = B0[:, 6144:8192]
                VV = B0[:, 8192:12288].rearrange("p (i d) -> p i d", d=256)
                GS0 = B0[:, 12288:14336]
                GS1 = B0[:, 14336:16384]
                DT = XFf[:, par * 2560:(par + 1) * 2560].rearrange("p (r c) -> p r c", c=512)
                return par, QT0, QT1, KT0, KT1, VV, GS0, GS1, DT

            def s4_load(h):
                par, QT0, QT1, KT0, KT1, VV, GS0, GS1, DT = s4_bufs(h)
                dma("sp", QT0, qs1[h * 256:h * 256 + 128, :], (), [("RQ", par, 0)], "rq0%d" % par)
                dma("sp", QT1, qs1[h * 256 + 128:h * 256 + 256, :], (), [("RQ", par, 1)], "rq1%d" % par)
                dma("sp", KT0, ks1[h * 256:h * 256 + 128, :], (), [("RK", par, 0)], "rk0%d" % par)
                dma("sp", KT1, ks1[h * 256 + 128:h * 256 + 256, :], (), [("RK", par, 1)], "rk1%d" % par)
                dma("sp", VV, vs1[:, h * 256:(h + 1) * 256].rearrange("(i p) d -> p i d", p=128), (), [("RV", par)],
                    "rv%d" % par)
                dma("sp", GS0, gs[h * 256:h * 256 + 128, :], (), [("RG", par, 0)], "rg0%d" % par)
                dma("sp", GS1, gs[h * 256 + 128:h * 256 + 256, :], (), [("RG", par, 1)], "rg1%d" % par)
                dma("sp", DT, dtab_in[h * 128:(h + 1) * 128, :].rearrange("p (r c) -> p r c", c=512), (),
                    [("DT", par)], "dt%d" % par)

            s4_load(0)
            for h in range(8):
                g_ = gam[h]
                if h + 1 < 8:
                    s4_load(h + 1)
                par, QT0, QT1, KT0, KT1, VV, GS0, GS1, DT = s4_bufs(h)
                YST = [BT[:, 0, :], BT[:, 1, :], BT[:, 2, :], BT[:, 3, :]]
                QTs = [QT0, QT1]
                KTs = [KT0, KT1]
                GSs = [GS0, GS1]
                pend = []
                tails = []
                for j in range(4):
                    jsl = slice(j * 512, (j + 1) * 512)
                    nm = 4 * j + 4
                    ob = (3, 4) if j % 2 == 0 else (5, 6)
                    for i in range(nm):
                        sj = s4ctr[0] % 3
                        s4ctr[0] += 1
                        pS = PS[sj]
                        for dc in range(2):
                            mm(pS[:], KTs[dc][:, i * 128:(i + 1) * 128], QTs[dc][:, jsl], dc == 0, dc == 1,
                               [("RK", par, dc), ("RQ", par, dc)], [("ps", sj)])
                        STt = SMB[:, sj * 512:(sj + 1) * 512]
                        if i >= 4 * j:
                            tt(STt, pS[:], DT[:, i - 4 * j, :], ALU.mult, [("ps", sj), ("DT", par)], [("ST", sj)])
                        else:
                            cst = float(g_ ** (j * 512 - i * 128 - 127))
                            stt(STt, pS[:], cst, DT[:, 4, :], ALU.mult, ALU.mult, [("ps", sj), ("DT", par)],
                                [("ST", sj)])

                        def C(i=i, nm=nm, STt=STt, sj=sj, ob=ob, j=j, jsl=jsl):
                            for vc in range(2):
                                mm(PS[ob[vc]][:], VV[:, i, vc * 128:(vc + 1) * 128], STt, i == 0, i == nm - 1,
                                   [("RV", par), ("ST", sj)], [("ps", ob[vc])])
                            if i != nm - 1:
                                return
                            sqs = []
                            for vc in range(2):
                                sqf, sqk = ft_new()
                                sq = sqf.bitcast(BF16)[:, 0:512]
                                act(sq, PS[ob[vc]][:], AF.Square, [("ps", ob[vc])], [sqk])
                                sqs.append((sq, sqk))

                            def tail(sqs=sqs, ob=ob, j=j, jsl=jsl):
                                nj = s4ctr[0] % 3
                                s4ctr[0] += 1
                                pN = PS[nj]
                                for vc in range(2):
                                    mm(pN[:], AVG_G, sqs[vc][0], vc == 0, vc == 1, [sqs[vc][1], ("CB",)], [("ps", nj)])
                                rs, rsk = ft_new()
                                sc.add("act", lambda e, o=rs, p_=pN: e.activation(out=o, in_=p_[:], func=AF.Ln, bias=EPS, scale=1.0),
                                       r=[("ps", nj)], w=[rsk])
                                sc.add("act", lambda e, o=rs: e.activation(out=o, in_=o, func=AF.Exp, scale=-0.5), r=[rsk], w=[rsk])
                                for vc in range(2):
                                    t1, t1k = ft_new()
                                    col = 112 + h * 2 + vc
                                    stt(t1, PS[ob[vc]][:], VECS[:, col:col + 1], rs, ALU.mult, ALU.mult,
                                        [("ps", ob[vc]), rsk, ("VECS",)], [t1k])
                                    yslot = vc * 2 + j // 2
                                    tt(YST[yslot][:, (j % 2) * 512:(j % 2 + 1) * 512], t1, GSs[vc][:, jsl], ALU.mult,
                                       [t1k, ("RG", par, vc)], [("BT", yslot)])
                            tails.append([3, tail])
                        pend.append(C)
                        if len(pend) > 1:
                            pend.pop(0)()
                        for tl_ in tails:
                            tl_[0] -= 1
                        while tails and tails[0][0] <= 0:
                            tails.pop(0)[1]()
                while pend:
                    pend.pop(0)()
                while tails:
                    tails.pop(0)[1]()
                for vc in range(2):
                    for hh in range(2):
                        yslot = vc * 2 + hh
                        dma("sp", os1[h * 256 + vc * 128:h * 256 + (vc + 1) * 128, hh * T:(hh + 1) * T], YST[yslot],
                            [("BT", yslot)], [("ys", h, vc, hh)], "bt%d" % yslot)
            sc.barrier()

        if stop_after >= 5:
            for hf in range(2):
                t0 = hf * T
                load_x(hf, xs1)
                load_h(hf, os1)
                out_proj(wo_r)
                norm_h(5 * 16)
                ffn(3)

                def o(k, n):
                    ot, otk = ft_new()
                    stt(ot, XF[:, k, nsl(n)], VECS[:, 96 + k:97 + k], RSTD[:, nsl(n)], ALU.mult, ALU.mult,
                        [("X", k, n), ("RSTD", n), ("VECS",)], [otk])
                    dma("sp", outT[k * 128:(k + 1) * 128, t0 + n * 512:t0 + (n + 1) * 512], ot, [otk],
                        [("out", k, n, hf)], "o%d" % (otk[1]))
                rms_to(96, o)
                sc.barrier()

        if debug_out == "xs":
            pass
        sc.barrier()
        sc.add("sp", None)

        sem_names = sc.finalize()
        sems = {}
        for nme in sem_names:
            sems[nme] = es.enter_context(nc.semaphore(nme))
        block = es.enter_context(nc.Block())

        @block.tensor
        def _(e):
            sc.emit("pe", e, sems)

        @block.scalar
        def _(e):
            sc.emit("act", e, sems)

        @block.vector
        def _(e):
            sc.emit("dve", e, sems)

        @block.gpsimd
        def _(e):
            sc.emit("pool", e, sems)

        @block.sync
        def _(e):
            sc.emit("sp", e, sems)

    return nc


def _prep_inputs(x, norm_gain, ffn_w_gate_up, ffn_w_down, moba_w_qkv, moba_w_o,
                 ret_w_in, ret_w_o, ret_gn_gain, final_norm):
    C = _get_consts()
    f = lambda a: np.ascontiguousarray(np.asarray(a, dtype=np.float32))
    vecs = np.zeros((128, 128), np.float32)
    ng = f(norm_gain).reshape(6, 16, 128)
    vecs[:, 0:96] = ng.transpose(2, 0, 1).reshape(128, 96)
    vecs[:, 96:112] = f(final_norm).reshape(16, 128).T
    vecs[:, 112:128] = f(ret_gn_gain).reshape(16, 128).T
    shared = {
        "wgu": f(ffn_w_gate_up).reshape(4 * D, 2 * FF),
        "wdn": f(ffn_w_down).reshape(4 * FF, D),
        "wqkv": f(moba_w_qkv).reshape(D, 3 * D),
        "wo_m": f(moba_w_o).reshape(D, D),
        "win": f(ret_w_in).reshape(D, 4 * D),
        "wo_r": f(ret_w_o).reshape(D, D),
        "vecs": vecs,
        "cosM": C["cosM"], "sinM": C["sinM"], "cosR": C["cosR"], "sinR": C["sinR"],
        "permF": C["permF"], "onesF": C["onesF"], "pastb": C["pastb"], "cb": C["cb"], "dtab": C["dtab"],
    }
    xf = f(x)
    in_maps = []
    zx = np.zeros((D, S), np.float32)
    for c in range(8):
        m = dict(shared)
        if c in ACTIVE:
            m["xT"] = np.ascontiguousarray(xf[ACTIVE.index(c)].T)
        else:
            m["xT"] = zx
        in_maps.append(m)
    return in_maps


def kernel(x, norm_gain, ffn_w_gate_up, ffn_w_down, moba_w_qkv, moba_w_o,
           ret_w_in, ret_w_o, ret_gn_gain, final_norm):
    in_maps = _prep_inputs(x, norm_gain, ffn_w_gate_up, ffn_w_down, moba_w_qkv, moba_w_o,
                           ret_w_in, ret_w_o, ret_gn_gain, final_norm)
    nc = build_program()
    res = run_bass_kernel_spmd(nc, in_maps, core_ids=list(range(8)))
    out = np.stack([np.ascontiguousarray(res.results[ACTIVE[b]]["outT"].T) for b in range(4)], axis=0)
    return out.astype(np.float32)
```
